# Optimizing a Trainium2 kernel written in Bass

```python
import jax, jax.numpy as jnp
from jax import lax
import numpy as np

D_MODEL = 2048
BATCH = 16
SEQ = 256
DEPTH = 2
DEC_BATCH = 8
DEC_SEQ = 2048
PAST_LEN = 512

GRID_W = 64
D_MIX = D_MODEL
FN_HEADS = 4
FN_DH = 128
FN_WIDTH = FN_HEADS * FN_DH
HG_HEADS = 4
HG_DK = 128
HG_DV = 128
HG_QK = HG_HEADS * HG_DK
HG_WIDTH = HG_HEADS * HG_DV
MLA_HEADS = 8
MLA_NOPE = 128
MLA_ROPE = 64
MLA_V = 128
Q_LORA = 768
KV_LORA = 512
MLA_WIDTH = MLA_HEADS * MLA_V
D_FF = 4 * D_MODEL
CHUNK = 64
Q_BLOCK = 128
ROPE_BASE = 10000.0
EPS = 1e-6
F_FLOOR = 1e-30
IN_SIZES = (FN_WIDTH, HG_QK, HG_WIDTH, HG_QK, HG_QK, HG_WIDTH, Q_LORA, KV_LORA, MLA_ROPE)
IN_COLS = sum(IN_SIZES)
f32 = jnp.float32

kernel_name = "hybrid_fourier_hgrn2_mla_dit_step"


def rmsnorm(x, g):
    xf = x.astype(f32)
    y = xf * lax.rsqrt(jnp.mean(xf * xf, axis=-1, keepdims=True) + EPS) * g.astype(f32)
    return y.astype(x.dtype)


def axial_rope_tables(rows):
    t = jnp.arange(rows * GRID_W)
    r = (t // GRID_W).astype(f32)
    col = (t % GRID_W).astype(f32)
    nf = MLA_ROPE // 4
    inv = ROPE_BASE ** (-jnp.arange(nf, dtype=f32) / nf)
    ar = r[:, None] * inv
    ac = col[:, None] * inv
    ang = jnp.concatenate([ar, ar, ac, ac], axis=-1)
    return jnp.cos(ang), jnp.sin(ang)


def apply_axial_rope(x, cos, sin):
    xf = x.astype(f32)
    xs = xf.reshape(x.shape[:-1] + (2, 2, MLA_ROPE // 4))
    rot = jnp.concatenate([-xs[..., 1:, :], xs[..., :1, :]], axis=-2).reshape(x.shape)
    return (xf * cos + rot * sin).astype(x.dtype)


def forget_gate(pre, lb):
    lbf = lb.astype(f32)
    f = lbf + (1.0 - lbf) * jax.nn.sigmoid(pre.astype(f32))
    return jnp.log(jnp.maximum(f, F_FLOOR)), 1.0 - f


def hgrn_chunked(q, k, v, log_f, s0):
    b_sz, n, h, _ = q.shape
    n_chunks = n // CHUNK

    def to_chunks(t):
        return t.reshape(b_sz, n_chunks, CHUNK, h, t.shape[-1]).transpose(1, 0, 3, 2, 4)

    lower = jnp.tril(jnp.ones((CHUNK, CHUNK), dtype=bool))[:, :, None]

    def step(state, blk):
        qc, kc, vc, fc = blk
        cum = jnp.cumsum(fc, axis=2)
        diff = cum[:, :, :, None, :] - cum[:, :, None, :, :]
        rel = jnp.where(lower, jnp.exp(jnp.minimum(diff, 0.0)), 0.0)
        scores = jnp.einsum("bhtk,bhtsk,bhsk->bhts", qc, rel, kc)
        out = (jnp.einsum("bhts,bhsv->bhtv", scores, vc)
               + jnp.einsum("bhtk,bhkv->bhtv", qc * jnp.exp(cum), state))
        last = cum[:, :, -1:, :]
        state = (jnp.exp(last[:, :, 0, :])[..., None] * state
                 + jnp.einsum("bhsk,bhsv->bhkv", kc * jnp.exp(last - cum), vc))
        return state, out

    s_final, out = lax.scan(step, s0, (to_chunks(q), to_chunks(k), to_chunks(v), to_chunks(log_f)))
    out = out.transpose(1, 0, 3, 2, 4).reshape(b_sz, n, h, v.shape[-1])
    return out, s_final


def mla_expand(c_kv, k_rope, w_kv_b):
    b_sz, n, _ = c_kv.shape
    kv = (c_kv @ w_kv_b).reshape(b_sz, n, MLA_HEADS, MLA_NOPE + MLA_V)
    k = jnp.concatenate(
        [kv[..., :MLA_NOPE], jnp.broadcast_to(k_rope[:, :, None, :], (b_sz, n, MLA_HEADS, MLA_ROPE))],
        axis=-1)
    return k, kv[..., MLA_NOPE:]


def attend(q, k, v):
    b_sz, sq, h, dh = q.shape
    nb = sq // Q_BLOCK
    qb = q.reshape(b_sz, nb, Q_BLOCK, h, dh).transpose(1, 0, 2, 3, 4)
    scale = dh ** -0.5

    def one_block(qi):
        s = jnp.einsum("bqhd,bkhd->bhqk", qi, k, preferred_element_type=f32) * scale
        p = jax.nn.softmax(s, axis=-1)
        return jnp.einsum("bhqk,bkhd->bqhd", p.astype(v.dtype), v)

    o = lax.map(one_block, qb)
    return o.transpose(1, 0, 2, 3, 4).reshape(b_sz, sq, h, v.shape[-1])


def token_mixers(h, lp, rope, ctx):
    b_sz, n, _ = h.shape
    proj = h @ lp["w_in"]
    pts = []
    acc = 0
    for s in IN_SIZES[:-1]:
        acc += s
        pts.append(acc)
    u_fn, hq, hv, hff, hfb, hgate, qa, kva, kr = jnp.split(proj, pts, axis=-1)

    u4 = u_fn.reshape(b_sz, n, FN_HEADS, FN_DH).astype(f32)
    y_fn = jnp.real(jnp.fft.fft2(u4, axes=(1, 3), norm="ortho")).reshape(b_sz, n, FN_WIDTH).astype(h.dtype)

    lb = lp["lb"]
    logf_f, key_f = forget_gate(hff.reshape(b_sz, n, HG_HEADS, HG_DK), lb[0].reshape(HG_HEADS, HG_DK))
    logf_b, key_b = forget_gate(hfb.reshape(b_sz, n, HG_HEADS, HG_DK), lb[1].reshape(HG_HEADS, HG_DK))
    q_hg = hq.reshape(b_sz, n, HG_HEADS, HG_DK).astype(f32)
    v_hg = hv.reshape(b_sz, n, HG_HEADS, HG_DV).astype(f32)
    if ctx is None:
        s0_f = jnp.zeros((b_sz, HG_HEADS, HG_DK, HG_DV), f32)
        s0_b = s0_f
    else:
        s0_f = ctx[2].astype(f32)
        s0_b = ctx[3].astype(f32)
    o_f, s_f = hgrn_chunked(q_hg, key_f, v_hg, logf_f, s0_f)
    flip = lambda t: jnp.flip(t, axis=1)
    o_b, s_b = hgrn_chunked(flip(q_hg), flip(key_b), flip(v_hg), flip(logf_b), s0_b)
    o_hg = rmsnorm(o_f + flip(o_b), lp["hg_gain"].reshape(HG_HEADS, HG_DV)).astype(h.dtype)
    y_hg = o_hg.reshape(b_sz, n, HG_WIDTH) * jax.nn.silu(hgate)

    c_kv = rmsnorm(kva, lp["kv_norm"])
    q = (rmsnorm(qa, lp["q_norm"]) @ lp["w_q_b"]).reshape(b_sz, n, MLA_HEADS, MLA_NOPE + MLA_ROPE)
    q_nope, q_rope = q[..., :MLA_NOPE], q[..., MLA_NOPE:]
    k_rope = kr
    if rope is not None:
        cos, sin = rope
        q_rope = apply_axial_rope(q_rope, cos[:, None, :], sin[:, None, :])
        k_rope = apply_axial_rope(kr, cos, sin)
    k_all, v_all = mla_expand(c_kv, k_rope, lp["w_kv_b"])
    if ctx is not None:
        k_c, v_c = mla_expand(ctx[0], ctx[1], lp["w_kv_b"])
        k_all = jnp.concatenate([k_c, k_all], axis=1)
        v_all = jnp.concatenate([v_c, v_all], axis=1)
    y_mla = attend(jnp.concatenate([q_nope, q_rope], axis=-1), k_all, v_all).reshape(b_sz, n, MLA_WIDTH)

    y = jnp.concatenate([y_fn, y_hg, y_mla], axis=-1) @ lp["w_out"]
    return y, (c_kv, kr, s_f.astype(h.dtype), s_b.astype(h.dtype))


def trunk_layer(x, cond, lp, rope, ctx):
    mod = jax.nn.silu(cond) @ lp["w_ada"] + lp["b_ada"]
    sh1, sc1, g1, sh2, sc2, g2 = jnp.split(mod[:, None, :], 6, axis=-1)
    h = rmsnorm(x, lp["g_pre_mix"]) * (1.0 + sc1) + sh1
    y, ctx_out = token_mixers(h, lp, rope, ctx)
    x = x + g1 * rmsnorm(y, lp["g_post_mix"])
    h = rmsnorm(x, lp["g_pre_ff"]) * (1.0 + sc2) + sh2
    f = jnp.square(jax.nn.relu(h @ lp["w_ff1"])) @ lp["w_ff2"]
    x = x + g2 * rmsnorm(f, lp["g_post_ff"])
    return x, ctx_out


def setup_inputs(seed: int = 0) -> dict:
    key = jax.random.key(seed)
    ks = jax.random.split(key, 32)
    nrm = lambda k, shape, scale: jax.random.normal(k, shape, f32) * scale
    gain = lambda k, shape: 1.0 + 0.05 * jax.random.normal(k, shape, f32)
    return {
        "x_prompt": nrm(ks[0], (BATCH, SEQ, D_MODEL), 1.0),
        "x_sample": nrm(ks[1], (DEC_BATCH, DEC_SEQ, D_MODEL), 1.0),
        "c": nrm(ks[2], (DEC_BATCH, D_MODEL), 1.0),
        "cache_ckv": nrm(ks[3], (DEC_BATCH, DEPTH, PAST_LEN, KV_LORA), 1.0),
        "cache_krope": nrm(ks[4], (DEC_BATCH, DEPTH, PAST_LEN, MLA_ROPE), 1.0),
        "state_hgrn": nrm(ks[5], (DEC_BATCH, DEPTH, 2, HG_HEADS, HG_DK, HG_DV), 0.5),
        "c_ctx": nrm(ks[6], (D_MODEL,), 1.0),
        "w_ada": nrm(ks[7], (DEPTH, D_MODEL, 6 * D_MODEL), 0.5 * D_MODEL ** -0.5),
        "b_ada": nrm(ks[8], (DEPTH, 6 * D_MODEL), 0.02),
        "g_pre_mix": gain(ks[9], (DEPTH, D_MODEL)),
        "g_post_mix": gain(ks[10], (DEPTH, D_MODEL)),
        "g_pre_ff": gain(ks[11], (DEPTH, D_MODEL)),
        "g_post_ff": gain(ks[12], (DEPTH, D_MODEL)),
        "w_in": nrm(ks[13], (DEPTH, D_MODEL, IN_COLS), D_MODEL ** -0.5),
        "hg_lb": nrm(ks[14], (DEPTH, 2, HG_QK), 1.0),
        "hg_gain": gain(ks[15], (DEPTH, HG_WIDTH)),
        "mla_q_norm": gain(ks[16], (DEPTH, Q_LORA)),
        "mla_kv_norm": gain(ks[17], (DEPTH, KV_LORA)),
        "w_q_b": nrm(ks[18], (DEPTH, Q_LORA, MLA_HEADS * (MLA_NOPE + MLA_ROPE)), Q_LORA ** -0.5),
        "w_kv_b": nrm(ks[19], (DEPTH, KV_LORA, MLA_HEADS * (MLA_NOPE + MLA_V)), KV_LORA ** -0.5),
        "w_out": nrm(ks[20], (DEPTH, D_MIX, D_MODEL), D_MIX ** -0.5),
        "w_ff1": nrm(ks[21], (DEPTH, D_MODEL, D_FF), D_MODEL ** -0.5),
        "w_ff2": nrm(ks[22], (DEPTH, D_FF, D_MODEL), D_FF ** -0.5),
    }


def reference(x_prompt, x_sample, c, cache_ckv, cache_krope, state_hgrn, c_ctx,
              w_ada, b_ada, g_pre_mix, g_post_mix, g_pre_ff, g_post_ff,
              w_in, hg_lb, hg_gain, mla_q_norm, mla_kv_norm, w_q_b, w_kv_b,
              w_out, w_ff1, w_ff2):
    p_lb = jax.nn.softmax(hg_lb.astype(f32), axis=0)
    lb_all = jnp.cumsum(p_lb, axis=0) - p_lb[0]

    rows = x_sample.shape[1] // GRID_W
    rope = axial_rope_tables(rows)
    cond_ctx = jnp.broadcast_to(c_ctx[None, :], (x_prompt.shape[0], c_ctx.shape[0]))

    yp = x_prompt
    ys = x_sample
    ckv_list, kr_list, st_list = [], [], []
    for l in range(DEPTH):
        lp = {
            "w_ada": w_ada[l], "b_ada": b_ada[l],
            "g_pre_mix": g_pre_mix[l], "g_post_mix": g_post_mix[l],
            "g_pre_ff": g_pre_ff[l], "g_post_ff": g_post_ff[l],
            "w_in": w_in[l], "lb": lb_all[l], "hg_gain": hg_gain[l],
            "q_norm": mla_q_norm[l], "kv_norm": mla_kv_norm[l],
            "w_q_b": w_q_b[l], "w_kv_b": w_kv_b[l], "w_out": w_out[l],
            "w_ff1": w_ff1[l], "w_ff2": w_ff2[l],
        }
        yp, (ckv_l, kr_l, sf_l, sb_l) = trunk_layer(yp, cond_ctx, lp, None, None)
        ckv_list.append(ckv_l)
        kr_list.append(kr_l)
        st_list.append(jnp.stack([sf_l, sb_l], axis=1))
        ctx = (cache_ckv[:, l], cache_krope[:, l], state_hgrn[:, l, 0], state_hgrn[:, l, 1])
        ys, _ = trunk_layer(ys, c, lp, rope, ctx)

    new_ckv = jnp.stack(ckv_list, axis=1)
    new_krope = jnp.stack(kr_list, axis=1)
    new_state_hgrn = jnp.stack(st_list, axis=1)
    return (yp, ys, new_ckv, new_krope, new_state_hgrn)
```

```python
import bisect
import contextlib
import numpy as np
import concourse.bass as bass
import concourse.mybir as mybir
from concourse.bass_utils import run_bass_kernel_spmd

F32 = mybir.dt.float32
BF16 = mybir.dt.bfloat16
AF = mybir.ActivationFunctionType
ALU = mybir.AluOpType
AX = mybir.AxisListType

D = 2048
NT = 2560
NSQ = 2048
NP = 256
CTX = 512
NK = CTX + NT
DEPTH = 2
INC = 4416
DFF = 8192
EPS = 1e-6
NSLOT = 3
DBG_STAGE = [99]
SLOT = 8192


class Eng:
    def __init__(self, name, q, sem, self_sync):
        self.name = name
        self.q = q
        self.sem = sem
        self.n = 0
        self.last = None
        self.sig_idx = []
        self.sig_cnt = []
        self.count = 0
        self.seen = {}
        self.self_sync = self_sync

    def value_for(self, idx):
        if self.sig_idx and self.sig_idx[-1] >= idx:
            j = bisect.bisect_left(self.sig_idx, idx)
            return self.sig_cnt[j]
        self.last.then_inc(self.sem, 1)
        self.count += 1
        self.sig_idx.append(self.n - 1)
        self.sig_cnt.append(self.count)
        return self.count


class Chan:
    def __init__(self, sem):
        self.sem = sem
        self.count = 0


class Sched:
    def __init__(self, nc, es, nchan):
        self.nc = nc
        mk = lambda n: es.enter_context(nc.semaphore(n))
        self.pe = Eng("pe", nc.tensor, mk("s_pe"), False)
        self.act = Eng("act", nc.scalar, mk("s_act"), True)
        self.dve = Eng("dve", nc.vector, mk("s_dve"), True)
        self.pool = Eng("pool", nc.gpsimd, mk("s_pool"), True)
        self.sp = Eng("sp", nc.sync, mk("s_sp"), False)
        self.chans = [Chan(mk(f"s_ch{i}")) for i in range(nchan)]
        self.chan_i = 0
        self.reserved = set()
        self.keychan = {}
        self.res = {}
        self.flip = 0

    def chan(self, reserve=False):
        while True:
            c = self.chans[self.chan_i % len(self.chans)]
            self.chan_i += 1
            if c not in self.reserved:
                break
        if reserve:
            self.reserved.add(c)
        return c

    def _need(self, w, tok, raw):
        if tok[0] == "E":
            e, idx = tok[1], tok[2]
            if e is w and not w.self_sync:
                return
            v = e.value_for(idx)
            key = e
        else:
            key, v = tok[1], tok[2]
        if w.seen.get(key, 0) >= v:
            return
        w.q.wait_ge(key.sem, v)
        w.seen[key] = v

    def _deps(self, w, rd, wr):
        for k in rd:
            r = self.res.get(k)
            if r is not None and r[0] is not None:
                self._need(w, r[0], True)
        for k in wr:
            r = self.res.get(k)
            if r is not None:
                if r[0] is not None:
                    self._need(w, r[0], False)
                for t in r[1].values():
                    self._need(w, t, False)

    def _commit(self, tok, owner, rd, wr):
        for k in rd:
            r = self.res.get(k)
            if r is None:
                r = self.res[k] = [None, {}]
            r[1][owner] = tok
        for k in wr:
            self.res[k] = [tok, {}]

    def op(self, eng, fn, rd=(), wr=()):
        self._deps(eng, rd, wr)
        inst = fn()
        eng.last = inst
        tok = ("E", eng, eng.n)
        eng.n += 1
        if eng.self_sync:
            eng.value_for(eng.n - 1)
        self._commit(tok, eng, rd, wr)
        return inst

    def dma(self, q, ck, out, in_, rd=(), wr=(), **kw):
        if isinstance(ck, Chan):
            ch = ck
        else:
            ch = self.keychan.get(ck)
            if ch is None:
                ch = self.keychan[ck] = self.chan()
                assert len(self.keychan) <= len(self.chans) - len(self.reserved), "out of DMA channels"
        self._deps(q, rd, wr)
        q.q.dma_start(out=out, in_=in_, **kw).then_inc(ch.sem, 16)
        ch.count += 16
        tok = ("C", ch, ch.count)
        self._commit(tok, ch, rd, wr)

    def barrier(self, bar_tile):
        d = self.dve
        for e in (self.pe, self.act, self.pool):
            if e.n > 0:
                self._need(d, ("E", e, e.n - 1), True)
        for c in self.chans:
            if c.count > 0:
                self._need(d, ("C", c, c.count), True)
        self.op(d, lambda: self.nc.vector.memset(bar_tile, 0.0), wr=[("bar",)])
        tok = self.res[("bar",)][0]
        for e in (self.pe, self.act, self.sp):
            self._need(e, tok, True)
        self.res = {k: v for k, v in self.res.items() if k[0] in ("ring", "wconv")}
        self.keychan = {}
        self.chan_i = 0

    def evac(self, out, in_, rd, wr):
        self.flip ^= 1
        nc = self.nc
        if self.flip:
            return self.op(self.act, lambda: nc.scalar.activation(out=out, in_=in_, func=AF.Copy), rd, wr)
        return self.op(self.dve, lambda: nc.vector.tensor_copy(out=out, in_=in_), rd, wr)


class Ring:
    def __init__(self, S, es):
        nc = S.nc
        self.S = S
        self.slots = [es.enter_context(nc.sbuf_tensor(f"ring{i}", [128, SLOT], BF16)) for i in range(NSLOT)]
        self.ch = [S.chan(reserve=True) for _ in range(NSLOT)]
        self.i = 0

    def run(self, steps):
        S = self.S
        n = len(steps)
        base = self.i
        issued = 0

        def issue(j):
            si = (base + j) % NSLOT
            key = ("ring", si)
            for ld in steps[j][0]:
                vf, src = ld[0], ld[1]
                rdk = ld[2] if len(ld) > 2 else ()
                S.dma(S.pool, self.ch[si], out=vf(self.slots[si]), in_=src, rd=rdk, wr=[key], max_dma_last_dim=8192)

        for j in range(n):
            while issued < min(n, j + NSLOT):
                issue(issued)
                issued += 1
            si = (base + j) % NSLOT
            steps[j][1](self.slots[si], ("ring", si))
        self.i = base + n


def wview(kc, cols):
    return lambda slot: slot[:, 0:kc * cols].rearrange("p (k c) -> p k c", k=kc)


def wsrc(w2d, r0, kc, c0, cols):
    return w2d[r0:r0 + 128 * kc, c0:c0 + cols].rearrange("(k p) c -> p k c", p=128)


def build(stop_after=None, debug_outs=()):
    nc = bass.Bass("TRN2", target_bir_lowering=False)
    es = contextlib.ExitStack()
    dbg = set(debug_outs)

    def din(name, shape, dt=F32):
        return nc.dram_tensor(name, list(shape), dt, kind="ExternalInput").ap()

    def dout(name, shape, dt=F32):
        return nc.dram_tensor(name, list(shape), dt, kind="ExternalOutput").ap()

    def dscr(name, shape, dt):
        kind = "ExternalOutput" if name in dbg else "Internal"
        return nc.dram_tensor(name, list(shape), dt, kind=kind).ap()

    x_s = din("x_s", [NSQ, D])
    x_p = din("x_p", [2 * NP, D])
    ccol = din("ccol", [128, 16, 2])
    cckv = din("cckv", [DEPTH, CTX, 512])
    ckr = din("ckr", [DEPTH, CTX, 64])
    st_in = din("st_in", [DEPTH, 2, 4, 128, 128])
    w_ada = din("w_ada", [DEPTH, D, 6 * D])
    b_ada = din("b_ada", [DEPTH, 6 * D])
    bcol = din("bcol", [128, DEPTH, 96])
    gcol = din("gcol", [128, DEPTH, 2, 16])
    g_post_mix = din("g_post_mix", [DEPTH, D])
    g_post_ff = din("g_post_ff", [DEPTH, D])
    w_in = din("w_in", [DEPTH, D, INC])
    lbcol = din("lbcol", [128, DEPTH, 2, 4])
    hg_gain = din("hg_gain", [DEPTH, 512])
    qnorm = din("qnorm", [DEPTH, 768])
    kvnorm = din("kvnorm", [DEPTH, 512])
    w_q_b = din("w_q_b", [DEPTH, 768, 1536])
    w_kv_b = din("w_kv_b", [DEPTH, 512, 2048])
    w_out = din("w_out", [DEPTH, D, D])
    w_ff1 = din("w_ff1", [DEPTH, D, DFF])
    w_ff2 = din("w_ff2", [DEPTH, DFF, D])
    c_ident = din("c_ident", [128, 128])
    c_cosT = din("c_cosT", [64, NSQ])
    c_sinT = din("c_sinT", [64, NSQ])
    c_rmat = din("c_rmat", [128, 64])
    c_dftL = din("c_dftL", [2, NSQ, NSQ])
    c_dftP = din("c_dftP", [2, NP, NP])
    c_dftd = din("c_dftd", [128, 256])
    c_mask = din("c_mask", [2, 128, 128])

    y_s = dout("y_s", [NSQ, D])
    y_p = dout("y_p", [2 * NP, D])
    o_ckv = dout("o_ckv", [2, DEPTH, NP, 512])
    o_kr = dout("o_kr", [2, DEPTH, NP, 64])
    o_st = dout("o_st", [2, DEPTH, 2, 4, 128, 128])

    d_uT = dscr("d_uT", [512, NT], BF16)
    d_hqT = dscr("d_hqT", [512, NT], BF16)
    d_hv = dscr("d_hv", [NT, 512], BF16)
    d_lf = dscr("d_lf", [2, 512, NT], F32)
    d_kk = dscr("d_kk", [2, 512, NT], BF16)
    d_gate = dscr("d_gate", [NT, 512], BF16)
    d_QT = dscr("d_QT", [8, 192, NT], BF16)
    d_KT = dscr("d_KT", [8, 128, NK], BF16)
    d_KR = dscr("d_KR", [64, NK], BF16)
    d_V = dscr("d_V", [NK, 1024], BF16)
    d_yT = dscr("d_yT", [D, NT], BF16)
    d_x1 = dscr("d_x1", [NT, D], F32)
    d_h2T = dscr("d_h2T", [D, NT], BF16)
    d_xres = dscr("d_xres", [NT, D], F32)
    d_wob = dscr("d_wob", [D, D], BF16)
    d_w1b = dscr("d_w1b", [D, DFF], BF16)
    d_w2b = dscr("d_w2b", [DFF, D], BF16)
    d_G = dscr("d_G", [2, 2, D], F32)

    S = Sched(nc, es, nchan=40)
    PE, ACT, DVE, POOL, SP = S.pe, S.act, S.dve, S.pool, S.sp
    ring = Ring(S, es)

    uniq = [0]

    def sb(st, name, shape, dt):
        uniq[0] += 1
        return st.enter_context(nc.sbuf_tensor(f"{name}_{uniq[0]}", list(shape), dt))

    banks = [es.enter_context(nc.psum_tensor(f"bank{i}", [128, 512], F32)) for i in range(8)]
    bank_i = [0]

    def nb():
        i = bank_i[0] % 8
        bank_i[0] += 1
        return banks[i], ("ps", i)

    ident_f = sb(es, "ident_f", [128, 128], F32)
    ident_b = sb(es, "ident_b", [128, 128], BF16)
    bar_t = sb(es, "bar_t", [128, 2], F32)
    modc = sb(es, "modc", [128, 4, 16, 2], F32)
    AB = sb(es, "AB", [128, 4, 16, 2], F32)
    gcol_t = sb(es, "gcol_t", [128, DEPTH, 2, 16], F32)
    bcol_t = sb(es, "bcol_t", [128, DEPTH, 96], F32)
    lb_t = sb(es, "lb_t", [128, DEPTH, 2, 4], F32)
    oml_t = sb(es, "oml_t", [128, DEPTH, 2, 4], F32)
    eps_t = sb(es, "eps_t", [128, 1], F32)

    ch0 = S.chan()
    S.dma(SP, "ident_f", out=ident_f[:], in_=c_ident[:, :], wr=["ident_f"])
    S.dma(SP, "gcol", out=gcol_t[:], in_=gcol[:, :, :, :], wr=["gcol"])
    S.dma(SP, "bcol", out=bcol_t[:], in_=bcol[:, :, :], wr=["bcol"])
    S.dma(SP, "lbraw", out=lb_t[:], in_=lbcol[:, :, :, :], wr=["lbraw"])
    S.op(DVE, lambda: nc.vector.tensor_copy(out=ident_b[:], in_=ident_f[:]), rd=["ident_f"], wr=["ident_b"])
    S.op(DVE, lambda: nc.vector.memset(eps_t[:], EPS), wr=["eps"])
    with contextlib.ExitStack() as st:
        t0 = sb(st, "lbtmp", [128, 8], F32)
        S.op(DVE, lambda: nc.vector.tensor_tensor(out=t0[:], in0=lb_t[:, 0].rearrange("p a b -> p (a b)"),
                                                  in1=lb_t[:, 1].rearrange("p a b -> p (a b)"), op=ALU.subtract),
             rd=["lbraw"], wr=["lbtmp"])
        S.op(ACT, lambda: nc.scalar.activation(out=t0[:], in_=t0[:], func=AF.Exp), rd=["lbtmp"], wr=["lbtmp"])
        S.op(DVE, lambda: nc.vector.tensor_scalar(out=t0[:], in0=t0[:], scalar1=1.0, scalar2=None, op0=ALU.add),
             rd=["lbtmp"], wr=["lbtmp"])
        S.op(DVE, lambda: nc.vector.reciprocal(out=lb_t[:, 1].rearrange("p a b -> p (a b)"), in_=t0[:]),
             rd=["lbtmp"], wr=["lbraw"])
        S.op(DVE, lambda: nc.vector.memset(lb_t[:, 0].rearrange("p a b -> p (a b)"), 0.0), rd=["lbraw"], wr=["lbraw"])
        S.op(DVE, lambda: nc.vector.tensor_scalar(out=oml_t[:].rearrange("p l a b -> p (l a b)"),
                                                  in0=lb_t[:].rearrange("p l a b -> p (l a b)"),
                                                  scalar1=-1.0, scalar2=1.0, op0=ALU.mult, op1=ALU.add),
             rd=["lbraw"], wr=["oml"])
        S.barrier(bar_t[:])

    def phase0(l):
        with contextlib.ExitStack() as st:
            cc = sb(st, "cc", [128, 16, 2], F32)
            s2 = sb(st, "s2", [128, 16, 2], BF16)
            srep = sb(st, "srep", [128, 16, 2, 128], BF16)
            bg = sb(st, "bg", [128, 2, D], F32)
            gp = sb(st, "gp", [128, 2, D], F32)
            gst = sb(st, "gst", [128, 2, 2, 512], F32)
            ch = S.chan()
            S.dma(SP, "cc", out=cc[:], in_=ccol[:, :, :], wr=["cc"])
            for v in range(2):
                S.dma(SP, ("bg", v), out=bg[:, v, :], in_=b_ada[l, (2 + 3 * v) * D:(3 + 3 * v) * D].partition_broadcast(128),
                      wr=[("bg", v)])
            S.dma(SP, ("gp", 0), out=gp[:, 0, :], in_=g_post_mix[l, :].partition_broadcast(128), wr=[("gp", 0)])
            S.dma(SP, ("gp", 1), out=gp[:, 1, :], in_=g_post_ff[l, :].partition_broadcast(128), wr=[("gp", 1)])
            S.op(ACT, lambda: nc.scalar.activation(out=s2[:], in_=cc[:], func=AF.Silu), rd=["cc"], wr=["s2"])
            S.op(DVE, lambda: nc.vector.tensor_copy(
                out=srep[:].rearrange("p k c m -> p (k c) m"),
                in_=s2[:].rearrange("p k c -> p (k c)").unsqueeze(2).broadcast_to([128, 32, 128])),
                rd=["s2"], wr=["srep"])
            steps = []
            wl = w_ada[l]
            vec_of = {0: 0, 1: 1, 3: 2, 4: 3}
            gch = S.chan()
            for sec in range(6):
                for j in range(4):
                    c0 = sec * D + j * 512
                    loads = [(wview(16, 512), wsrc(wl, 0, 16, c0, 512))]
                    if sec in vec_of:
                        def comp(slot, key, sec=sec, j=j):
                            w3 = wview(16, 512)(slot)
                            bk, bkey = nb()
                            for sub in range(4):
                                for k in range(16):
                                    S.op(PE, lambda: nc.tensor.matmul(
                                        bk[:, sub * 2:sub * 2 + 2], lhsT=w3[:, k, sub * 128:(sub + 1) * 128],
                                        rhs=s2[:, k, :], start=(k == 0), stop=(k == 15)),
                                        rd=[key, "s2"], wr=[bkey])
                            vi = vec_of[sec]
                            ch0_ = sec * 16 + j * 4
                            S.op(DVE, lambda: nc.vector.tensor_tensor(
                                out=modc[:, vi, j * 4:(j + 1) * 4, :],
                                in0=bk[:, 0:8].rearrange("p (s c) -> p s c", c=2),
                                in1=bcol_t[:, l, ch0_:ch0_ + 4].unsqueeze(2).broadcast_to([128, 4, 2]),
                                op=ALU.add), rd=[bkey, "bcol"], wr=[("modc", vi)])
                    else:
                        def comp(slot, key, sec=sec, j=j):
                            w3 = wview(16, 512)(slot)
                            v = 0 if sec == 2 else 1
                            for cond in range(2):
                                bk, bkey = nb()
                                for k in range(16):
                                    S.op(PE, lambda: nc.tensor.matmul(
                                        bk[:, :], lhsT=srep[:, k, cond, :], rhs=w3[:, k, :],
                                        start=(k == 0), stop=(k == 15)), rd=[key, "srep"], wr=[bkey])
                                gk = ("gst", v, cond)
                                S.op(DVE, lambda: nc.vector.tensor_tensor(
                                    out=gst[:, v, cond, :], in0=bk[:, :], in1=bg[:, v, j * 512:(j + 1) * 512],
                                    op=ALU.add), rd=[bkey, ("bg", v)], wr=[gk])
                                S.op(DVE, lambda: nc.vector.tensor_tensor(
                                    out=gst[:, v, cond, :], in0=gst[:, v, cond, :], in1=gp[:, v, j * 512:(j + 1) * 512],
                                    op=ALU.mult), rd=[gk, ("gp", v)], wr=[gk])
                                S.dma(SP, gk, out=d_G[v, cond, j * 512:(j + 1) * 512].unsqueeze(0),
                                      in_=gst[0:1, v, cond, :], rd=[gk], wr=[("dG", v, cond, j)])
                    steps.append((loads, comp))
            ring.run(steps)
            for half in range(2):
                S.op(DVE, lambda: nc.vector.tensor_scalar(
                    out=AB[:, 2 * half].rearrange("p k c -> p (k c)"),
                    in0=modc[:, 2 * half + 1].rearrange("p k c -> p (k c)"),
                    scalar1=1.0, scalar2=None, op0=ALU.add), rd=[("modc", 2 * half + 1)], wr=[("AB", 2 * half)])
                S.op(DVE, lambda: nc.vector.tensor_tensor(
                    out=AB[:, 2 * half], in0=AB[:, 2 * half],
                    in1=gcol_t[:, l, half, :].unsqueeze(2).broadcast_to([128, 16, 2]), op=ALU.mult),
                    rd=[("AB", 2 * half), "gcol"], wr=[("AB", 2 * half)])
                S.op(DVE, lambda: nc.vector.tensor_copy(out=AB[:, 2 * half + 1], in_=modc[:, 2 * half]),
                     rd=[("modc", 2 * half)], wr=[("AB", 2 * half + 1)])
            S.barrier(bar_t[:])

    def rms_front(st_name, xt, xkey, xn, xnkey, small, idx):
        junk = small["junk"]
        ss = small["ss"][:, idx:idx + 1]
        S.op(ACT, lambda: nc.scalar.activation(out=junk[:], in_=xt, func=AF.Square, accum_out=ss),
             rd=[xkey], wr=list(small.get("jkeys", ["junk"])) + [("ss", idx)])
        S.op(ACT, lambda: nc.scalar.activation(out=ss, in_=ss, func=AF.Sqrt, bias=eps_t[:], scale=1.0 / D),
             rd=[("ss", idx), "eps"], wr=[("ss", idx)])
        S.op(DVE, lambda: nc.vector.reciprocal(out=ss, in_=ss), rd=[("ss", idx)], wr=[("ss", idx)])
        S.op(DVE, lambda: nc.vector.tensor_scalar(out=xn, in0=xt, scalar1=ss, scalar2=None, op0=ALU.mult),
             rd=[xkey, ("ss", idx)], wr=[xnkey])

    def transpose_mod(xn_tiles, xn_keys, hT, Aidx, cond):
        for g in range(8):
            bk, bkey = nb()
            pb = bk[:, :].bitcast(BF16).rearrange("p (c t) -> p c t", c=2)
            for cc in range(2):
                c = g * 2 + cc
                for sub in range(4):
                    S.op(PE, lambda: nc.tensor.transpose(
                        out=pb[:, cc, sub * 128:(sub + 1) * 128], in_=xn_tiles[sub][:, c * 128:(c + 1) * 128],
                        identity=ident_b[:]), rd=[xn_keys[sub], "ident_b"], wr=[bkey])
            for cc in range(2):
                c = g * 2 + cc
                S.op(ACT, lambda: nc.scalar.activation(
                    out=hT[:, c, :], in_=pb[:, cc, :], func=AF.Identity,
                    scale=AB[:, Aidx, c, cond:cond + 1], bias=AB[:, Aidx + 1, c, cond:cond + 1]),
                    rd=[bkey, ("AB", Aidx), ("AB", Aidx + 1)], wr=[("hT", c)])

    def tile_src(l, tt):
        if l == 0:
            if tt < 4:
                return [x_s[tt * 512 + s * 128: tt * 512 + (s + 1) * 128, :] for s in range(4)]
            return [x_p[s * 128:(s + 1) * 128, :] for s in range(4)]
        return [d_xres[tt * 512 + s * 128: tt * 512 + (s + 1) * 128, :] for s in range(4)]

    def kv_backend(st, l, ckv_tm, ckv_keys, kr_tm, kr_keys, key0, pos0):
        ckvT = sb(st, "ckvT", [128, 4, 512], BF16)
        krT = sb(st, "krT", [64, 512], F32)
        krb = sb(st, "krb", [128, 512], BF16)
        kst = sb(st, "kst", [128, 2, 512], BF16)
        vst = sb(st, "vst", [128, 4, 1024], BF16)
        chk = S.chan()
        chv = S.chan()
        for s in range(4):
            bk, bkey = nb()
            for c in range(4):
                S.op(PE, lambda: nc.tensor.transpose(out=bk[:, c * 128:(c + 1) * 128],
                                                     in_=ckv_tm[s][:, c * 128:(c + 1) * 128], identity=ident_f[:]),
                     rd=[ckv_keys[s], "ident_f"], wr=[bkey])
            S.evac(ckvT[:, :, s * 128:(s + 1) * 128], bk[:, :].rearrange("p (c t) -> p c t", c=4),
                   rd=[bkey], wr=[("ckvT", s)])
            bk2, bkey2 = nb()
            S.op(PE, lambda: nc.tensor.transpose(out=bk2[0:64, 0:128], in_=kr_tm[s], identity=ident_f[:]),
                 rd=[kr_keys[s], "ident_f"], wr=[bkey2])
            S.evac(krT[:, s * 128:(s + 1) * 128], bk2[0:64, 0:128], rd=[bkey2], wr=[("krT", s)])
        ckv_all = [("ckvT", s) for s in range(4)]
        krT_all = [("krT", s) for s in range(4)]
        S.op(DVE, lambda: nc.vector.memset(krb[64:128, :], 0.0), wr=["krbpad"])
        S.op(DVE, lambda: nc.vector.tensor_copy(out=krb[0:64, :], in_=krT[:]), rd=krT_all, wr=["krb"])
        if pos0 is not None:
            rope_apply(st, krT[:], krT_all, krb[:, :], ["krb", "krbpad"], pos0, krb[0:64, :], "krb2")
            S.dma(SP, "krb", out=d_KR[:, key0:key0 + 512], in_=krb[0:64, :], rd=["krb2"], wr=[("dKR", key0)])
        else:
            S.dma(SP, "krb", out=d_KR[:, key0:key0 + 512], in_=krb[0:64, :], rd=["krb"], wr=[("dKR", key0)])
        steps = []
        wl = w_kv_b[l]
        for hp in range(4):
            loads = [(wview(4, 512), wsrc(wl, 0, 4, hp * 512, 512))]

            def comp(slot, key, hp=hp):
                w3 = wview(4, 512)(slot)
                for hh in range(2):
                    bk, bkey = nb()
                    for k in range(4):
                        S.op(PE, lambda: nc.tensor.matmul(bk[:, :], lhsT=w3[:, k, hh * 256:hh * 256 + 128],
                                                          rhs=ckvT[:, k, :], start=(k == 0), stop=(k == 3)),
                             rd=[key] + ckv_all, wr=[bkey])
                    S.evac(kst[:, hh, :], bk[:, :], rd=[bkey], wr=[("kst", hh)])
                    S.dma(SP, ("kst", hh), out=d_KT[hp * 2 + hh, :, key0:key0 + 512], in_=kst[:, hh, :],
                          rd=[("kst", hh)], wr=[("dKT", hp * 2 + hh, key0)])
                wv = w3[:, :, :].rearrange("p k (h c) -> p k h c", h=2)
                for s in range(4):
                    bk, bkey = nb()
                    for k in range(4):
                        S.op(PE, lambda: nc.tensor.matmul(bk[:, 0:256].rearrange("p (h c) -> p h c", h=2),
                                                          lhsT=ckvT[:, k, s * 128:(s + 1) * 128],
                                                          rhs=wv[:, k, :, 128:256], start=(k == 0), stop=(k == 3)),
                             rd=[key, ("ckvT", s)], wr=[bkey])
                    S.evac(vst[:, s, hp * 256:(hp + 1) * 256], bk[:, 0:256], rd=[bkey], wr=[("vst", s)])
            steps.append((loads, comp))
        ring.run(steps)
        for s in range(4):
            S.dma(SP, ("vst", s), out=d_V[key0 + s * 128:key0 + (s + 1) * 128, :], in_=vst[:, s, :],
                  rd=[("vst", s)], wr=[("dV", key0, s)])

    rope_tabs = {}

    def rope_apply(st, x_f32, xkeys, xb, xbkey, pos0, out_b, outkey):
        cosT, sinT, rmb = rope_tabs["cos"], rope_tabs["sin"], rope_tabs["rm"]
        cnt = rope_tabs["n"] = rope_tabs.get("n", 0) + 1
        t1 = rope_tabs["t1"][cnt % 2]
        t1k = ("ropet1", cnt % 2)
        S.op(DVE, lambda: nc.vector.tensor_tensor(out=t1[:], in0=x_f32, in1=cosT[:, pos0:pos0 + 512], op=ALU.mult),
             rd=list(xkeys) + ["ropetab"], wr=[t1k])
        if DBG_STAGE[0] == 96 and DBG_STAGE[1] <= 1:
            S.op(DVE, lambda: nc.vector.tensor_copy(out=out_b, in_=t1[:]), rd=[t1k], wr=[outkey])
            return
        bk, bkey = nb()
        S.op(PE, lambda: nc.tensor.matmul(bk[0:64, :], lhsT=rmb[:], rhs=xb, start=True, stop=True),
             rd=list(xbkey) + ["ropetab"], wr=[bkey])
        if DBG_STAGE[0] == 96 and DBG_STAGE[1] <= 2:
            S.op(DVE, lambda: nc.vector.tensor_copy(out=out_b, in_=bk[0:64, :]), rd=[t1k, bkey], wr=[outkey])
            return
        t2 = rope_tabs["t2"][cnt % 2]
        t2k = ("ropet2", cnt % 2)
        S.op(DVE, lambda: nc.vector.tensor_tensor(out=t2[:], in0=bk[0:64, :], in1=sinT[:, pos0:pos0 + 512],
                                                  op=ALU.mult), rd=[bkey, "ropetab"], wr=[t2k])
        S.op(DVE, lambda: nc.vector.tensor_tensor(out=out_b, in0=t1[:], in1=t2[:], op=ALU.add),
             rd=[t1k, t2k] + list(xbkey), wr=[outkey])

    def load_rope_tabs(st):
        cosT = sb(st, "cosT", [64, NSQ], F32)
        sinT = sb(st, "sinT", [64, NSQ], F32)
        rmf = sb(st, "rmf", [128, 64], F32)
        rmb = sb(st, "rmb", [128, 64], BF16)
        ch = S.chan()
        S.dma(SP, "ropetab_c", out=cosT[:], in_=c_cosT[:, :], wr=["ropetab_c"])
        S.dma(SP, "ropetab_s", out=sinT[:], in_=c_sinT[:, :], wr=["ropetab_s"])
        S.dma(SP, "rmf", out=rmf[:], in_=c_rmat[:, :], wr=["rmf"])
        S.op(DVE, lambda: nc.vector.tensor_copy(out=rmb[:], in_=rmf[:]), rd=["rmf", "ropetab_c", "ropetab_s"],
             wr=["ropetab"])
        rope_tabs.update(cos=cosT, sin=sinT, rm=rmb,
                         t1=[sb(st, f"ropet1_{i}", [64, 512], F32) for i in range(2)],
                         t2=[sb(st, f"ropet2_{i}", [64, 512], F32) for i in range(2)])

    def phase1_ctx(l):
        with contextlib.ExitStack() as st:
            ck = [sb(st, f"cck{s}", [128, 512], F32) for s in range(4)]
            kr = [sb(st, f"ckr{s}", [128, 64], F32) for s in range(4)]
            ch = S.chan()
            for s in range(4):
                S.dma(SP, ("cck", s), out=ck[s][:], in_=cckv[l, s * 128:(s + 1) * 128, :], wr=[("cck", s)])
                S.dma(SP, ("ckr", s), out=kr[s][:], in_=ckr[l, s * 128:(s + 1) * 128, :], wr=[("ckr", s)])
            kv_backend(st, l, [t[:] for t in ck], [("cck", s) for s in range(4)],
                       [t[:] for t in kr], [("ckr", s) for s in range(4)], 0, None)
            S.barrier(bar_t[:])

    def phase1(l, tt):
        cond = 0 if tt < 4 else 1
        tok0 = tt * 512
        is_s = tt < 4
        with contextlib.ExitStack() as st:
            hT = sb(st, "hT", [128, 16, 512], BF16)
            with contextlib.ExitStack() as st2:
                xt = [sb(st2, f"xt{i}", [128, D], F32) for i in range(2)]
                xn = [sb(st2, f"xn{i}", [128, D], BF16) for i in range(4)]
                small = {"junk": sb(st2, "junk", [128, D], BF16), "ss": sb(st2, "ss", [128, 8], F32)}
                chx = [S.chan() for _ in range(2)]
                srcs = tile_src(l, tt)
                for s in range(4):
                    S.dma(SP, ("xt", s % 2), out=xt[s % 2][:], in_=srcs[s], wr=[("xt", s % 2)])
                    rms_front("p1", xt[s % 2][:], ("xt", s % 2), xn[s][:], ("xn", s), small, s)
                transpose_mod([t for t in xn], [("xn", s) for s in range(4)], hT, 0, cond)
                S.barrier(bar_t[:])
            small = {"junk": sb(st, "junk", [128, 768], BF16), "ss": sb(st, "ss", [128, 8], F32)}
            if DBG_STAGE[0] <= 1:
                return
            fst = sb(st, "fst", [128, 4, 512], BF16)
            lst = sb(st, "lst", [128, 4, 512], F32)
            sg = [sb(st, f"sg{i}", [128, 512], F32) for i in range(2)]
            tst = sb(st, "tst", [128, 4, 512], BF16)
            mla = [sb(st, f"mla{i}", [128, 1344], F32) for i in range(4)]
            qn = sb(st, "qn", [128, 768], BF16)
            ckvn = [sb(st, f"ckvn{i}", [128, 512], F32) for i in range(4)]
            qnT = sb(st, "qnT", [128, 6, 512], BF16)
            qnb = sb(st, "qnb", [128, 768], F32)
            kvb = sb(st, "kvb", [128, 512], F32)
            qst = sb(st, "qst", [128, 2, 512], BF16)
            qrb = sb(st, "qrb", [128, 2, 512], BF16)
            S.op(DVE, lambda: nc.vector.memset(qrb[64:128, :, :], 0.0), wr=["qrbpad"])
            qro = sb(st, "qro", [64, 2, 512], BF16)
            qxf = sb(st, "qxf", [64, 2, 512], F32)
            if is_s:
                load_rope_tabs(st)
            chs = S.chan()
            S.dma(SP, "qnb", out=qnb[:], in_=qnorm[l, :].partition_broadcast(128), wr=["qnb"])
            S.dma(SP, "kvb", out=kvb[:], in_=kvnorm[l, :].partition_broadcast(128), wr=["kvb"])
            hT_all = [("hT", c) for c in range(16)]
            wl = w_in[l]
            cho = S.chan()
            steps = []

            def fm_group(c0, post):
                loads = [(wview(16, 512), wsrc(wl, 0, 16, c0, 512))]

                def comp(slot, key):
                    w3 = wview(16, 512)(slot)
                    for j in range(4):
                        bk, bkey = nb()
                        for k in range(16):
                            S.op(PE, lambda: nc.tensor.matmul(bk[:, :], lhsT=w3[:, k, j * 128:(j + 1) * 128],
                                                              rhs=hT[:, k, :], start=(k == 0), stop=(k == 15)),
                                 rd=[key, ("hT", k)], wr=[bkey])
                        post(j, bk, bkey)
                steps.append((loads, comp))

            def tm_group(c0, ncols, post):
                loads = [(wview(16, ncols), wsrc(wl, 0, 16, c0, ncols))]

                def comp(slot, key):
                    w3 = wview(16, ncols)(slot)
                    for s in range(4):
                        bk, bkey = nb()
                        for k in range(16):
                            S.op(PE, lambda: nc.tensor.matmul(bk[:, 0:ncols], lhsT=hT[:, k, s * 128:(s + 1) * 128],
                                                              rhs=w3[:, k, :], start=(k == 0), stop=(k == 15)),
                                 rd=[key, ("hT", k)], wr=[bkey])
                        post(s, bk, bkey)
                steps.append((loads, comp))

            def post_plain(dst):
                def post(j, bk, bkey):
                    S.evac(fst[:, j, :], bk[:, :], rd=[bkey], wr=[("fst", j)])
                    S.dma(SP, ("fst", j), out=dst[j * 128:(j + 1) * 128, tok0:tok0 + 512], in_=fst[:, j, :],
                          rd=[("fst", j)], wr=[("dfm", id(dst), j, tt)])
                return post

            def post_gate(dirn):
                def post(j, bk, bkey):
                    g = sg[j % 2]
                    gk = ("sg", j % 2)
                    S.op(ACT, lambda: nc.scalar.activation(out=g[:], in_=bk[:, :], func=AF.Sigmoid),
                         rd=[bkey], wr=[gk])
                    S.op(DVE, lambda: nc.vector.tensor_scalar(out=g[:], in0=g[:], scalar1=oml_t[:, l, dirn, j:j + 1],
                                                              scalar2=lb_t[:, l, dirn, j:j + 1], op0=ALU.mult,
                                                              op1=ALU.add), rd=[gk, "oml", "lbraw"], wr=[gk])
                    S.op(DVE, lambda: nc.vector.tensor_scalar(out=fst[:, j, :], in0=g[:], scalar1=-1.0, scalar2=1.0,
                                                              op0=ALU.mult, op1=ALU.add), rd=[gk], wr=[("fst", j)])
                    S.op(DVE, lambda: nc.vector.tensor_scalar(out=g[:], in0=g[:], scalar1=1e-30, scalar2=None,
                                                              op0=ALU.max), rd=[gk], wr=[gk])
                    S.op(ACT, lambda: nc.scalar.activation(out=lst[:, j, :], in_=g[:], func=AF.Ln),
                         rd=[gk], wr=[("lst", j)])
                    S.dma(SP, ("fst", j), out=d_kk[dirn, j * 128:(j + 1) * 128, tok0:tok0 + 512], in_=fst[:, j, :],
                          rd=[("fst", j)], wr=[("dkk", dirn, j, tt)])
                    S.dma(SP, ("lst", j), out=d_lf[dirn, j * 128:(j + 1) * 128, tok0:tok0 + 512], in_=lst[:, j, :],
                          rd=[("lst", j)], wr=[("dlf", dirn, j, tt)])
                return post

            def post_tm(dst, silu):
                def post(s, bk, bkey):
                    if silu:
                        S.op(ACT, lambda: nc.scalar.activation(out=tst[:, s, :], in_=bk[:, :], func=AF.Silu),
                             rd=[bkey], wr=[("tst", s)])
                    else:
                        S.evac(tst[:, s, :], bk[:, :], rd=[bkey], wr=[("tst", s)])
                    S.dma(SP, ("tst", s), out=dst[tok0 + s * 128:tok0 + (s + 1) * 128, :], in_=tst[:, s, :],
                          rd=[("tst", s)], wr=[("dtm", id(dst), s, tt)])
                return post

            def post_mla(off, ncols):
                def post(s, bk, bkey):
                    S.evac(mla[s][:, off:off + ncols], bk[:, 0:ncols], rd=[bkey], wr=[("mla", s, off)])
                return post

            fm_group(0, post_plain(d_uT))
            fm_group(512, post_plain(d_hqT))
            tm_group(1024, 512, post_tm(d_hv, False))
            fm_group(1536, post_gate(0))
            fm_group(2048, post_gate(1))
            tm_group(2560, 512, post_tm(d_gate, True))
            tm_group(3072, 512, post_mla(0, 512))
            tm_group(3584, 512, post_mla(512, 512))
            tm_group(4096, 320, post_mla(1024, 320))
            if DBG_STAGE[0] <= 2:
                steps = steps[:DBG_STAGE[1]]
            ring.run(steps)
            if DBG_STAGE[0] <= 2:
                S.barrier(bar_t[:])
                return

            ss = small["ss"]
            junk = small["junk"]
            for s in range(4):
                mk = [("mla", s, 0), ("mla", s, 512), ("mla", s, 1024)]
                for (i, (a, b)) in enumerate(((0, 768), (768, 1280))):
                    sidx = 4 + i
                    ssv = ss[:, sidx:sidx + 1]
                    n = b - a
                    S.op(ACT, lambda: nc.scalar.activation(out=junk[:, 0:n], in_=mla[s][:, a:b], func=AF.Square,
                                                           accum_out=ssv), rd=mk, wr=["junk", ("ss", sidx)])
                    S.op(ACT, lambda: nc.scalar.activation(out=ssv, in_=ssv, func=AF.Sqrt, bias=eps_t[:],
                                                           scale=1.0 / n), rd=[("ss", sidx), "eps"], wr=[("ss", sidx)])
                    S.op(DVE, lambda: nc.vector.reciprocal(out=ssv, in_=ssv), rd=[("ss", sidx)], wr=[("ss", sidx)])
                S.op(DVE, lambda: nc.vector.scalar_tensor_tensor(out=qn[:], in0=mla[s][:, 0:768], scalar=ss[:, 4:5],
                                                                 in1=qnb[:], op0=ALU.mult, op1=ALU.mult),
                     rd=mk + [("ss", 4), "qnb"], wr=["qn"])
                S.op(DVE, lambda: nc.vector.scalar_tensor_tensor(out=ckvn[s][:], in0=mla[s][:, 768:1280],
                                                                 scalar=ss[:, 5:6], in1=kvb[:], op0=ALU.mult,
                                                                 op1=ALU.mult),
                     rd=mk + [("ss", 5), "kvb"], wr=[("ckvn", s)])
                if not is_s:
                    sq, r0 = divmod(s * 128, NP)
                    S.dma(SP, ("ckvn", s), out=o_ckv[sq, l, r0:r0 + 128, :], in_=ckvn[s][:], rd=[("ckvn", s)],
                          wr=[("ockv", s)])
                    S.dma(SP, ("mla", s), out=o_kr[sq, l, r0:r0 + 128, :], in_=mla[s][:, 1280:1344], rd=mk,
                          wr=[("okr", s)])
                bk, bkey = nb()
                pb = bk[:, :].bitcast(BF16)
                for c in range(6):
                    S.op(PE, lambda: nc.tensor.transpose(out=pb[:, c * 128:(c + 1) * 128],
                                                         in_=qn[:, c * 128:(c + 1) * 128], identity=ident_b[:]),
                         rd=["qn", "ident_b"], wr=[bkey])
                S.evac(qnT[:, :, s * 128:(s + 1) * 128], pb[:, 0:768].rearrange("p (c t) -> p c t", c=6),
                       rd=[bkey], wr=[("qnT", s)])
            qnT_all = [("qnT", s) for s in range(4)]
            if DBG_STAGE[0] <= 3:
                S.barrier(bar_t[:])
                return

            steps = []
            wq = w_q_b[l]
            for hp in range(4):
                loads = [(wview(6, 384), wsrc(wq, 0, 6, hp * 384, 384))]

                def comp(slot, key, hp=hp):
                    w3 = wview(6, 384)(slot)
                    for hh in range(2):
                        h = hp * 2 + hh
                        bk, bkey = nb()
                        for k in range(6):
                            S.op(PE, lambda: nc.tensor.matmul(bk[:, :], lhsT=w3[:, k, hh * 192:hh * 192 + 128],
                                                              rhs=qnT[:, k, :], start=(k == 0), stop=(k == 5)),
                                 rd=[key] + qnT_all, wr=[bkey])
                        S.evac(qst[:, hh, :], bk[:, :], rd=[bkey], wr=[("qst", hh)])
                        S.dma(SP, ("qst", hh), out=d_QT[h, 0:128, tok0:tok0 + 512], in_=qst[:, hh, :], rd=[("qst", hh)],
                              wr=[("dQT", h, 0, tt)])
                        bk2, bkey2 = nb()
                        for k in range(6):
                            S.op(PE, lambda: nc.tensor.matmul(bk2[0:64, :], lhsT=w3[:, k, hh * 192 + 128:hh * 192 + 192],
                                                              rhs=qnT[:, k, :], start=(k == 0), stop=(k == 5)),
                                 rd=[key] + qnT_all, wr=[bkey2])
                        S.op(ACT, lambda: nc.scalar.activation(out=qxf[:, hh, :], in_=bk2[0:64, :], func=AF.Copy),
                             rd=[bkey2], wr=[("qxf", hh)])
                        S.op(DVE, lambda: nc.vector.tensor_copy(out=qrb[0:64, hh, :], in_=qxf[:, hh, :]),
                             rd=[("qxf", hh)], wr=[("qrb", hh)])
                        if is_s and DBG_STAGE[0] not in (98, 97):
                            rope_apply(st, qxf[:, hh, :], [("qxf", hh)], qrb[:, hh, :], [("qrb", hh), "qrbpad"], tok0, qro[:, hh, :],
                                       ("qro", hh))
                            S.dma(SP, ("qro", hh), out=d_QT[h, 128:192, tok0:tok0 + 512], in_=qro[:, hh, :],
                                  rd=[("qro", hh)], wr=[("dQT", h, 1, tt)])
                        else:
                            S.dma(SP, ("qrb", hh), out=d_QT[h, 128:192, tok0:tok0 + 512], in_=qrb[0:64, hh, :],
                                  rd=[("qrb", hh)], wr=[("dQT", h, 1, tt)])
                steps.append((loads, comp))
            ring.run(steps)

            kv_backend(st, l, [t[:] for t in ckvn], [("ckvn", s) for s in range(4)],
                       [mla[s][:, 1280:1344] for s in range(4)],
                       [("mla", s, 1024) for s in range(4)], CTX + tok0, tok0 if (is_s and DBG_STAGE[0] not in (98,)) else None)
            S.barrier(bar_t[:])


    def bankx(i):
        return banks[i], ("ps", i)

    def phase_fn(l, tok0, N, dft):
        nch = N // 128
        ncol = min(512, N)
        nblk = N // ncol
        with contextlib.ExitStack() as st:
            uT = sb(st, "uT", [128, 4, N], BF16)
            ab = sb(st, "ab", [128, nch, 4, 256], BF16)
            ddf = sb(st, "ddf", [128, 256], F32)
            ddb = sb(st, "ddb", [128, 256], BF16)
            yst = sb(st, "yst", [128, 4, 512], BF16)
            S.dma(SP, "ddf", out=ddf[:], in_=c_dftd[:, :], wr=["ddf"])
            S.op(DVE, lambda: nc.vector.tensor_copy(out=ddb[:], in_=ddf[:]), rd=["ddf"], wr=["ddb"])
            for h in range(4):
                S.dma(SP, ("uT", h), out=uT[:, h, :], in_=d_uT[h * 128:(h + 1) * 128, tok0:tok0 + N], wr=[("uT", h)])
            for m in range(nch):
                for hp in range(2):
                    bk, bkey = nb()
                    for hh in range(2):
                        h = hp * 2 + hh
                        S.op(PE, lambda: nc.tensor.matmul(bk[:, hh * 256:(hh + 1) * 256],
                                                          lhsT=uT[:, h, m * 128:(m + 1) * 128], rhs=ddb[:],
                                                          start=True, stop=True), rd=[("uT", h), "ddb"], wr=[bkey])
                    S.evac(ab[:, m, hp * 2:(hp + 1) * 2, :], bk[:, :].rearrange("p (h c) -> p h c", h=2),
                           rd=[bkey], wr=[("ab", m, hp)])
            ab_all = [("ab", m, hp) for m in range(nch) for hp in range(2)]
            steps = []
            for nbk in range(nblk):
                for part in range(2):
                    src = dft[part, :, nbk * ncol:(nbk + 1) * ncol].rearrange("(k p) c -> p k c", p=128)
                    loads = [(wview(nch, ncol), src)]

                    def comp(slot, key, nbk=nbk, part=part):
                        w3 = wview(nch, ncol)(slot)
                        for h in range(4):
                            bk, bkey = bankx((nbk % 2) * 4 + h)
                            for m in range(nch):
                                S.op(PE, lambda: nc.tensor.matmul(
                                    bk[:, 0:ncol], lhsT=ab[:, m, h, part * 128:(part + 1) * 128], rhs=w3[:, m, :],
                                    start=(part == 0 and m == 0), stop=(part == 1 and m == nch - 1)),
                                    rd=[key] + ab_all, wr=[bkey])
                            if part == 1:
                                S.evac(yst[:, h, 0:ncol], bk[:, 0:ncol], rd=[bkey], wr=[("yst", h)])
                                S.dma(SP, ("yst", h), out=d_yT[h * 128:(h + 1) * 128,
                                                              tok0 + nbk * ncol:tok0 + (nbk + 1) * ncol],
                                      in_=yst[:, h, 0:ncol], rd=[("yst", h)], wr=[("dyT", h, tok0, nbk)])
                    steps.append((loads, comp))
            ring.run(steps)
            S.barrier(bar_t[:])

    def phase_attn(l, tok0, N, key_lo, nkeys):
        nkc = nkeys // 128
        qb = min(512, N)
        nqb = N // qb
        scale = 192.0 ** -0.5
        with contextlib.ExitStack() as st:
            kr = sb(st, "kr", [128, nkeys], BF16)
            ones = sb(st, "ones", [128, 128], BF16)
            qn_ = [sb(st, f"aqn{i}", [128, N], BF16) for i in range(2)]
            qr_ = [sb(st, f"aqr{i}", [128, N], BF16) for i in range(2)]
            kn_ = [sb(st, f"akn{i}", [128, nkeys], BF16) for i in range(2)]
            v_ = [sb(st, f"av{i}", [128, nkc, 128], BF16) for i in range(2)]
            pT = [sb(st, f"pT{i}", [128, 512], BF16) for i in range(4)]
            rec = [sb(st, f"rec{i}", [128, 512], F32) for i in range(2)]
            yst = [sb(st, f"ayst{i}", [128, 512], BF16) for i in range(2)]
            S.op(DVE, lambda: nc.vector.memset(kr[64:128, :], 0.0), wr=["krpad"])
            for i in range(2):
                S.op(DVE, lambda: nc.vector.memset(qr_[i][64:128, :], 0.0), wr=[("aqrpad", i)])
            S.dma(SP, "kr", out=kr[0:64, :], in_=d_KR[:, key_lo:key_lo + nkeys], wr=["kr"])
            S.op(DVE, lambda: nc.vector.memset(ones[:], 1.0), wr=["ones"])
            pi = 0
            qcount = 0
            for h in range(8):
                b = h % 2
                S.dma(SP, ("aqn", b), out=qn_[b][:], in_=d_QT[h, 0:128, tok0:tok0 + N], wr=[("aqn", b)])
                S.dma(SP, ("aqr", b), out=qr_[b][0:64, :], in_=d_QT[h, 128:192, tok0:tok0 + N], wr=[("aqr", b)])
                S.dma(SP, ("akn", b), out=kn_[b][:], in_=d_KT[h, :, key_lo:key_lo + nkeys], wr=[("akn", b)])
                S.dma(SP, ("av", b), out=v_[b][:],
                      in_=d_V[key_lo:key_lo + nkeys, h * 128:(h + 1) * 128].rearrange("(c p) d -> p c d", p=128),
                      wr=[("av", b)])
                for qi in range(nqb):
                    par = qcount % 2
                    qcount += 1
                    bo, bokey = bankx(par * 2)
                    br, brkey = bankx(par * 2 + 1)
                    qs = slice(qi * qb, (qi + 1) * qb)
                    LOOK = 3
                    pend = []

                    def emit_scores(kc):
                        nonlocal pi
                        bs, bskey = bankx(4 + (pi % 4))
                        p_t = pT[pi % 4]
                        pkey = ("pT", pi % 4)
                        pi += 1
                        ks = slice(kc * 128, (kc + 1) * 128)
                        S.op(PE, lambda: nc.tensor.matmul(bs[:, 0:qb], lhsT=kn_[b][:, ks], rhs=qn_[b][:, qs],
                                                          start=True, stop=False),
                             rd=[("akn", b), ("aqn", b)], wr=[bskey])
                        S.op(PE, lambda: nc.tensor.matmul(bs[:, 0:qb], lhsT=kr[:, ks], rhs=qr_[b][:, qs],
                                                          start=False, stop=True),
                             rd=["kr", "krpad", ("aqr", b), ("aqrpad", b)], wr=[bskey])
                        S.op(ACT, lambda: nc.scalar.activation(out=p_t[:, 0:qb], in_=bs[:, 0:qb], func=AF.Exp,
                                                               scale=scale), rd=[bskey], wr=[pkey])
                        pend.append((kc, p_t, pkey))

                    def emit_pv():
                        kc, p_t, pkey = pend.pop(0)
                        S.op(PE, lambda: nc.tensor.matmul(bo[:, 0:qb], lhsT=v_[b][:, kc, :], rhs=p_t[:, 0:qb],
                                                          start=(kc == 0), stop=(kc == nkc - 1)),
                             rd=[("av", b), pkey], wr=[bokey])
                        S.op(PE, lambda: nc.tensor.matmul(br[:, 0:qb], lhsT=ones[:], rhs=p_t[:, 0:qb],
                                                          start=(kc == 0), stop=(kc == nkc - 1)),
                             rd=["ones", pkey], wr=[brkey])
                    for kc in range(nkc):
                        emit_scores(kc)
                        if len(pend) > LOOK:
                            emit_pv()
                    while pend:
                        emit_pv()
                    S.op(DVE, lambda: nc.vector.reciprocal(out=rec[par][:, 0:qb], in_=br[:, 0:qb]),
                         rd=[brkey], wr=[("rec", par)])
                    S.op(DVE, lambda: nc.vector.tensor_tensor(out=yst[par][:, 0:qb], in0=bo[:, 0:qb],
                                                              in1=rec[par][:, 0:qb], op=ALU.mult),
                         rd=[bokey, ("rec", par)], wr=[("ayst", par)])
                    S.dma(SP, ("ayst", par), out=d_yT[1024 + h * 128:1024 + (h + 1) * 128,
                                                      tok0 + qi * qb:tok0 + (qi + 1) * qb],
                          in_=yst[par][:, 0:qb], rd=[("ayst", par)], wr=[("dyTa", h, tok0, qi)])
            S.barrier(bar_t[:])

    def phase3a(l, tt):
        cond = 0 if tt < 4 else 1
        tok0 = tt * 512
        with contextlib.ExitStack() as st:
            yT = sb(st, "yT", [128, 16, 512], BF16)
            G1 = sb(st, "G1", [128, D], F32)
            xt = [sb(st, f"x3_{i}", [128, D], F32) for i in range(4)]
            yo = [sb(st, f"yo{i}", [128, D], F32) for i in range(4)]
            xn = [sb(st, f"xn3_{i}", [128, D], BF16) for i in range(4)]
            h2T = sb(st, "h2T", [128, 16, 512], BF16)
            small = {"junk": sb(st, "junk3", [128, D], BF16), "ss": sb(st, "ss3", [128, 8], F32)}
            for c in range(16):
                S.dma(SP, ("yT", c), out=yT[:, c, :], in_=d_yT[c * 128:(c + 1) * 128, tok0:tok0 + 512],
                      wr=[("yT", c)])
            S.dma(SP, "G1", out=G1[:], in_=d_G[0, cond, :].partition_broadcast(128), wr=["G1"])
            srcs = tile_src(l, tt)
            for s in range(4):
                S.dma(SP, ("x3", s), out=xt[s][:], in_=srcs[s], wr=[("x3", s)])
            yT_all = [("yT", c) for c in range(16)]
            steps = []
            wl = w_out[l]
            for n in range(4):
                loads = [(wview(16, 512), wsrc(wl, 0, 16, n * 512, 512))]

                def comp(slot, key, n=n):
                    w3 = wview(16, 512)(slot)
                    for s in range(4):
                        bk, bkey = nb()
                        for k in range(16):
                            S.op(PE, lambda: nc.tensor.matmul(bk[:, :], lhsT=yT[:, k, s * 128:(s + 1) * 128],
                                                              rhs=w3[:, k, :], start=(k == 0), stop=(k == 15)),
                                 rd=[key, ("yT", k)], wr=[bkey])
                        S.evac(yo[s][:, n * 512:(n + 1) * 512], bk[:, :], rd=[bkey], wr=[("yo", s, n)])
                steps.append((loads, comp))
            ring.run(steps)
            junk = small["junk"]
            ss = small["ss"]
            for s in range(4):
                yk = [("yo", s, n) for n in range(4)]
                ssv = ss[:, 4 + (s % 2):5 + (s % 2)]
                sk = ("ss", 4 + (s % 2))
                S.op(ACT, lambda: nc.scalar.activation(out=junk[:], in_=yo[s][:], func=AF.Square, accum_out=ssv),
                     rd=yk, wr=["junk", sk])
                S.op(ACT, lambda: nc.scalar.activation(out=ssv, in_=ssv, func=AF.Sqrt, bias=eps_t[:], scale=1.0 / D),
                     rd=[sk, "eps"], wr=[sk])
                S.op(DVE, lambda: nc.vector.reciprocal(out=ssv, in_=ssv), rd=[sk], wr=[sk])
                S.op(DVE, lambda: nc.vector.tensor_tensor(out=yo[s][:], in0=yo[s][:], in1=G1[:], op=ALU.mult),
                     rd=yk + ["G1"], wr=[("yog", s)])
                S.op(DVE, lambda: nc.vector.scalar_tensor_tensor(out=xt[s][:], in0=yo[s][:], scalar=ssv, in1=xt[s][:],
                                                                 op0=ALU.mult, op1=ALU.add),
                     rd=[("yog", s), sk, ("x3", s)], wr=[("x3", s)])
                S.dma(SP, ("x3", s), out=d_x1[tok0 + s * 128:tok0 + (s + 1) * 128, :], in_=xt[s][:],
                      rd=[("x3", s)], wr=[("dx1", tt, s)])
                rms_front("p3", xt[s][:], ("x3", s), xn[s][:], ("xn", s), small, s % 2)
            transpose_mod(xn, [("xn", s) for s in range(4)], h2T, 2, cond)
            for c in range(16):
                S.dma(SP, ("hT", c), out=d_h2T[c * 128:(c + 1) * 128, tok0:tok0 + 512], in_=h2T[:, c, :],
                      rd=[("hT", c)], wr=[("dh2T", tt, c)])
            S.barrier(bar_t[:])

    def phase3b(l, tt):
        cond = 0 if tt < 4 else 1
        tok0 = tt * 512
        with contextlib.ExitStack() as st:
            h2T = sb(st, "h2Tb", [128, 16, 512], BF16)
            hid = sb(st, "hid", [128, 64, 512], BF16)
            G2 = sb(st, "G2", [128, D], F32)
            fo = [sb(st, f"fo{i}", [128, D], F32) for i in range(4)]
            xt = [sb(st, f"x4_{i}", [128, D], F32) for i in range(2)]
            sq = [sb(st, f"sq{i}", [128, 512], F32) for i in range(2)]
            junk = sb(st, "junk4", [128, D], BF16)
            ss = sb(st, "ss4", [128, 8], F32)
            for c in range(16):
                S.dma(SP, ("h2T", c), out=h2T[:, c, :], in_=d_h2T[c * 128:(c + 1) * 128, tok0:tok0 + 512],
                      wr=[("h2T", c)])
            S.dma(SP, "G2", out=G2[:], in_=d_G[1, cond, :].partition_broadcast(128), wr=["G2"])
            h_all = [("h2T", c) for c in range(16)]
            steps = []
            w1 = w_ff1[l]
            for cb in range(16):
                loads = [(wview(16, 512), wsrc(w1, 0, 16, cb * 512, 512))]

                def comp(slot, key, cb=cb):
                    w3 = wview(16, 512)(slot)
                    for j in range(4):
                        bk, bkey = nb()
                        for k in range(16):
                            S.op(PE, lambda: nc.tensor.matmul(bk[:, :], lhsT=w3[:, k, j * 128:(j + 1) * 128],
                                                              rhs=h2T[:, k, :], start=(k == 0), stop=(k == 15)),
                                 rd=[key, ("h2T", k)], wr=[bkey])
                        q_ = sq[j % 2]
                        S.op(ACT, lambda: nc.scalar.activation(out=q_[:], in_=bk[:, :], func=AF.Square),
                             rd=[bkey], wr=[("sq", j % 2)])
                        S.op(DVE, lambda: nc.vector.scalar_tensor_tensor(out=hid[:, cb * 4 + j, :], in0=bk[:, :],
                                                                         scalar=0.0, in1=q_[:], op0=ALU.is_gt,
                                                                         op1=ALU.mult),
                             rd=[bkey, ("sq", j % 2)], wr=[("hid", cb * 4 + j)])
                steps.append((loads, comp))
            w2 = w_ff2[l]
            for n in range(4):
                for kb in range(4):
                    loads = [(wview(16, 512), wsrc(w2, kb * 2048, 16, n * 512, 512))]

                    def comp(slot, key, n=n, kb=kb):
                        w3 = wview(16, 512)(slot)
                        for s in range(4):
                            bk, bkey = bankx((n % 2) * 4 + s)
                            for k in range(16):
                                S.op(PE, lambda: nc.tensor.matmul(
                                    bk[:, :], lhsT=hid[:, kb * 16 + k, s * 128:(s + 1) * 128], rhs=w3[:, k, :],
                                    start=(kb == 0 and k == 0), stop=(kb == 3 and k == 15)),
                                    rd=[key, ("hid", kb * 16 + k)], wr=[bkey])
                            if kb == 3:
                                S.evac(fo[s][:, n * 512:(n + 1) * 512], bk[:, :], rd=[bkey], wr=[("fo", s, n)])
                    steps.append((loads, comp))
            ring.run(steps)
            for s in range(4):
                b = s % 2
                S.dma(SP, ("x4", b), out=xt[b][:], in_=d_x1[tok0 + s * 128:tok0 + (s + 1) * 128, :],
                      wr=[("x4", b)])
                fk = [("fo", s, n) for n in range(4)]
                ssv = ss[:, b:b + 1]
                sk = ("ss", b)
                S.op(ACT, lambda: nc.scalar.activation(out=junk[:], in_=fo[s][:], func=AF.Square, accum_out=ssv),
                     rd=fk, wr=["junk", sk])
                S.op(ACT, lambda: nc.scalar.activation(out=ssv, in_=ssv, func=AF.Sqrt, bias=eps_t[:], scale=1.0 / D),
                     rd=[sk, "eps"], wr=[sk])
                S.op(DVE, lambda: nc.vector.reciprocal(out=ssv, in_=ssv), rd=[sk], wr=[sk])
                S.op(DVE, lambda: nc.vector.tensor_tensor(out=fo[s][:], in0=fo[s][:], in1=G2[:], op=ALU.mult),
                     rd=fk + ["G2"], wr=[("fog", s)])
                S.op(DVE, lambda: nc.vector.scalar_tensor_tensor(out=xt[b][:], in0=fo[s][:], scalar=ssv, in1=xt[b][:],
                                                                 op0=ALU.mult, op1=ALU.add),
                     rd=[("fog", s), sk, ("x4", b)], wr=[("x4", b)])
                if l == DEPTH - 1:
                    dst = y_s[tok0 + s * 128:tok0 + (s + 1) * 128, :] if tt < 4 else y_p[s * 128:(s + 1) * 128, :]
                else:
                    dst = d_xres[tok0 + s * 128:tok0 + (s + 1) * 128, :]
                S.dma(SP, ("x4", b), out=dst, in_=xt[b][:], rd=[("x4", b)], wr=[("dxo", tt, s)])
            S.barrier(bar_t[:])


    conv_ch = [S.chan(reserve=True) for _ in range(DEPTH)]

    def convert_weights(l):
        ch = conv_ch[l]
        wr_ = [("wconv",)]
        for r in range(0, D, 256):
            S.dma(POOL, ch, out=d_wob[r:r + 256, :], in_=w_out[l, r:r + 256, :], rd=(), wr=wr_, max_dma_last_dim=8192)
        for r in range(0, D, 128):
            S.dma(POOL, ch, out=d_w1b[r:r + 128, :], in_=w_ff1[l, r:r + 128, :], rd=(), wr=wr_, max_dma_last_dim=8192)
        for r in range(0, DFF, 512):
            S.dma(POOL, ch, out=d_w2b[r:r + 512, :], in_=w_ff2[l, r:r + 512, :], rd=(), wr=wr_, max_dma_last_dim=8192)

    def phase3_all(l):
        with contextlib.ExitStack() as st:
            buf16 = sb(st, "buf16", [128, 16, 512], BF16)
            b16f = buf16[:].rearrange("p c t -> p (c t)")
            G = sb(st, "G12", [128, D], F32)
            xt = [sb(st, f"x3_{i}", [128, D], F32) for i in range(2)]
            yo = [sb(st, f"yo{i}", [128, D], F32) for i in range(4)]
            h2T = sb(st, "h2T", [128, 16, 512], BF16)
            hid = sb(st, "hid", [128, 64, 512], BF16)
            sq = [sb(st, "sq0", [128, 512], F32)] * 2
            junk = h2T[:, 0:4, :].rearrange("p c t -> p (c t)")
            jkeys = [("hT", c) for c in range(4)]
            small = {"junk": junk, "ss": sb(st, "ss3", [128, 8], F32), "jkeys": jkeys}
            ss = small["ss"]
            wl, w1, w2 = d_wob, d_w1b, d_w2b
            wk = [("wconv",)]
            for tt in range(5):
                cond = 0 if tt < 4 else 1
                tok0 = tt * 512
                for g in range(4):
                    S.dma(SP, ("b16", g), out=buf16[:, 4 * g:4 * g + 4, :],
                          in_=d_yT[g * 512:(g + 1) * 512, tok0:tok0 + 512].rearrange("(c p) t -> p c t", p=128),
                          wr=[("b16", g)])
                S.dma(SP, "G", out=G[:], in_=d_G[0, cond, :].partition_broadcast(128), wr=["G"])
                srcs = tile_src(l, tt)
                steps = []
                for n in range(4):
                    loads = [(wview(16, 512), wsrc(wl, 0, 16, n * 512, 512), wk)]

                    def comp(slot, key, n=n):
                        w3 = wview(16, 512)(slot)
                        for s_ in range(4):
                            bk, bkey = nb()
                            for k in range(16):
                                S.op(PE, lambda: nc.tensor.matmul(bk[:, :], lhsT=buf16[:, k, s_ * 128:(s_ + 1) * 128],
                                                                  rhs=w3[:, k, :], start=(k == 0), stop=(k == 15)),
                                     rd=[key, ("b16", k // 4)], wr=[bkey])
                            S.evac(yo[s_][:, n * 512:(n + 1) * 512], bk[:, :], rd=[bkey], wr=[("yo", s_, n)])
                    steps.append((loads, comp))
                ring.run(steps)
                for s_ in range(4):
                    b = s_ % 2
                    S.dma(SP, ("x3", b), out=xt[b][:], in_=srcs[s_], wr=[("x3", b)])
                    yk = [("yo", s_, n) for n in range(4)]
                    ssv = ss[:, 4 + b:5 + b]
                    sk = ("ss", 4 + b)
                    S.op(ACT, lambda: nc.scalar.activation(out=junk[:], in_=yo[s_][:], func=AF.Square, accum_out=ssv),
                         rd=yk, wr=jkeys + [sk])
                    S.op(ACT, lambda: nc.scalar.activation(out=ssv, in_=ssv, func=AF.Sqrt, bias=eps_t[:],
                                                           scale=1.0 / D), rd=[sk, "eps"], wr=[sk])
                    S.op(DVE, lambda: nc.vector.reciprocal(out=ssv, in_=ssv), rd=[sk], wr=[sk])
                    S.op(DVE, lambda: nc.vector.tensor_tensor(out=yo[s_][:], in0=yo[s_][:], in1=G[:], op=ALU.mult),
                         rd=yk + ["G"], wr=yk)
                    S.op(DVE, lambda: nc.vector.scalar_tensor_tensor(out=xt[b][:], in0=yo[s_][:], scalar=ssv,
                                                                     in1=xt[b][:], op0=ALU.mult, op1=ALU.add),
                         rd=yk + [sk, ("x3", b)], wr=[("x3", b)])
                    S.dma(SP, ("x3", b), out=d_x1[tok0 + s_ * 128:tok0 + (s_ + 1) * 128, :], in_=xt[b][:],
                          rd=[("x3", b)], wr=[("dx1", tt, s_)])
                    rms_front("p3", xt[b][:], ("x3", b), b16f[:, s_ * D:(s_ + 1) * D], ("b16", s_), small, b)
                transpose_mod([b16f[:, s_ * D:(s_ + 1) * D] for s_ in range(4)], [("b16", s_) for s_ in range(4)],
                              h2T, 2, cond)
                steps = []
                for cb in range(16):
                    loads = [(wview(16, 512), wsrc(w1, 0, 16, cb * 512, 512), wk)]

                    def comp(slot, key, cb=cb):
                        w3 = wview(16, 512)(slot)
                        for j in range(4):
                            bk, bkey = nb()
                            for k in range(16):
                                S.op(PE, lambda: nc.tensor.matmul(bk[:, :], lhsT=w3[:, k, j * 128:(j + 1) * 128],
                                                                  rhs=h2T[:, k, :], start=(k == 0), stop=(k == 15)),
                                     rd=[key, ("hT", k)], wr=[bkey])
                            q_ = sq[j % 2]
                            S.op(ACT, lambda: nc.scalar.activation(out=q_[:], in_=bk[:, :], func=AF.Square),
                                 rd=[bkey], wr=[("sq", 0)])
                            S.op(DVE, lambda: nc.vector.scalar_tensor_tensor(out=hid[:, cb * 4 + j, :], in0=bk[:, :],
                                                                             scalar=0.0, in1=q_[:], op0=ALU.is_gt,
                                                                             op1=ALU.mult),
                                 rd=[bkey, ("sq", 0)], wr=[("hid", cb * 4 + j)])
                    steps.append((loads, comp))
                for n in range(4):
                    for kb in range(4):
                        loads = [(wview(16, 512), wsrc(w2, kb * 2048, 16, n * 512, 512), wk)]

                        def comp(slot, key, n=n, kb=kb):
                            w3 = wview(16, 512)(slot)
                            for s_ in range(4):
                                bk, bkey = bankx((n % 2) * 4 + s_)
                                for k in range(16):
                                    S.op(PE, lambda: nc.tensor.matmul(
                                        bk[:, :], lhsT=hid[:, kb * 16 + k, s_ * 128:(s_ + 1) * 128], rhs=w3[:, k, :],
                                        start=(kb == 0 and k == 0), stop=(kb == 3 and k == 15)),
                                        rd=[key, ("hid", kb * 16 + k)], wr=[bkey])
                                if kb == 3:
                                    S.evac(yo[s_][:, n * 512:(n + 1) * 512], bk[:, :], rd=[bkey], wr=[("yo", s_, n)])
                        steps.append((loads, comp))
                ring.run(steps)
                S.dma(SP, "G", out=G[:], in_=d_G[1, cond, :].partition_broadcast(128), wr=["G"])
                for s_ in range(4):
                    b = s_ % 2
                    S.dma(SP, ("x3", b), out=xt[b][:], in_=d_x1[tok0 + s_ * 128:tok0 + (s_ + 1) * 128, :],
                          rd=[("dx1", tt, s_)], wr=[("x3", b)])
                    fk = [("yo", s_, n) for n in range(4)]
                    ssv = ss[:, 6 + b:7 + b]
                    sk = ("ss", 6 + b)
                    S.op(ACT, lambda: nc.scalar.activation(out=junk[:], in_=yo[s_][:], func=AF.Square, accum_out=ssv),
                         rd=fk, wr=jkeys + [sk])
                    S.op(ACT, lambda: nc.scalar.activation(out=ssv, in_=ssv, func=AF.Sqrt, bias=eps_t[:],
                                                           scale=1.0 / D), rd=[sk, "eps"], wr=[sk])
                    S.op(DVE, lambda: nc.vector.reciprocal(out=ssv, in_=ssv), rd=[sk], wr=[sk])
                    S.op(DVE, lambda: nc.vector.tensor_tensor(out=yo[s_][:], in0=yo[s_][:], in1=G[:], op=ALU.mult),
                         rd=fk + ["G"], wr=fk)
                    S.op(DVE, lambda: nc.vector.scalar_tensor_tensor(out=xt[b][:], in0=yo[s_][:], scalar=ssv,
                                                                     in1=xt[b][:], op0=ALU.mult, op1=ALU.add),
                         rd=fk + [sk, ("x3", b)], wr=[("x3", b)])
                    if l == DEPTH - 1:
                        dst = y_s[tok0 + s_ * 128:tok0 + (s_ + 1) * 128, :] if tt < 4 \
                            else y_p[s_ * 128:(s_ + 1) * 128, :]
                    else:
                        dst = d_xres[tok0 + s_ * 128:tok0 + (s_ + 1) * 128, :]
                    S.dma(SP, ("x3", b), out=dst, in_=xt[b][:], rd=[("x3", b)], wr=[("dxo", tt, s_)])
            S.barrier(bar_t[:])

    def phase_hgrn(l, tok0, N, sample, sq):
        nch = N // 128
        ncg = min(4, nch)
        with contextlib.ExitStack() as st:
            maskr = sb(st, "maskr", [128, 2, 2, 128], F32)
            gainb = sb(st, "gainb", [128, 512], F32)
            qT = sb(st, "hqT", [128, N], BF16)
            v = sb(st, "hv", [128, nch, 128], BF16)
            gate = sb(st, "hgate", [128, nch, 128], BF16)
            lf = [sb(st, f"lf{d}", [128, N], F32) for d in range(2)]
            kk = [sb(st, f"kk{d}", [128, N], BF16) for d in range(2)]
            Pp = [sb(st, f"Pp{d}", [128, N], F32) for d in range(2)]
            etmp = sb(st, "etmp", [128, N], F32)
            Qt = [sb(st, f"Qt{d}", [128, N], BF16) for d in range(2)]
            Qtc = [sb(st, f"Qtc{d}", [128, N], BF16) for d in range(2)]
            Kneg = [sb(st, f"Kneg{d}", [128, N], BF16) for d in range(2)]
            Kpos = [sb(st, f"Kpos{d}", [128, N], BF16) for d in range(2)]
            Qh = [sb(st, f"Qh{d}", [128, N], BF16) for d in range(2)]
            KhT = [sb(st, f"KhT{d}", [128, N], BF16) for d in range(2)]
            Khtm = [sb(st, f"Khtm{d}", [128, nch, 128], BF16) for d in range(2)]
            Sin = [sb(st, f"Sin{d}", [128, nch, 128], BF16) for d in range(2)]
            Sst = [sb(st, f"Sst{d}", [128, 128], F32) for d in range(2)]
            sc = sb(st, "hsc", [128, 2, 4, nch], F32)
            sc64 = sb(st, "hsc64", [128, 2, 2 * nch], F32)
            Qm = [sb(st, f"Qm{d}", [128, N], BF16) for d in range(2)]
            Km = [sb(st, f"Km{d}", [128, N], BF16) for d in range(2)]
            scT = [sb(st, f"scT{i}", [128, 2, 2, 128], BF16) for i in range(2)]
            for i in range(2):
                S.op(DVE, lambda: nc.vector.memset(scT[i][:], 0.0), wr=[("scT", i)])
            ssq = sb(st, "hssq", [128, 4], F32)
            hjunk = sb(st, "hjunk", [128, 128], BF16)
            ytm = sb(st, "ytm", [128, 4, 128], F32)
            ytb = sb(st, "ytb", [128, 4, 128], BF16)
            yst = sb(st, "hyst", [128, 512], BF16)
            for d in range(2):
                for r in range(2):
                    S.dma(SP, ("maskr", d, r), out=maskr[:, d, r, :], in_=c_mask[d, :, :], wr=[("maskr", d, r)])
            mask_all = [("maskr", d, r) for d in range(2) for r in range(2)]
            S.dma(SP, "gainb", out=gainb[:], in_=hg_gain[l, :].partition_broadcast(128), wr=["gainb"])
            for h in range(4):
                hs = slice(h * 128, (h + 1) * 128)
                S.dma(SP, "hqT", out=qT[:], in_=d_hqT[hs, tok0:tok0 + N], wr=["hqT"])
                S.dma(SP, "hv", out=v[:], in_=d_hv[tok0:tok0 + N, hs].rearrange("(c p) d -> p c d", p=128), wr=["hv"])
                S.dma(SP, "hgate", out=gate[:], in_=d_gate[tok0:tok0 + N, hs].rearrange("(c p) d -> p c d", p=128),
                      wr=["hgate"])
                for d in range(2):
                    S.dma(SP, ("lf", d), out=lf[d][:], in_=d_lf[d, hs, tok0:tok0 + N], wr=[("lf", d)])
                    S.dma(SP, ("kk", d), out=kk[d][:], in_=d_kk[d, hs, tok0:tok0 + N], wr=[("kk", d)])
                    if sample:
                        S.dma(SP, ("Sst", d), out=Sst[d][:], in_=st_in[l, d, h, :, :], wr=[("Sst", d)])
                    else:
                        S.op(DVE, lambda: nc.vector.memset(Sst[d][:], 0.0), wr=[("Sst", d)])
                for d in range(2):
                    Pk = ("Pp", d)
                    S.op(DVE, lambda: nc.vector.memset(etmp[:], 1.0), wr=["etmp"])
                    S.op(DVE, lambda: nc.vector.tensor_tensor_scan(out=Pp[d][:], data0=etmp[:], data1=lf[d][:],
                                                                   initial=0.0, op0=ALU.mult, op1=ALU.add),
                         rd=["etmp", ("lf", d)], wr=[Pk])
                    dtmp = lf[d]
                    dk = ("lf", d)
                    Pv = Pp[d][:].rearrange("p (c t) -> p c t", t=128)
                    r_, a_, b_, dec_ = (sc[:, d, i, :] for i in range(4))
                    sk = ("hsc", d)
                    Pv64 = Pp[d][:].rearrange("p (c t) -> p c t", t=64)
                    r64 = sc64[:, d, :]
                    if d == 0:
                        S.op(DVE, lambda: nc.vector.tensor_copy(out=r_, in_=Pv[:, :, 63]), rd=[Pk], wr=[sk])
                        S.op(DVE, lambda: nc.vector.tensor_copy(out=b_, in_=Pv[:, :, 127]), rd=[Pk], wr=[sk])
                        S.op(DVE, lambda: nc.vector.memset(a_[:, 0:1], 0.0), wr=[sk])
                        if nch > 1:
                            S.op(DVE, lambda: nc.vector.tensor_copy(out=a_[:, 1:nch], in_=Pv[:, 0:nch - 1, 127]),
                                 rd=[Pk], wr=[sk])
                    else:
                        S.op(DVE, lambda: nc.vector.tensor_scalar(out=a_, in0=Pv[:, :, 127], scalar1=-1.0, scalar2=None,
                                                                  op0=ALU.mult), rd=[Pk], wr=[sk])
                        S.op(DVE, lambda: nc.vector.tensor_tensor(out=Pp[d][:], in0=lf[d][:], in1=Pp[d][:],
                                                                  op=ALU.subtract), rd=[Pk, ("lf", d)], wr=[Pk])
                        S.op(DVE, lambda: nc.vector.tensor_copy(out=r_, in_=Pv[:, :, 64]), rd=[Pk], wr=[sk])
                        S.op(DVE, lambda: nc.vector.tensor_copy(out=b_, in_=Pv[:, :, 0]), rd=[Pk], wr=[sk])
                    S.op(DVE, lambda: nc.vector.tensor_copy(out=r64, in_=Pv64[:, :, 31 + d]), rd=[Pk], wr=[sk])
                    S.op(DVE, lambda: nc.vector.tensor_tensor(out=dec_, in0=b_, in1=a_, op=ALU.subtract),
                         rd=[sk], wr=[sk])
                    S.op(ACT, lambda: nc.scalar.activation(out=dec_, in_=dec_, func=AF.Exp), rd=[sk], wr=[sk])
                    dv = dtmp[:].rearrange("p (c t) -> p c t", t=128)

                    def bsub(scal, w=128):
                        n_ = N // w
                        S.op(DVE, lambda: nc.vector.tensor_tensor(
                            out=dtmp[:].rearrange("p (c t) -> p c t", t=w),
                            in0=Pp[d][:].rearrange("p (c t) -> p c t", t=w),
                            in1=scal.unsqueeze(2).broadcast_to([128, n_, w]), op=ALU.subtract),
                            rd=[Pk, sk], wr=[dk])

                    def expmul(mode, src, srckey, dst, dstkey):
                        if mode == "exp":
                            S.op(ACT, lambda: nc.scalar.activation(out=etmp[:], in_=dtmp[:], func=AF.Exp),
                                 rd=[dk], wr=["etmp"])
                        else:
                            rs = 1.0 if mode == "expnegmax" else -1.0
                            es = 1.0 if mode == "expm1negmin" else -1.0
                            S.op(ACT, lambda: nc.scalar.activation(out=etmp[:], in_=dtmp[:], func=AF.Relu, scale=rs),
                                 rd=[dk], wr=["etmp"])
                            S.op(ACT, lambda: nc.scalar.activation(out=etmp[:], in_=etmp[:], func=AF.Exp, scale=es),
                                 rd=["etmp"], wr=["etmp"])
                        if mode == "expm1negmin":
                            S.op(DVE, lambda: nc.vector.scalar_tensor_tensor(out=dst[:], in0=etmp[:], scalar=-1.0,
                                                                             in1=src[:], op0=ALU.add, op1=ALU.mult),
                                 rd=["etmp", srckey], wr=[dstkey])
                        else:
                            S.op(DVE, lambda: nc.vector.tensor_tensor(out=dst[:], in0=src[:], in1=etmp[:],
                                                                      op=ALU.mult), rd=["etmp", srckey], wr=[dstkey])
                    bsub(r64, 64)
                    expmul("exp", qT, "hqT", Qt[d], ("Qt", d))
                    expmul("expmin", qT, "hqT", Qtc[d], ("Qtc", d))
                    expmul("expnegmax", kk[d], ("kk", d), Kneg[d], ("Kneg", d))
                    expmul("expm1negmin", kk[d], ("kk", d), Kpos[d], ("Kpos", d))
                    bsub(a_)
                    expmul("expmin", qT, "hqT", Qh[d], ("Qh", d))
                    bsub(b_)
                    expmul("expnegmax", kk[d], ("kk", d), KhT[d], ("KhT", d))
                    bsub(r_)
                    expmul("expmin", qT, "hqT", Qm[d], ("Qm", d))
                    expmul("expnegmax", kk[d], ("kk", d), Km[d], ("Km", d))
                    for c0 in range(0, nch, 4):
                        bk, bkey = bankx((c0 // 4) % 2)
                        pb = bk[:, :].bitcast(BF16)
                        nn = min(4, nch - c0)
                        for cc in range(nn):
                            c = c0 + cc
                            S.op(PE, lambda: nc.tensor.transpose(out=pb[:, cc * 128:(cc + 1) * 128],
                                                                 in_=KhT[d][:, c * 128:(c + 1) * 128],
                                                                 identity=ident_b[:]),
                                 rd=[("KhT", d), "ident_b"], wr=[bkey])
                        S.evac(Khtm[d][:, c0:c0 + nn, :], pb[:, 0:nn * 128].rearrange("p (c k) -> p c k", k=128),
                               rd=[bkey], wr=[("Khtm", d, c0)])
                    for c in range(nch):
                        bk, bkey = bankx(4 + c // 4)
                        S.op(PE, lambda: nc.tensor.matmul(bk[:, (c % 4) * 128:(c % 4 + 1) * 128], lhsT=Khtm[d][:, c, :],
                                                          rhs=v[:, c, :], start=True, stop=True),
                             rd=[("Khtm", d, (c // 4) * 4), "hv"], wr=[bkey])
                    order = range(nch) if d == 0 else range(nch - 1, -1, -1)
                    for c in order:
                        bk, bkey = bankx(4 + c // 4)
                        S.op(ACT, lambda: nc.scalar.activation(out=Sin[d][:, c, :], in_=Sst[d][:], func=AF.Copy),
                             rd=[("Sst", d)], wr=[("Sin", d)])
                        S.op(DVE, lambda: nc.vector.scalar_tensor_tensor(
                            out=Sst[d][:], in0=Sst[d][:], scalar=dec_[:, c:c + 1],
                            in1=bk[:, (c % 4) * 128:(c % 4 + 1) * 128], op0=ALU.mult, op1=ALU.add),
                            rd=[("Sst", d), sk, bkey], wr=[("Sst", d)])
                    if not sample:
                        S.dma(SP, ("Sst", d), out=o_st[sq, l, d, h, :, :], in_=Sst[d][:], rd=[("Sst", d)],
                              wr=[("ost", d, h)])
                for c0 in range(0, nch, ncg):
                    bo, bokey = bankx(2 + (c0 // ncg) % 2)
                    for cg in range(0, ncg, 2):
                        gi = (c0 + cg) // 2
                        bs, bskey = bankx(gi % 2)
                        for d in range(2):
                            for cc in range(2):
                                c = c0 + cg + cc
                                A_ = slice(c * 128, c * 128 + 64)
                                B_ = slice(c * 128 + 64, (c + 1) * 128)
                                R0 = (d * 2 + cc) * 128
                                rdk = [("Kneg", d), ("Kpos", d), ("Qt", d), ("Qtc", d), ("Km", d), ("Qm", d)]
                                for (po, X_, co) in ((slice(0, 64), A_, R0), (slice(64, 128), B_, R0 + 64)):
                                    S.op(PE, lambda: nc.tensor.matmul(bs[po, co:co + 64], lhsT=Kneg[d][:, X_],
                                                                      rhs=Qt[d][:, X_], start=True, stop=False),
                                         rd=rdk, wr=[bskey])
                                    S.op(PE, lambda: nc.tensor.matmul(bs[po, co:co + 64], lhsT=Kpos[d][:, X_],
                                                                      rhs=Qtc[d][:, X_], start=False, stop=True),
                                         rd=rdk, wr=[bskey])
                                if d == 0:
                                    S.op(PE, lambda: nc.tensor.matmul(bs[0:64, R0 + 64:R0 + 128], lhsT=Km[d][:, A_],
                                                                      rhs=Qm[d][:, B_], start=True, stop=True),
                                         rd=rdk, wr=[bskey])
                                else:
                                    S.op(PE, lambda: nc.tensor.matmul(bs[64:128, R0:R0 + 64], lhsT=Km[d][:, B_],
                                                                      rhs=Qm[d][:, A_], start=True, stop=True),
                                         rd=rdk, wr=[bskey])
                        sT = scT[gi % 2]
                        sTk = ("scT", gi % 2)
                        bs4 = bs[:, :].rearrange("p (d c t) -> p d c t", d=2, c=2)
                        mku = maskr[:].bitcast(mybir.dt.uint32)
                        for (ps_, d_, ts_) in ((slice(0, 64), 0, slice(0, 128)), (slice(0, 64), 1, slice(0, 64)),
                                               (slice(64, 128), 0, slice(64, 128)), (slice(64, 128), 1, slice(0, 128))):
                            S.op(DVE, lambda: nc.vector.copy_predicated(out=sT[ps_, d_, :, ts_], mask=mku[ps_, d_, :, ts_],
                                                                        data=bs4[ps_, d_, :, ts_]),
                                 rd=[bskey] + mask_all, wr=[sTk])
                        for cc in range(2):
                            c = c0 + cg + cc
                            cs = slice(c * 128, (c + 1) * 128)
                            reg = bo[:, (cg + cc) * 128:(cg + cc + 1) * 128]
                            S.op(PE, lambda: nc.tensor.matmul(reg, lhsT=sT[:, 0, cc, :], rhs=v[:, c, :], start=True,
                                                              stop=False), rd=[sTk, "hv"], wr=[bokey])
                            S.op(PE, lambda: nc.tensor.matmul(reg, lhsT=sT[:, 1, cc, :], rhs=v[:, c, :], start=False,
                                                              stop=False), rd=[sTk, "hv"], wr=[bokey])
                            S.op(PE, lambda: nc.tensor.matmul(reg, lhsT=Qh[0][:, cs], rhs=Sin[0][:, c, :], start=False,
                                                              stop=False), rd=[("Qh", 0), ("Sin", 0)], wr=[bokey])
                            S.op(PE, lambda: nc.tensor.matmul(reg, lhsT=Qh[1][:, cs], rhs=Sin[1][:, c, :], start=False,
                                                              stop=True), rd=[("Qh", 1), ("Sin", 1)], wr=[bokey])
                    for cc in range(ncg):
                        S.op(ACT, lambda: nc.scalar.activation(out=hjunk[:], in_=bo[:, cc * 128:(cc + 1) * 128],
                                                               func=AF.Square, accum_out=ssq[:, cc:cc + 1]),
                             rd=[bokey], wr=["hjunk", ("hssq", cc)])
                    sqk = [("hssq", cc) for cc in range(ncg)]
                    S.op(ACT, lambda: nc.scalar.activation(out=ssq[:, 0:ncg], in_=ssq[:, 0:ncg], func=AF.Sqrt,
                                                           bias=eps_t[:], scale=1.0 / 128), rd=sqk + ["eps"], wr=sqk)
                    S.op(DVE, lambda: nc.vector.reciprocal(out=ssq[:, 0:ncg], in_=ssq[:, 0:ncg]), rd=sqk, wr=sqk)
                    for cc in range(ncg):
                        S.op(DVE, lambda: nc.vector.scalar_tensor_tensor(
                            out=ytm[:, cc, :], in0=bo[:, cc * 128:(cc + 1) * 128], scalar=ssq[:, cc:cc + 1],
                            in1=gainb[:, hs], op0=ALU.mult, op1=ALU.mult),
                            rd=[bokey, ("hssq", cc), "gainb"], wr=[("ytm", cc)])
                    ytk = [("ytm", cc) for cc in range(ncg)]
                    S.op(DVE, lambda: nc.vector.tensor_tensor(out=ytb[:, 0:ncg, :], in0=ytm[:, 0:ncg, :],
                                                              in1=gate[:, c0:c0 + ncg, :], op=ALU.mult),
                         rd=ytk + ["hgate"], wr=["ytb"])
                    bt, btkey = bankx(6 + (c0 // ncg) % 2)
                    pb = bt[:, :].bitcast(BF16)
                    for cc in range(ncg):
                        S.op(PE, lambda: nc.tensor.transpose(out=pb[:, cc * 128:(cc + 1) * 128], in_=ytb[:, cc, :],
                                                             identity=ident_b[:]), rd=["ytb", "ident_b"], wr=[btkey])
                    S.evac(yst[:, 0:ncg * 128], pb[:, 0:ncg * 128], rd=[btkey], wr=["hyst"])
                    S.dma(SP, "hyst", out=d_yT[512 + h * 128:512 + (h + 1) * 128,
                                               tok0 + c0 * 128:tok0 + (c0 + ncg) * 128],
                          in_=yst[:, 0:ncg * 128], rd=["hyst"], wr=[("dyTh", h, tok0, c0)])
            S.barrier(bar_t[:])

    plan = []
    seqs = [(0, NSQ, True, 0, c_dftL, 0, CTX + NSQ),
            (NSQ, NP, False, 0, c_dftP, CTX + NSQ, NP),
            (NSQ + NP, NP, False, 1, c_dftP, CTX + NSQ + NP, NP)]
    for l in range(DEPTH):
        plan.append(("p0", l))
        plan.append(("p1c", l))
        for tt in range(5):
            plan.append(("p1", l, tt))
        for si in range(3):
            plan.append(("fn", l, si))
            if si == 0:
                plan.append(("conv", l))
            plan.append(("hg", l, si))
            plan.append(("at", l, si))
        plan.append(("p3", l))
    for item in plan:
        k = item[0]
        if k == "p0":
            phase0(item[1])
        elif k == "p1c":
            phase1_ctx(item[1])
        elif k == "p1":
            phase1(item[1], item[2])
        elif k in ("fn", "hg", "at"):
            tok0, N, smp, sq, dft, klo, nk = seqs[item[2]]
            if k == "fn":
                phase_fn(item[1], tok0, N, dft)
            elif k == "hg":
                phase_hgrn(item[1], tok0, N, smp, sq)
            else:
                phase_attn(item[1], tok0, N, klo, nk)
        elif k == "conv":
            convert_weights(item[1])
        elif k == "p3":
            phase3_all(item[1])
        elif k == "p3a":
            phase3a(item[1], item[2])
        elif k == "p3b":
            phase3b(item[1], item[2])
        if stop_after is not None and item == stop_after:
            break

    for c in S.chans:
        if c.count > 0:
            S._need(SP, ("C", c, c.count), True)
    es.close()
    return nc


def _consts():
    f = np.float64
    ident = np.eye(128, dtype=np.float32)
    t = np.arange(NSQ)
    r = (t // 64).astype(f)
    col = (t % 64).astype(f)
    nf = 16
    inv = 10000.0 ** (-np.arange(nf, dtype=f) / nf)
    ar = r[:, None] * inv
    ac = col[:, None] * inv
    ang = np.concatenate([ar, ar, ac, ac], axis=-1)
    cosT = np.cos(ang).T.astype(np.float32)
    sinT = np.sin(ang).T.astype(np.float32)
    R = np.zeros((128, 64), np.float32)
    for a in range(2):
        for j in range(16):
            i0 = a * 32 + j
            i1 = a * 32 + 16 + j
            R[i1, i0] = -1.0
            R[i0, i1] = 1.0

    def dft(n):
        k = np.arange(n)
        a = 2 * np.pi * ((k[:, None] * k[None, :]) % n) / n
        return np.cos(a), np.sin(a)
    cL, sL = dft(NSQ)
    dftL = np.stack([cL, -sL]).astype(np.float32) / np.sqrt(NSQ).astype(np.float32)
    cP, sP = dft(NP)
    dftP = np.stack([cP, -sP]).astype(np.float32) / np.sqrt(NP).astype(np.float32)
    cd, sd = dft(128)
    dftd = np.concatenate([cd, sd], axis=1).astype(np.float32) / np.float32(np.sqrt(128))
    s_i = np.arange(128)[:, None]
    t_i = np.arange(128)[None, :]
    mask = np.stack([(s_i <= t_i), (s_i >= t_i)]).astype(np.float32)
    return dict(c_ident=ident, c_cosT=np.ascontiguousarray(cosT), c_sinT=np.ascontiguousarray(sinT), c_rmat=R,
                c_dftL=np.ascontiguousarray(dftL.astype(np.float32)), c_dftP=np.ascontiguousarray(dftP.astype(np.float32)),
                c_dftd=np.ascontiguousarray(dftd), c_mask=mask)


def _col(v):
    v = np.asarray(v)
    sh = v.shape
    v2 = v.reshape(sh[:-1] + (sh[-1] // 128, 128))
    return np.ascontiguousarray(np.moveaxis(v2, -1, 0))


def make_in_maps(inp):
    A = {k: np.ascontiguousarray(np.asarray(v, dtype=np.float32)) for k, v in inp.items()}
    shared = dict(
        w_ada=A["w_ada"], b_ada=A["b_ada"], bcol=_col(A["b_ada"]),
        gcol=_col(np.stack([A["g_pre_mix"], A["g_pre_ff"]], axis=1)),
        g_post_mix=A["g_post_mix"], g_post_ff=A["g_post_ff"], w_in=A["w_in"],
        lbcol=_col(A["hg_lb"]), hg_gain=A["hg_gain"], qnorm=A["mla_q_norm"], kvnorm=A["mla_kv_norm"],
        w_q_b=A["w_q_b"], w_kv_b=A["w_kv_b"], w_out=A["w_out"], w_ff1=A["w_ff1"], w_ff2=A["w_ff2"],
    )
    shared.update(_consts())
    maps = []
    for i in range(8):
        m = dict(shared)
        m["x_s"] = A["x_sample"][i]
        m["x_p"] = np.ascontiguousarray(A["x_prompt"][2 * i:2 * i + 2].reshape(2 * NP, D))
        m["ccol"] = _col(np.stack([A["c"][i], A["c_ctx"]], axis=0)).reshape(128, 2, 16).transpose(0, 2, 1).copy()
        m["cckv"] = A["cache_ckv"][i]
        m["ckr"] = A["cache_krope"][i]
        m["st_in"] = A["state_hgrn"][i]
        maps.append(m)
    return maps


_NC_CACHE = {}


def kernel(**inputs):
    if "nc" not in _NC_CACHE:
        _NC_CACHE["nc"] = build()
    nc = _NC_CACHE["nc"]
    maps = make_in_maps(inputs)
    res = run_bass_kernel_spmd(nc, maps, core_ids=list(range(8)))
    R = res.results
    y_p = np.concatenate([R[i]["y_p"].reshape(2, NP, D) for i in range(8)], axis=0)
    y_s = np.stack([R[i]["y_s"] for i in range(8)], axis=0)
    ockv = np.concatenate([R[i]["o_ckv"] for i in range(8)], axis=0)
    okr = np.concatenate([R[i]["o_kr"] for i in range(8)], axis=0)
    ost = np.concatenate([R[i]["o_st"] for i in range(8)], axis=0)
    return (y_p.astype(np.float32), y_s.astype(np.float32), ockv.astype(np.float32), okr.astype(np.float32),
            ost.astype(np.float32))
```

```python
import bisect
import contextlib
import numpy as np
import concourse.bass as bass
import concourse.mybir as mybir
from concourse.bass_utils import run_bass_kernel_spmd

F32 = mybir.dt.float32
BF16 = mybir.dt.bfloat16
AF = mybir.ActivationFunctionType
ALU = mybir.AluOpType
AX = mybir.AxisListType

D = 2048
NT = 2560
NSQ = 2048
NP = 256
CTX = 512
NK = CTX + NT
DEPTH = 2
INC = 4416
DFF = 8192
EPS = 1e-6
NSLOT = 3
DBG_STAGE = [99]
SLOT = 8192


class Eng:
    def __init__(self, name, q, sem, self_sync):
        self.name = name
        self.q = q
        self.sem = sem
        self.n = 0
        self.last = None
        self.sig_idx = []
        self.sig_cnt = []
        self.count = 0
        self.seen = {}
        self.self_sync = self_sync

    def value_for(self, idx):
        if self.sig_idx and self.sig_idx[-1] >= idx:
            j = bisect.bisect_left(self.sig_idx, idx)
            return self.sig_cnt[j]
        self.last.then_inc(self.sem, 1)
        self.count += 1
        self.sig_idx.append(self.n - 1)
        self.sig_cnt.append(self.count)
        return self.count


class Chan:
    def __init__(self, sem):
        self.sem = sem
        self.count = 0


class Sched:
    def __init__(self, nc, es, nchan):
        self.nc = nc
        mk = lambda n: es.enter_context(nc.semaphore(n))
        self.pe = Eng("pe", nc.tensor, mk("s_pe"), False)
        self.act = Eng("act", nc.scalar, mk("s_act"), True)
        self.dve = Eng("dve", nc.vector, mk("s_dve"), True)
        self.pool = Eng("pool", nc.gpsimd, mk("s_pool"), True)
        self.sp = Eng("sp", nc.sync, mk("s_sp"), False)
        self.chans = [Chan(mk(f"s_ch{i}")) for i in range(nchan)]
        self.chan_i = 0
        self.reserved = set()
        self.keychan = {}
        self.res = {}
        self.flip = 0

    def chan(self, reserve=False):
        while True:
            c = self.chans[self.chan_i % len(self.chans)]
            self.chan_i += 1
            if c not in self.reserved:
                break
        if reserve:
            self.reserved.add(c)
        return c

    def _need(self, w, tok, raw):
        if tok[0] == "E":
            e, idx = tok[1], tok[2]
            if e is w and not w.self_sync:
                return
            v = e.value_for(idx)
            key = e
        else:
            key, v = tok[1], tok[2]
        if w.seen.get(key, 0) >= v:
            return
        w.q.wait_ge(key.sem, v)
        w.seen[key] = v

    def _deps(self, w, rd, wr):
        for k in rd:
            r = self.res.get(k)
            if r is not None and r[0] is not None:
                self._need(w, r[0], True)
        for k in wr:
            r = self.res.get(k)
            if r is not None:
                if r[0] is not None:
                    self._need(w, r[0], False)
                for t in r[1].values():
                    self._need(w, t, False)

    def _commit(self, tok, owner, rd, wr):
        for k in rd:
            r = self.res.get(k)
            if r is None:
                r = self.res[k] = [None, {}]
            r[1][owner] = tok
        for k in wr:
            self.res[k] = [tok, {}]

    def op(self, eng, fn, rd=(), wr=()):
        self._deps(eng, rd, wr)
        inst = fn()
        eng.last = inst
        tok = ("E", eng, eng.n)
        eng.n += 1
        if eng.self_sync:
            eng.value_for(eng.n - 1)
        self._commit(tok, eng, rd, wr)
        return inst

    def dma(self, q, ck, out, in_, rd=(), wr=(), **kw):
        if isinstance(ck, Chan):
            ch = ck
        else:
            ch = self.keychan.get(ck)
            if ch is None:
                ch = self.keychan[ck] = self.chan()
                assert len(self.keychan) <= len(self.chans) - len(self.reserved), "out of DMA channels"
        self._deps(q, rd, wr)
        q.q.dma_start(out=out, in_=in_, **kw).then_inc(ch.sem, 16)
        ch.count += 16
        tok = ("C", ch, ch.count)
        self._commit(tok, ch, rd, wr)

    def barrier(self, bar_tile):
        d = self.dve
        for e in (self.pe, self.act, self.pool):
            if e.n > 0:
                self._need(d, ("E", e, e.n - 1), True)
        for c in self.chans:
            if c.count > 0:
                self._need(d, ("C", c, c.count), True)
        self.op(d, lambda: self.nc.vector.memset(bar_tile, 0.0), wr=[("bar",)])
        tok = self.res[("bar",)][0]
        for e in (self.pe, self.act, self.sp):
            self._need(e, tok, True)
        self.res = {k: v for k, v in self.res.items() if k[0] in ("ring", "wconv")}
        self.keychan = {}
        self.chan_i = 0

    def evac(self, out, in_, rd, wr):
        self.flip ^= 1
        nc = self.nc
        if self.flip:
            return self.op(self.act, lambda: nc.scalar.activation(out=out, in_=in_, func=AF.Copy), rd, wr)
        return self.op(self.dve, lambda: nc.vector.tensor_copy(out=out, in_=in_), rd, wr)


class Ring:
    def __init__(self, S, es):
        nc = S.nc
        self.S = S
        self.slots = [es.enter_context(nc.sbuf_tensor(f"ring{i}", [128, SLOT], BF16)) for i in range(NSLOT)]
        self.ch = [S.chan(reserve=True) for _ in range(NSLOT)]
        self.i = 0

    def run(self, steps):
        S = self.S
        n = len(steps)
        base = self.i
        issued = 0

        def issue(j):
            si = (base + j) % NSLOT
            key = ("ring", si)
            for ld in steps[j][0]:
                vf, src = ld[0], ld[1]
                rdk = ld[2] if len(ld) > 2 else ()
                S.dma(S.pool, self.ch[si], out=vf(self.slots[si]), in_=src, rd=rdk, wr=[key], max_dma_last_dim=8192)

        for j in range(n):
            while issued < min(n, j + NSLOT):
                issue(issued)
                issued += 1
            si = (base + j) % NSLOT
            steps[j][1](self.slots[si], ("ring", si))
        self.i = base + n


def wview(kc, cols):
    return lambda slot: slot[:, 0:kc * cols].rearrange("p (k c) -> p k c", k=kc)


def wsrc(w2d, r0, kc, c0, cols):
    return w2d[r0:r0 + 128 * kc, c0:c0 + cols].rearrange("(k p) c -> p k c", p=128)


def build(stop_after=None, debug_outs=()):
    nc = bass.Bass("TRN2", target_bir_lowering=False)
    es = contextlib.ExitStack()
    dbg = set(debug_outs)

    def din(name, shape, dt=F32):
        return nc.dram_tensor(name, list(shape), dt, kind="ExternalInput").ap()

    def dout(name, shape, dt=F32):
        return nc.dram_tensor(name, list(shape), dt, kind="ExternalOutput").ap()

    def dscr(name, shape, dt):
        kind = "ExternalOutput" if name in dbg else "Internal"
        return nc.dram_tensor(name, list(shape), dt, kind=kind).ap()

    x_s = din("x_s", [NSQ, D])
    x_p = din("x_p", [2 * NP, D])
    ccol = din("ccol", [128, 16, 2])
    cckv = din("cckv", [DEPTH, CTX, 512])
    ckr = din("ckr", [DEPTH, CTX, 64])
    st_in = din("st_in", [DEPTH, 2, 4, 128, 128])
    w_ada = din("w_ada", [DEPTH, D, 6 * D])
    b_ada = din("b_ada", [DEPTH, 6 * D])
    bcol = din("bcol", [128, DEPTH, 96])
    gcol = din("gcol", [128, DEPTH, 2, 16])
    g_post_mix = din("g_post_mix", [DEPTH, D])
    g_post_ff = din("g_post_ff", [DEPTH, D])
    w_in = din("w_in", [DEPTH, D, INC])
    lbcol = din("lbcol", [128, DEPTH, 2, 4])
    hg_gain = din("hg_gain", [DEPTH, 512])
    qnorm = din("qnorm", [DEPTH, 768])
    kvnorm = din("kvnorm", [DEPTH, 512])
    w_q_b = din("w_q_b", [DEPTH, 768, 1536])
    w_kv_b = din("w_kv_b", [DEPTH, 512, 2048])
    w_out = din("w_out", [DEPTH, D, D])
    w_ff1 = din("w_ff1", [DEPTH, D, DFF])
    w_ff2 = din("w_ff2", [DEPTH, DFF, D])
    c_ident = din("c_ident", [128, 128])
    c_cosT = din("c_cosT", [64, NSQ])
    c_sinT = din("c_sinT", [64, NSQ])
    c_rmat = din("c_rmat", [128, 64])
    c_dftL = din("c_dftL", [2, NSQ, NSQ])
    c_dftP = din("c_dftP", [2, NP, NP])
    c_dftd = din("c_dftd", [128, 256])
    c_mask = din("c_mask", [2, 128, 128])

    y_s = dout("y_s", [NSQ, D])
    y_p = dout("y_p", [2 * NP, D])
    o_ckv = dout("o_ckv", [2, DEPTH, NP, 512])
    o_kr = dout("o_kr", [2, DEPTH, NP, 64])
    o_st = dout("o_st", [2, DEPTH, 2, 4, 128, 128])

    d_uT = dscr("d_uT", [512, NT], BF16)
    d_hqT = dscr("d_hqT", [512, NT], BF16)
    d_hv = dscr("d_hv", [NT, 512], BF16)
    d_lf = dscr("d_lf", [2, 512, NT], F32)
    d_kk = dscr("d_kk", [2, 512, NT], BF16)
    d_gate = dscr("d_gate", [NT, 512], BF16)
    d_QT = dscr("d_QT", [8, 192, NT], BF16)
    d_KT = dscr("d_KT", [8, 128, NK], BF16)
    d_KR = dscr("d_KR", [64, NK], BF16)
    d_V = dscr("d_V", [NK, 1024], BF16)
    d_yT = dscr("d_yT", [D, NT], BF16)
    d_x1 = dscr("d_x1", [NT, D], F32)
    d_h2T = dscr("d_h2T", [D, NT], BF16)
    d_xres = dscr("d_xres", [NT, D], F32)
    d_wob = dscr("d_wob", [D, D], BF16)
    d_w1b = dscr("d_w1b", [D, DFF], BF16)
    d_w2b = dscr("d_w2b", [DFF, D], BF16)
    d_G = dscr("d_G", [2, 2, D], F32)

    S = Sched(nc, es, nchan=40)
    PE, ACT, DVE, POOL, SP = S.pe, S.act, S.dve, S.pool, S.sp
    ring = Ring(S, es)

    uniq = [0]

    class Reuse:
        def __init__(self, st, tag):
            self.st = st
            self.tag = tag
            self.cache = {}

    def sb(st, name, shape, dt):
        if isinstance(st, Reuse):
            k = (st.tag, name, tuple(shape), str(dt))
            if k not in st.cache:
                uniq[0] += 1
                st.cache[k] = st.st.enter_context(nc.sbuf_tensor(f"{name}_{uniq[0]}", list(shape), dt))
            return st.cache[k]
        uniq[0] += 1
        return st.enter_context(nc.sbuf_tensor(f"{name}_{uniq[0]}", list(shape), dt))

    def scope(outer):
        return contextlib.nullcontext(outer) if outer is not None else contextlib.ExitStack()

    banks = [es.enter_context(nc.psum_tensor(f"bank{i}", [128, 512], F32)) for i in range(8)]
    bank_i = [0]

    def nb():
        i = bank_i[0] % 8
        bank_i[0] += 1
        return banks[i], ("ps", i)

    ident_f = sb(es, "ident_f", [128, 128], F32)
    ident_b = sb(es, "ident_b", [128, 128], BF16)
    bar_t = sb(es, "bar_t", [128, 2], F32)
    modc = sb(es, "modc", [128, 4, 16, 2], F32)
    AB = sb(es, "AB", [128, 4, 16, 2], F32)
    gcol_t = sb(es, "gcol_t", [128, DEPTH, 2, 16], F32)
    bcol_t = sb(es, "bcol_t", [128, DEPTH, 96], F32)
    lb_t = sb(es, "lb_t", [128, DEPTH, 2, 4], F32)
    oml_t = sb(es, "oml_t", [128, DEPTH, 2, 4], F32)
    eps_t = sb(es, "eps_t", [128, 1], F32)

    ch0 = S.chan()
    S.dma(SP, "ident_f", out=ident_f[:], in_=c_ident[:, :], wr=["ident_f"])
    S.dma(SP, "gcol", out=gcol_t[:], in_=gcol[:, :, :, :], wr=["gcol"])
    S.dma(SP, "bcol", out=bcol_t[:], in_=bcol[:, :, :], wr=["bcol"])
    S.dma(SP, "lbraw", out=lb_t[:], in_=lbcol[:, :, :, :], wr=["lbraw"])
    S.op(DVE, lambda: nc.vector.tensor_copy(out=ident_b[:], in_=ident_f[:]), rd=["ident_f"], wr=["ident_b"])
    S.op(DVE, lambda: nc.vector.memset(eps_t[:], EPS), wr=["eps"])
    with contextlib.ExitStack() as st:
        t0 = sb(st, "lbtmp", [128, 8], F32)
        S.op(DVE, lambda: nc.vector.tensor_tensor(out=t0[:], in0=lb_t[:, 0].rearrange("p a b -> p (a b)"),
                                                  in1=lb_t[:, 1].rearrange("p a b -> p (a b)"), op=ALU.subtract),
             rd=["lbraw"], wr=["lbtmp"])
        S.op(ACT, lambda: nc.scalar.activation(out=t0[:], in_=t0[:], func=AF.Exp), rd=["lbtmp"], wr=["lbtmp"])
        S.op(DVE, lambda: nc.vector.tensor_scalar(out=t0[:], in0=t0[:], scalar1=1.0, scalar2=None, op0=ALU.add),
             rd=["lbtmp"], wr=["lbtmp"])
        S.op(DVE, lambda: nc.vector.reciprocal(out=lb_t[:, 1].rearrange("p a b -> p (a b)"), in_=t0[:]),
             rd=["lbtmp"], wr=["lbraw"])
        S.op(DVE, lambda: nc.vector.memset(lb_t[:, 0].rearrange("p a b -> p (a b)"), 0.0), rd=["lbraw"], wr=["lbraw"])
        S.op(DVE, lambda: nc.vector.tensor_scalar(out=oml_t[:].rearrange("p l a b -> p (l a b)"),
                                                  in0=lb_t[:].rearrange("p l a b -> p (l a b)"),
                                                  scalar1=-1.0, scalar2=1.0, op0=ALU.mult, op1=ALU.add),
             rd=["lbraw"], wr=["oml"])
        S.barrier(bar_t[:])

    def phase0(l):
        with contextlib.ExitStack() as st:
            cc = sb(st, "cc", [128, 16, 2], F32)
            s2 = sb(st, "s2", [128, 16, 2], BF16)
            srep = sb(st, "srep", [128, 16, 2, 128], BF16)
            bg = sb(st, "bg", [128, 2, D], F32)
            gp = sb(st, "gp", [128, 2, D], F32)
            gst = sb(st, "gst", [128, 2, 2, 512], F32)
            ch = S.chan()
            S.dma(SP, "cc", out=cc[:], in_=ccol[:, :, :], wr=["cc"])
            for v in range(2):
                S.dma(SP, ("bg", v), out=bg[:, v, :], in_=b_ada[l, (2 + 3 * v) * D:(3 + 3 * v) * D].partition_broadcast(128),
                      wr=[("bg", v)])
            S.dma(SP, ("gp", 0), out=gp[:, 0, :], in_=g_post_mix[l, :].partition_broadcast(128), wr=[("gp", 0)])
            S.dma(SP, ("gp", 1), out=gp[:, 1, :], in_=g_post_ff[l, :].partition_broadcast(128), wr=[("gp", 1)])
            S.op(ACT, lambda: nc.scalar.activation(out=s2[:], in_=cc[:], func=AF.Silu), rd=["cc"], wr=["s2"])
            S.op(DVE, lambda: nc.vector.tensor_copy(
                out=srep[:].rearrange("p k c m -> p (k c) m"),
                in_=s2[:].rearrange("p k c -> p (k c)").unsqueeze(2).broadcast_to([128, 32, 128])),
                rd=["s2"], wr=["srep"])
            steps = []
            wl = w_ada[l]
            vec_of = {0: 0, 1: 1, 3: 2, 4: 3}
            gch = S.chan()
            for sec in range(6):
                for j in range(4):
                    c0 = sec * D + j * 512
                    loads = [(wview(16, 512), wsrc(wl, 0, 16, c0, 512))]
                    if sec in vec_of:
                        def comp(slot, key, sec=sec, j=j):
                            w3 = wview(16, 512)(slot)
                            bk, bkey = nb()
                            for sub in range(4):
                                for k in range(16):
                                    S.op(PE, lambda: nc.tensor.matmul(
                                        bk[:, sub * 2:sub * 2 + 2], lhsT=w3[:, k, sub * 128:(sub + 1) * 128],
                                        rhs=s2[:, k, :], start=(k == 0), stop=(k == 15)),
                                        rd=[key, "s2"], wr=[bkey])
                            vi = vec_of[sec]
                            ch0_ = sec * 16 + j * 4
                            S.op(DVE, lambda: nc.vector.tensor_tensor(
                                out=modc[:, vi, j * 4:(j + 1) * 4, :],
                                in0=bk[:, 0:8].rearrange("p (s c) -> p s c", c=2),
                                in1=bcol_t[:, l, ch0_:ch0_ + 4].unsqueeze(2).broadcast_to([128, 4, 2]),
                                op=ALU.add), rd=[bkey, "bcol"], wr=[("modc", vi)])
                    else:
                        def comp(slot, key, sec=sec, j=j):
                            w3 = wview(16, 512)(slot)
                            v = 0 if sec == 2 else 1
                            for cond in range(2):
                                bk, bkey = nb()
                                for k in range(16):
                                    S.op(PE, lambda: nc.tensor.matmul(
                                        bk[:, :], lhsT=srep[:, k, cond, :], rhs=w3[:, k, :],
                                        start=(k == 0), stop=(k == 15)), rd=[key, "srep"], wr=[bkey])
                                gk = ("gst", v, cond)
                                S.op(DVE, lambda: nc.vector.tensor_tensor(
                                    out=gst[:, v, cond, :], in0=bk[:, :], in1=bg[:, v, j * 512:(j + 1) * 512],
                                    op=ALU.add), rd=[bkey, ("bg", v)], wr=[gk])
                                S.op(DVE, lambda: nc.vector.tensor_tensor(
                                    out=gst[:, v, cond, :], in0=gst[:, v, cond, :], in1=gp[:, v, j * 512:(j + 1) * 512],
                                    op=ALU.mult), rd=[gk, ("gp", v)], wr=[gk])
                                S.dma(SP, gk, out=d_G[v, cond, j * 512:(j + 1) * 512].unsqueeze(0),
                                      in_=gst[0:1, v, cond, :], rd=[gk], wr=[("dG", v, cond, j)])
                    steps.append((loads, comp))
            ring.run(steps)
            for half in range(2):
                S.op(DVE, lambda: nc.vector.tensor_scalar(
                    out=AB[:, 2 * half].rearrange("p k c -> p (k c)"),
                    in0=modc[:, 2 * half + 1].rearrange("p k c -> p (k c)"),
                    scalar1=1.0, scalar2=None, op0=ALU.add), rd=[("modc", 2 * half + 1)], wr=[("AB", 2 * half)])
                S.op(DVE, lambda: nc.vector.tensor_tensor(
                    out=AB[:, 2 * half], in0=AB[:, 2 * half],
                    in1=gcol_t[:, l, half, :].unsqueeze(2).broadcast_to([128, 16, 2]), op=ALU.mult),
                    rd=[("AB", 2 * half), "gcol"], wr=[("AB", 2 * half)])
                S.op(DVE, lambda: nc.vector.tensor_copy(out=AB[:, 2 * half + 1], in_=modc[:, 2 * half]),
                     rd=[("modc", 2 * half)], wr=[("AB", 2 * half + 1)])
            S.barrier(bar_t[:])

    def rms_front(st_name, xt, xkey, xn, xnkey, small, idx):
        junk = small["junk"]
        ss = small["ss"][:, idx:idx + 1]
        S.op(ACT, lambda: nc.scalar.activation(out=junk[:], in_=xt, func=AF.Square, accum_out=ss),
             rd=[xkey], wr=list(small.get("jkeys", ["junk"])) + [("ss", idx)])
        S.op(ACT, lambda: nc.scalar.activation(out=ss, in_=ss, func=AF.Sqrt, bias=eps_t[:], scale=1.0 / D),
             rd=[("ss", idx), "eps"], wr=[("ss", idx)])
        S.op(DVE, lambda: nc.vector.reciprocal(out=ss, in_=ss), rd=[("ss", idx)], wr=[("ss", idx)])
        S.op(DVE, lambda: nc.vector.tensor_scalar(out=xn, in0=xt, scalar1=ss, scalar2=None, op0=ALU.mult),
             rd=[xkey, ("ss", idx)], wr=[xnkey])

    def transpose_mod(xn_tiles, xn_keys, hT, Aidx, cond):
        for g in range(8):
            bk, bkey = nb()
            pb = bk[:, :].bitcast(BF16).rearrange("p (c t) -> p c t", c=2)
            for cc in range(2):
                c = g * 2 + cc
                for sub in range(4):
                    S.op(PE, lambda: nc.tensor.transpose(
                        out=pb[:, cc, sub * 128:(sub + 1) * 128], in_=xn_tiles[sub][:, c * 128:(c + 1) * 128],
                        identity=ident_b[:]), rd=[xn_keys[sub], "ident_b"], wr=[bkey])
            for cc in range(2):
                c = g * 2 + cc
                S.op(ACT, lambda: nc.scalar.activation(
                    out=hT[:, c, :], in_=pb[:, cc, :], func=AF.Identity,
                    scale=AB[:, Aidx, c, cond:cond + 1], bias=AB[:, Aidx + 1, c, cond:cond + 1]),
                    rd=[bkey, ("AB", Aidx), ("AB", Aidx + 1)], wr=[("hT", c)])

    def tile_src(l, tt):
        if l == 0:
            if tt < 4:
                return [x_s[tt * 512 + s * 128: tt * 512 + (s + 1) * 128, :] for s in range(4)]
            return [x_p[s * 128:(s + 1) * 128, :] for s in range(4)]
        return [d_xres[tt * 512 + s * 128: tt * 512 + (s + 1) * 128, :] for s in range(4)]

    def kv_backend(st, l, ckv_tm, ckv_keys, kr_tm, kr_keys, key0, pos0):
        ckvT = sb(st, "ckvT", [128, 4, 512], BF16)
        krT = sb(st, "krT", [64, 512], F32)
        krb = sb(st, "krb", [128, 512], BF16)
        kst = sb(st, "kst", [128, 2, 512], BF16)
        vst = sb(st, "vst", [128, 4, 1024], BF16)
        chk = S.chan()
        chv = S.chan()
        for s in range(4):
            bk, bkey = nb()
            for c in range(4):
                S.op(PE, lambda: nc.tensor.transpose(out=bk[:, c * 128:(c + 1) * 128],
                                                     in_=ckv_tm[s][:, c * 128:(c + 1) * 128], identity=ident_f[:]),
                     rd=[ckv_keys[s], "ident_f"], wr=[bkey])
            S.evac(ckvT[:, :, s * 128:(s + 1) * 128], bk[:, :].rearrange("p (c t) -> p c t", c=4),
                   rd=[bkey], wr=[("ckvT", s)])
            bk2, bkey2 = nb()
            S.op(PE, lambda: nc.tensor.transpose(out=bk2[0:64, 0:128], in_=kr_tm[s], identity=ident_f[:]),
                 rd=[kr_keys[s], "ident_f"], wr=[bkey2])
            S.evac(krT[:, s * 128:(s + 1) * 128], bk2[0:64, 0:128], rd=[bkey2], wr=[("krT", s)])
        ckv_all = [("ckvT", s) for s in range(4)]
        krT_all = [("krT", s) for s in range(4)]
        S.op(DVE, lambda: nc.vector.memset(krb[64:128, :], 0.0), wr=["krbpad"])
        S.op(DVE, lambda: nc.vector.tensor_copy(out=krb[0:64, :], in_=krT[:]), rd=krT_all, wr=["krb"])
        if pos0 is not None:
            rope_apply(st, krT[:], krT_all, krb[:, :], ["krb", "krbpad"], pos0, krb[0:64, :], "krb2")
            S.dma(SP, "krb", out=d_KR[:, key0:key0 + 512], in_=krb[0:64, :], rd=["krb2"], wr=[("dKR", key0)])
        else:
            S.dma(SP, "krb", out=d_KR[:, key0:key0 + 512], in_=krb[0:64, :], rd=["krb"], wr=[("dKR", key0)])
        steps = []
        wl = w_kv_b[l]
        for hp in range(4):
            loads = [(wview(4, 512), wsrc(wl, 0, 4, hp * 512, 512))]

            def comp(slot, key, hp=hp):
                w3 = wview(4, 512)(slot)
                for hh in range(2):
                    bk, bkey = nb()
                    for k in range(4):
                        S.op(PE, lambda: nc.tensor.matmul(bk[:, :], lhsT=w3[:, k, hh * 256:hh * 256 + 128],
                                                          rhs=ckvT[:, k, :], start=(k == 0), stop=(k == 3)),
                             rd=[key] + ckv_all, wr=[bkey])
                    S.evac(kst[:, hh, :], bk[:, :], rd=[bkey], wr=[("kst", hh)])
                    S.dma(SP, ("kst", hh), out=d_KT[hp * 2 + hh, :, key0:key0 + 512], in_=kst[:, hh, :],
                          rd=[("kst", hh)], wr=[("dKT", hp * 2 + hh, key0)])
                wv = w3[:, :, :].rearrange("p k (h c) -> p k h c", h=2)
                for s in range(4):
                    bk, bkey = nb()
                    for k in range(4):
                        S.op(PE, lambda: nc.tensor.matmul(bk[:, 0:256].rearrange("p (h c) -> p h c", h=2),
                                                          lhsT=ckvT[:, k, s * 128:(s + 1) * 128],
                                                          rhs=wv[:, k, :, 128:256], start=(k == 0), stop=(k == 3)),
                             rd=[key, ("ckvT", s)], wr=[bkey])
                    S.evac(vst[:, s, hp * 256:(hp + 1) * 256], bk[:, 0:256], rd=[bkey], wr=[("vst", s)])
            steps.append((loads, comp))
        ring.run(steps)
        for s in range(4):
            S.dma(SP, ("vst", s), out=d_V[key0 + s * 128:key0 + (s + 1) * 128, :], in_=vst[:, s, :],
                  rd=[("vst", s)], wr=[("dV", key0, s)])

    rope_tabs = {}

    def rope_apply(st, x_f32, xkeys, xb, xbkey, pos0, out_b, outkey):
        cosT, sinT, rmb = rope_tabs["cos"], rope_tabs["sin"], rope_tabs["rm"]
        cnt = rope_tabs["n"] = rope_tabs.get("n", 0) + 1
        t1 = rope_tabs["t1"][cnt % 2]
        t1k = ("ropet1", cnt % 2)
        S.op(DVE, lambda: nc.vector.tensor_tensor(out=t1[:], in0=x_f32, in1=cosT[:, pos0:pos0 + 512], op=ALU.mult),
             rd=list(xkeys) + ["ropetab"], wr=[t1k])
        if DBG_STAGE[0] == 96 and DBG_STAGE[1] <= 1:
            S.op(DVE, lambda: nc.vector.tensor_copy(out=out_b, in_=t1[:]), rd=[t1k], wr=[outkey])
            return
        bk, bkey = nb()
        S.op(PE, lambda: nc.tensor.matmul(bk[0:64, :], lhsT=rmb[:], rhs=xb, start=True, stop=True),
             rd=list(xbkey) + ["ropetab"], wr=[bkey])
        if DBG_STAGE[0] == 96 and DBG_STAGE[1] <= 2:
            S.op(DVE, lambda: nc.vector.tensor_copy(out=out_b, in_=bk[0:64, :]), rd=[t1k, bkey], wr=[outkey])
            return
        t2 = rope_tabs["t2"][cnt % 2]
        t2k = ("ropet2", cnt % 2)
        S.op(DVE, lambda: nc.vector.tensor_tensor(out=t2[:], in0=bk[0:64, :], in1=sinT[:, pos0:pos0 + 512],
                                                  op=ALU.mult), rd=[bkey, "ropetab"], wr=[t2k])
        S.op(DVE, lambda: nc.vector.tensor_tensor(out=out_b, in0=t1[:], in1=t2[:], op=ALU.add),
             rd=[t1k, t2k] + list(xbkey), wr=[outkey])

    def load_rope_tabs(st):
        cosT = sb(st, "cosT", [64, NSQ], F32)
        sinT = sb(st, "sinT", [64, NSQ], F32)
        rmf = sb(st, "rmf", [128, 64], F32)
        rmb = sb(st, "rmb", [128, 64], BF16)
        ch = S.chan()
        S.dma(SP, "ropetab_c", out=cosT[:], in_=c_cosT[:, :], wr=["ropetab_c"])
        S.dma(SP, "ropetab_s", out=sinT[:], in_=c_sinT[:, :], wr=["ropetab_s"])
        S.dma(SP, "rmf", out=rmf[:], in_=c_rmat[:, :], wr=["rmf"])
        S.op(DVE, lambda: nc.vector.tensor_copy(out=rmb[:], in_=rmf[:]), rd=["rmf", "ropetab_c", "ropetab_s"],
             wr=["ropetab"])
        rope_tabs.update(cos=cosT, sin=sinT, rm=rmb,
                         t1=[sb(st, f"ropet1_{i}", [64, 512], F32) for i in range(2)],
                         t2=[sb(st, f"ropet2_{i}", [64, 512], F32) for i in range(2)])

    def phase1_ctx(l):
        with contextlib.ExitStack() as st:
            ck = [sb(st, f"cck{s}", [128, 512], F32) for s in range(4)]
            kr = [sb(st, f"ckr{s}", [128, 64], F32) for s in range(4)]
            ch = S.chan()
            for s in range(4):
                S.dma(SP, ("cck", s), out=ck[s][:], in_=cckv[l, s * 128:(s + 1) * 128, :], wr=[("cck", s)])
                S.dma(SP, ("ckr", s), out=kr[s][:], in_=ckr[l, s * 128:(s + 1) * 128, :], wr=[("ckr", s)])
            kv_backend(st, l, [t[:] for t in ck], [("cck", s) for s in range(4)],
                       [t[:] for t in kr], [("ckr", s) for s in range(4)], 0, None)
            S.barrier(bar_t[:])

    def phase1(l, tt):
        cond = 0 if tt < 4 else 1
        tok0 = tt * 512
        is_s = tt < 4
        with contextlib.ExitStack() as st:
            hT = sb(st, "hT", [128, 16, 512], BF16)
            with contextlib.ExitStack() as st2:
                xt = [sb(st2, f"xt{i}", [128, D], F32) for i in range(2)]
                xn = [sb(st2, f"xn{i}", [128, D], BF16) for i in range(4)]
                small = {"junk": sb(st2, "junk", [128, D], BF16), "ss": sb(st2, "ss", [128, 8], F32)}
                chx = [S.chan() for _ in range(2)]
                srcs = tile_src(l, tt)
                for s in range(4):
                    S.dma(SP, ("xt", s % 2), out=xt[s % 2][:], in_=srcs[s], wr=[("xt", s % 2)])
                    rms_front("p1", xt[s % 2][:], ("xt", s % 2), xn[s][:], ("xn", s), small, s)
                transpose_mod([t for t in xn], [("xn", s) for s in range(4)], hT, 0, cond)
                S.barrier(bar_t[:])
            small = {"junk": sb(st, "junk", [128, 768], BF16), "ss": sb(st, "ss", [128, 8], F32)}
            if DBG_STAGE[0] <= 1:
                return
            fst = sb(st, "fst", [128, 4, 512], BF16)
            lst = sb(st, "lst", [128, 4, 512], F32)
            sg = [sb(st, f"sg{i}", [128, 512], F32) for i in range(2)]
            tst = sb(st, "tst", [128, 4, 512], BF16)
            mla = [sb(st, f"mla{i}", [128, 1344], F32) for i in range(4)]
            qn = sb(st, "qn", [128, 768], BF16)
            ckvn = [sb(st, f"ckvn{i}", [128, 512], F32) for i in range(4)]
            qnT = sb(st, "qnT", [128, 6, 512], BF16)
            qnb = sb(st, "qnb", [128, 768], F32)
            kvb = sb(st, "kvb", [128, 512], F32)
            qst = sb(st, "qst", [128, 2, 512], BF16)
            qrb = sb(st, "qrb", [128, 2, 512], BF16)
            S.op(DVE, lambda: nc.vector.memset(qrb[64:128, :, :], 0.0), wr=["qrbpad"])
            qro = sb(st, "qro", [64, 2, 512], BF16)
            qxf = sb(st, "qxf", [64, 2, 512], F32)
            if is_s:
                load_rope_tabs(st)
            chs = S.chan()
            S.dma(SP, "qnb", out=qnb[:], in_=qnorm[l, :].partition_broadcast(128), wr=["qnb"])
            S.dma(SP, "kvb", out=kvb[:], in_=kvnorm[l, :].partition_broadcast(128), wr=["kvb"])
            hT_all = [("hT", c) for c in range(16)]
            wl = w_in[l]
            cho = S.chan()
            steps = []

            def fm_group(c0, post):
                loads = [(wview(16, 512), wsrc(wl, 0, 16, c0, 512))]

                def comp(slot, key):
                    w3 = wview(16, 512)(slot)
                    for j in range(4):
                        bk, bkey = nb()
                        for k in range(16):
                            S.op(PE, lambda: nc.tensor.matmul(bk[:, :], lhsT=w3[:, k, j * 128:(j + 1) * 128],
                                                              rhs=hT[:, k, :], start=(k == 0), stop=(k == 15)),
                                 rd=[key, ("hT", k)], wr=[bkey])
                        post(j, bk, bkey)
                steps.append((loads, comp))

            def tm_group(c0, ncols, post):
                loads = [(wview(16, ncols), wsrc(wl, 0, 16, c0, ncols))]

                def comp(slot, key):
                    w3 = wview(16, ncols)(slot)
                    for s in range(4):
                        bk, bkey = nb()
                        for k in range(16):
                            S.op(PE, lambda: nc.tensor.matmul(bk[:, 0:ncols], lhsT=hT[:, k, s * 128:(s + 1) * 128],
                                                              rhs=w3[:, k, :], start=(k == 0), stop=(k == 15)),
                                 rd=[key, ("hT", k)], wr=[bkey])
                        post(s, bk, bkey)
                steps.append((loads, comp))

            def post_plain(dst):
                def post(j, bk, bkey):
                    S.evac(fst[:, j, :], bk[:, :], rd=[bkey], wr=[("fst", j)])
                    S.dma(SP, ("fst", j), out=dst[j * 128:(j + 1) * 128, tok0:tok0 + 512], in_=fst[:, j, :],
                          rd=[("fst", j)], wr=[("dfm", id(dst), j, tt)])
                return post

            def post_gate(dirn):
                def post(j, bk, bkey):
                    g = sg[j % 2]
                    gk = ("sg", j % 2)
                    S.op(ACT, lambda: nc.scalar.activation(out=g[:], in_=bk[:, :], func=AF.Sigmoid),
                         rd=[bkey], wr=[gk])
                    S.op(DVE, lambda: nc.vector.tensor_scalar(out=g[:], in0=g[:], scalar1=oml_t[:, l, dirn, j:j + 1],
                                                              scalar2=lb_t[:, l, dirn, j:j + 1], op0=ALU.mult,
                                                              op1=ALU.add), rd=[gk, "oml", "lbraw"], wr=[gk])
                    S.op(DVE, lambda: nc.vector.tensor_scalar(out=fst[:, j, :], in0=g[:], scalar1=-1.0, scalar2=1.0,
                                                              op0=ALU.mult, op1=ALU.add), rd=[gk], wr=[("fst", j)])
                    S.op(DVE, lambda: nc.vector.tensor_scalar(out=g[:], in0=g[:], scalar1=1e-30, scalar2=None,
                                                              op0=ALU.max), rd=[gk], wr=[gk])
                    S.op(ACT, lambda: nc.scalar.activation(out=lst[:, j, :], in_=g[:], func=AF.Ln),
                         rd=[gk], wr=[("lst", j)])
                    S.dma(SP, ("fst", j), out=d_kk[dirn, j * 128:(j + 1) * 128, tok0:tok0 + 512], in_=fst[:, j, :],
                          rd=[("fst", j)], wr=[("dkk", dirn, j, tt)])
                    S.dma(SP, ("lst", j), out=d_lf[dirn, j * 128:(j + 1) * 128, tok0:tok0 + 512], in_=lst[:, j, :],
                          rd=[("lst", j)], wr=[("dlf", dirn, j, tt)])
                return post

            def post_tm(dst, silu):
                def post(s, bk, bkey):
                    if silu:
                        S.op(ACT, lambda: nc.scalar.activation(out=tst[:, s, :], in_=bk[:, :], func=AF.Silu),
                             rd=[bkey], wr=[("tst", s)])
                    else:
                        S.evac(tst[:, s, :], bk[:, :], rd=[bkey], wr=[("tst", s)])
                    S.dma(SP, ("tst", s), out=dst[tok0 + s * 128:tok0 + (s + 1) * 128, :], in_=tst[:, s, :],
                          rd=[("tst", s)], wr=[("dtm", id(dst), s, tt)])
                return post

            def post_mla(off, ncols):
                def post(s, bk, bkey):
                    S.evac(mla[s][:, off:off + ncols], bk[:, 0:ncols], rd=[bkey], wr=[("mla", s, off)])
                return post

            fm_group(0, post_plain(d_uT))
            fm_group(512, post_plain(d_hqT))
            tm_group(1024, 512, post_tm(d_hv, False))
            fm_group(1536, post_gate(0))
            fm_group(2048, post_gate(1))
            tm_group(2560, 512, post_tm(d_gate, True))
            tm_group(3072, 512, post_mla(0, 512))
            tm_group(3584, 512, post_mla(512, 512))
            tm_group(4096, 320, post_mla(1024, 320))
            if DBG_STAGE[0] <= 2:
                steps = steps[:DBG_STAGE[1]]
            ring.run(steps)
            if DBG_STAGE[0] <= 2:
                S.barrier(bar_t[:])
                return

            ss = small["ss"]
            junk = small["junk"]
            for s in range(4):
                mk = [("mla", s, 0), ("mla", s, 512), ("mla", s, 1024)]
                for (i, (a, b)) in enumerate(((0, 768), (768, 1280))):
                    sidx = 4 + i
                    ssv = ss[:, sidx:sidx + 1]
                    n = b - a
                    S.op(ACT, lambda: nc.scalar.activation(out=junk[:, 0:n], in_=mla[s][:, a:b], func=AF.Square,
                                                           accum_out=ssv), rd=mk, wr=["junk", ("ss", sidx)])
                    S.op(ACT, lambda: nc.scalar.activation(out=ssv, in_=ssv, func=AF.Sqrt, bias=eps_t[:],
                                                           scale=1.0 / n), rd=[("ss", sidx), "eps"], wr=[("ss", sidx)])
                    S.op(DVE, lambda: nc.vector.reciprocal(out=ssv, in_=ssv), rd=[("ss", sidx)], wr=[("ss", sidx)])
                S.op(DVE, lambda: nc.vector.scalar_tensor_tensor(out=qn[:], in0=mla[s][:, 0:768], scalar=ss[:, 4:5],
                                                                 in1=qnb[:], op0=ALU.mult, op1=ALU.mult),
                     rd=mk + [("ss", 4), "qnb"], wr=["qn"])
                S.op(DVE, lambda: nc.vector.scalar_tensor_tensor(out=ckvn[s][:], in0=mla[s][:, 768:1280],
                                                                 scalar=ss[:, 5:6], in1=kvb[:], op0=ALU.mult,
                                                                 op1=ALU.mult),
                     rd=mk + [("ss", 5), "kvb"], wr=[("ckvn", s)])
                if not is_s:
                    sq, r0 = divmod(s * 128, NP)
                    S.dma(SP, ("ckvn", s), out=o_ckv[sq, l, r0:r0 + 128, :], in_=ckvn[s][:], rd=[("ckvn", s)],
                          wr=[("ockv", s)])
                    S.dma(SP, ("mla", s), out=o_kr[sq, l, r0:r0 + 128, :], in_=mla[s][:, 1280:1344], rd=mk,
                          wr=[("okr", s)])
                bk, bkey = nb()
                pb = bk[:, :].bitcast(BF16)
                for c in range(6):
                    S.op(PE, lambda: nc.tensor.transpose(out=pb[:, c * 128:(c + 1) * 128],
                                                         in_=qn[:, c * 128:(c + 1) * 128], identity=ident_b[:]),
                         rd=["qn", "ident_b"], wr=[bkey])
                S.evac(qnT[:, :, s * 128:(s + 1) * 128], pb[:, 0:768].rearrange("p (c t) -> p c t", c=6),
                       rd=[bkey], wr=[("qnT", s)])
            qnT_all = [("qnT", s) for s in range(4)]
            if DBG_STAGE[0] <= 3:
                S.barrier(bar_t[:])
                return

            steps = []
            wq = w_q_b[l]
            for hp in range(4):
                loads = [(wview(6, 384), wsrc(wq, 0, 6, hp * 384, 384))]

                def comp(slot, key, hp=hp):
                    w3 = wview(6, 384)(slot)
                    for hh in range(2):
                        h = hp * 2 + hh
                        bk, bkey = nb()
                        for k in range(6):
                            S.op(PE, lambda: nc.tensor.matmul(bk[:, :], lhsT=w3[:, k, hh * 192:hh * 192 + 128],
                                                              rhs=qnT[:, k, :], start=(k == 0), stop=(k == 5)),
                                 rd=[key] + qnT_all, wr=[bkey])
                        S.evac(qst[:, hh, :], bk[:, :], rd=[bkey], wr=[("qst", hh)])
                        S.dma(SP, ("qst", hh), out=d_QT[h, 0:128, tok0:tok0 + 512], in_=qst[:, hh, :], rd=[("qst", hh)],
                              wr=[("dQT", h, 0, tt)])
                        bk2, bkey2 = nb()
                        for k in range(6):
                            S.op(PE, lambda: nc.tensor.matmul(bk2[0:64, :], lhsT=w3[:, k, hh * 192 + 128:hh * 192 + 192],
                                                              rhs=qnT[:, k, :], start=(k == 0), stop=(k == 5)),
                                 rd=[key] + qnT_all, wr=[bkey2])
                        S.op(ACT, lambda: nc.scalar.activation(out=qxf[:, hh, :], in_=bk2[0:64, :], func=AF.Copy),
                             rd=[bkey2], wr=[("qxf", hh)])
                        S.op(DVE, lambda: nc.vector.tensor_copy(out=qrb[0:64, hh, :], in_=qxf[:, hh, :]),
                             rd=[("qxf", hh)], wr=[("qrb", hh)])
                        if is_s and DBG_STAGE[0] not in (98, 97):
                            rope_apply(st, qxf[:, hh, :], [("qxf", hh)], qrb[:, hh, :], [("qrb", hh), "qrbpad"], tok0, qro[:, hh, :],
                                       ("qro", hh))
                            S.dma(SP, ("qro", hh), out=d_QT[h, 128:192, tok0:tok0 + 512], in_=qro[:, hh, :],
                                  rd=[("qro", hh)], wr=[("dQT", h, 1, tt)])
                        else:
                            S.dma(SP, ("qrb", hh), out=d_QT[h, 128:192, tok0:tok0 + 512], in_=qrb[0:64, hh, :],
                                  rd=[("qrb", hh)], wr=[("dQT", h, 1, tt)])
                steps.append((loads, comp))
            ring.run(steps)

            kv_backend(st, l, [t[:] for t in ckvn], [("ckvn", s) for s in range(4)],
                       [mla[s][:, 1280:1344] for s in range(4)],
                       [("mla", s, 1024) for s in range(4)], CTX + tok0, tok0 if (is_s and DBG_STAGE[0] not in (98,)) else None)
            S.barrier(bar_t[:])


    def bankx(i):
        return banks[i], ("ps", i)

    def phase_fn(l, tok0, N, dft, outer=None):
        nch = N // 128
        ncol = min(512, N)
        nblk = N // ncol
        with scope(outer) as st:
            uT = sb(st, "uT", [128, 4, N], BF16)
            ab = sb(st, "ab", [128, nch, 4, 256], BF16)
            ddf = sb(st, "ddf", [128, 256], F32)
            ddb = sb(st, "ddb", [128, 256], BF16)
            yst = sb(st, "yst", [128, 4, 512], BF16)
            S.dma(SP, "ddf", out=ddf[:], in_=c_dftd[:, :], wr=["ddf"])
            S.op(DVE, lambda: nc.vector.tensor_copy(out=ddb[:], in_=ddf[:]), rd=["ddf"], wr=["ddb"])
            for h in range(4):
                S.dma(SP, ("uT", h), out=uT[:, h, :], in_=d_uT[h * 128:(h + 1) * 128, tok0:tok0 + N], wr=[("uT", h)])
            for m in range(nch):
                for hp in range(2):
                    bk, bkey = nb()
                    for hh in range(2):
                        h = hp * 2 + hh
                        S.op(PE, lambda: nc.tensor.matmul(bk[:, hh * 256:(hh + 1) * 256],
                                                          lhsT=uT[:, h, m * 128:(m + 1) * 128], rhs=ddb[:],
                                                          start=True, stop=True), rd=[("uT", h), "ddb"], wr=[bkey])
                    S.evac(ab[:, m, hp * 2:(hp + 1) * 2, :], bk[:, :].rearrange("p (h c) -> p h c", h=2),
                           rd=[bkey], wr=[("ab", m, hp)])
            ab_all = [("ab", m, hp) for m in range(nch) for hp in range(2)]
            steps = []
            for nbk in range(nblk):
                for part in range(2):
                    src = dft[part, :, nbk * ncol:(nbk + 1) * ncol].rearrange("(k p) c -> p k c", p=128)
                    loads = [(wview(nch, ncol), src)]

                    def comp(slot, key, nbk=nbk, part=part):
                        w3 = wview(nch, ncol)(slot)
                        for h in range(4):
                            bk, bkey = bankx((nbk % 2) * 4 + h)
                            for m in range(nch):
                                S.op(PE, lambda: nc.tensor.matmul(
                                    bk[:, 0:ncol], lhsT=ab[:, m, h, part * 128:(part + 1) * 128], rhs=w3[:, m, :],
                                    start=(part == 0 and m == 0), stop=(part == 1 and m == nch - 1)),
                                    rd=[key] + ab_all, wr=[bkey])
                            if part == 1:
                                S.evac(yst[:, h, 0:ncol], bk[:, 0:ncol], rd=[bkey], wr=[("yst", h)])
                                S.dma(SP, ("yst", h), out=d_yT[h * 128:(h + 1) * 128,
                                                              tok0 + nbk * ncol:tok0 + (nbk + 1) * ncol],
                                      in_=yst[:, h, 0:ncol], rd=[("yst", h)], wr=[("dyT", h, tok0, nbk)])
                    steps.append((loads, comp))
            ring.run(steps)
            if outer is None:
                S.barrier(bar_t[:])

    def phase_attn(l, tok0, N, key_lo, nkeys, outer=None):
        nkc = nkeys // 128
        qb = min(512, N)
        nqb = N // qb
        scale = 192.0 ** -0.5
        with scope(outer) as st:
            kr = sb(st, "kr", [128, nkeys], BF16)
            ones = sb(st, "ones", [128, 128], BF16)
            qn_ = [sb(st, f"aqn{i}", [128, N], BF16) for i in range(2)]
            qr_ = [sb(st, f"aqr{i}", [128, N], BF16) for i in range(2)]
            kn_ = [sb(st, f"akn{i}", [128, nkeys], BF16) for i in range(2)]
            v_ = [sb(st, f"av{i}", [128, nkc, 128], BF16) for i in range(2)]
            pT = [sb(st, f"pT{i}", [128, 512], BF16) for i in range(4)]
            rec = [sb(st, f"rec{i}", [128, 512], F32) for i in range(2)]
            yst = [sb(st, f"ayst{i}", [128, 512], BF16) for i in range(2)]
            S.op(DVE, lambda: nc.vector.memset(kr[64:128, :], 0.0), wr=["krpad"])
            for i in range(2):
                S.op(DVE, lambda: nc.vector.memset(qr_[i][64:128, :], 0.0), wr=[("aqrpad", i)])
            S.dma(SP, "kr", out=kr[0:64, :], in_=d_KR[:, key_lo:key_lo + nkeys], wr=["kr"])
            S.op(DVE, lambda: nc.vector.memset(ones[:], 1.0), wr=["ones"])
            pi = 0
            qcount = 0
            for h in range(8):
                b = h % 2
                S.dma(SP, ("aqn", b), out=qn_[b][:], in_=d_QT[h, 0:128, tok0:tok0 + N], wr=[("aqn", b)])
                S.dma(SP, ("aqr", b), out=qr_[b][0:64, :], in_=d_QT[h, 128:192, tok0:tok0 + N], wr=[("aqr", b)])
                S.dma(SP, ("akn", b), out=kn_[b][:], in_=d_KT[h, :, key_lo:key_lo + nkeys], wr=[("akn", b)])
                S.dma(SP, ("av", b), out=v_[b][:],
                      in_=d_V[key_lo:key_lo + nkeys, h * 128:(h + 1) * 128].rearrange("(c p) d -> p c d", p=128),
                      wr=[("av", b)])
                for qi in range(nqb):
                    par = qcount % 2
                    qcount += 1
                    bo, bokey = bankx(par * 2)
                    br, brkey = bankx(par * 2 + 1)
                    qs = slice(qi * qb, (qi + 1) * qb)
                    LOOK = 3
                    pend = []

                    def emit_scores(kc):
                        nonlocal pi
                        bs, bskey = bankx(4 + (pi % 4))
                        p_t = pT[pi % 4]
                        pkey = ("pT", pi % 4)
                        pi += 1
                        ks = slice(kc * 128, (kc + 1) * 128)
                        S.op(PE, lambda: nc.tensor.matmul(bs[:, 0:qb], lhsT=kn_[b][:, ks], rhs=qn_[b][:, qs],
                                                          start=True, stop=False),
                             rd=[("akn", b), ("aqn", b)], wr=[bskey])
                        S.op(PE, lambda: nc.tensor.matmul(bs[:, 0:qb], lhsT=kr[:, ks], rhs=qr_[b][:, qs],
                                                          start=False, stop=True),
                             rd=["kr", "krpad", ("aqr", b), ("aqrpad", b)], wr=[bskey])
                        S.op(ACT, lambda: nc.scalar.activation(out=p_t[:, 0:qb], in_=bs[:, 0:qb], func=AF.Exp,
                                                               scale=scale), rd=[bskey], wr=[pkey])
                        pend.append((kc, p_t, pkey))

                    def emit_pv():
                        kc, p_t, pkey = pend.pop(0)
                        S.op(PE, lambda: nc.tensor.matmul(bo[:, 0:qb], lhsT=v_[b][:, kc, :], rhs=p_t[:, 0:qb],
                                                          start=(kc == 0), stop=(kc == nkc - 1)),
                             rd=[("av", b), pkey], wr=[bokey])
                        S.op(PE, lambda: nc.tensor.matmul(br[:, 0:qb], lhsT=ones[:], rhs=p_t[:, 0:qb],
                                                          start=(kc == 0), stop=(kc == nkc - 1)),
                             rd=["ones", pkey], wr=[brkey])
                    for kc in range(nkc):
                        emit_scores(kc)
                        if len(pend) > LOOK:
                            emit_pv()
                    while pend:
                        emit_pv()
                    S.op(DVE, lambda: nc.vector.reciprocal(out=rec[par][:, 0:qb], in_=br[:, 0:qb]),
                         rd=[brkey], wr=[("rec", par)])
                    S.op(DVE, lambda: nc.vector.tensor_tensor(out=yst[par][:, 0:qb], in0=bo[:, 0:qb],
                                                              in1=rec[par][:, 0:qb], op=ALU.mult),
                         rd=[bokey, ("rec", par)], wr=[("ayst", par)])
                    S.dma(SP, ("ayst", par), out=d_yT[1024 + h * 128:1024 + (h + 1) * 128,
                                                      tok0 + qi * qb:tok0 + (qi + 1) * qb],
                          in_=yst[par][:, 0:qb], rd=[("ayst", par)], wr=[("dyTa", h, tok0, qi)])
            if outer is None:
                S.barrier(bar_t[:])

    def phase3a(l, tt):
        cond = 0 if tt < 4 else 1
        tok0 = tt * 512
        with contextlib.ExitStack() as st:
            yT = sb(st, "yT", [128, 16, 512], BF16)
            G1 = sb(st, "G1", [128, D], F32)
            xt = [sb(st, f"x3_{i}", [128, D], F32) for i in range(4)]
            yo = [sb(st, f"yo{i}", [128, D], F32) for i in range(4)]
            xn = [sb(st, f"xn3_{i}", [128, D], BF16) for i in range(4)]
            h2T = sb(st, "h2T", [128, 16, 512], BF16)
            small = {"junk": sb(st, "junk3", [128, D], BF16), "ss": sb(st, "ss3", [128, 8], F32)}
            for c in range(16):
                S.dma(SP, ("yT", c), out=yT[:, c, :], in_=d_yT[c * 128:(c + 1) * 128, tok0:tok0 + 512],
                      wr=[("yT", c)])
            S.dma(SP, "G1", out=G1[:], in_=d_G[0, cond, :].partition_broadcast(128), wr=["G1"])
            srcs = tile_src(l, tt)
            for s in range(4):
                S.dma(SP, ("x3", s), out=xt[s][:], in_=srcs[s], wr=[("x3", s)])
            yT_all = [("yT", c) for c in range(16)]
            steps = []
            wl = w_out[l]
            for n in range(4):
                loads = [(wview(16, 512), wsrc(wl, 0, 16, n * 512, 512))]

                def comp(slot, key, n=n):
                    w3 = wview(16, 512)(slot)
                    for s in range(4):
                        bk, bkey = nb()
                        for k in range(16):
                            S.op(PE, lambda: nc.tensor.matmul(bk[:, :], lhsT=yT[:, k, s * 128:(s + 1) * 128],
                                                              rhs=w3[:, k, :], start=(k == 0), stop=(k == 15)),
                                 rd=[key, ("yT", k)], wr=[bkey])
                        S.evac(yo[s][:, n * 512:(n + 1) * 512], bk[:, :], rd=[bkey], wr=[("yo", s, n)])
                steps.append((loads, comp))
            ring.run(steps)
            junk = small["junk"]
            ss = small["ss"]
            for s in range(4):
                yk = [("yo", s, n) for n in range(4)]
                ssv = ss[:, 4 + (s % 2):5 + (s % 2)]
                sk = ("ss", 4 + (s % 2))
                S.op(ACT, lambda: nc.scalar.activation(out=junk[:], in_=yo[s][:], func=AF.Square, accum_out=ssv),
                     rd=yk, wr=["junk", sk])
                S.op(ACT, lambda: nc.scalar.activation(out=ssv, in_=ssv, func=AF.Sqrt, bias=eps_t[:], scale=1.0 / D),
                     rd=[sk, "eps"], wr=[sk])
                S.op(DVE, lambda: nc.vector.reciprocal(out=ssv, in_=ssv), rd=[sk], wr=[sk])
                S.op(DVE, lambda: nc.vector.tensor_tensor(out=yo[s][:], in0=yo[s][:], in1=G1[:], op=ALU.mult),
                     rd=yk + ["G1"], wr=[("yog", s)])
                S.op(DVE, lambda: nc.vector.scalar_tensor_tensor(out=xt[s][:], in0=yo[s][:], scalar=ssv, in1=xt[s][:],
                                                                 op0=ALU.mult, op1=ALU.add),
                     rd=[("yog", s), sk, ("x3", s)], wr=[("x3", s)])
                S.dma(SP, ("x3", s), out=d_x1[tok0 + s * 128:tok0 + (s + 1) * 128, :], in_=xt[s][:],
                      rd=[("x3", s)], wr=[("dx1", tt, s)])
                rms_front("p3", xt[s][:], ("x3", s), xn[s][:], ("xn", s), small, s % 2)
            transpose_mod(xn, [("xn", s) for s in range(4)], h2T, 2, cond)
            for c in range(16):
                S.dma(SP, ("hT", c), out=d_h2T[c * 128:(c + 1) * 128, tok0:tok0 + 512], in_=h2T[:, c, :],
                      rd=[("hT", c)], wr=[("dh2T", tt, c)])
            S.barrier(bar_t[:])

    def phase3b(l, tt):
        cond = 0 if tt < 4 else 1
        tok0 = tt * 512
        with contextlib.ExitStack() as st:
            h2T = sb(st, "h2Tb", [128, 16, 512], BF16)
            hid = sb(st, "hid", [128, 64, 512], BF16)
            G2 = sb(st, "G2", [128, D], F32)
            fo = [sb(st, f"fo{i}", [128, D], F32) for i in range(4)]
            xt = [sb(st, f"x4_{i}", [128, D], F32) for i in range(2)]
            sq = [sb(st, f"sq{i}", [128, 512], F32) for i in range(2)]
            junk = sb(st, "junk4", [128, D], BF16)
            ss = sb(st, "ss4", [128, 8], F32)
            for c in range(16):
                S.dma(SP, ("h2T", c), out=h2T[:, c, :], in_=d_h2T[c * 128:(c + 1) * 128, tok0:tok0 + 512],
                      wr=[("h2T", c)])
            S.dma(SP, "G2", out=G2[:], in_=d_G[1, cond, :].partition_broadcast(128), wr=["G2"])
            h_all = [("h2T", c) for c in range(16)]
            steps = []
            w1 = w_ff1[l]
            for cb in range(16):
                loads = [(wview(16, 512), wsrc(w1, 0, 16, cb * 512, 512))]

                def comp(slot, key, cb=cb):
                    w3 = wview(16, 512)(slot)
                    for j in range(4):
                        bk, bkey = nb()
                        for k in range(16):
                            S.op(PE, lambda: nc.tensor.matmul(bk[:, :], lhsT=w3[:, k, j * 128:(j + 1) * 128],
                                                              rhs=h2T[:, k, :], start=(k == 0), stop=(k == 15)),
                                 rd=[key, ("h2T", k)], wr=[bkey])
                        q_ = sq[j % 2]
                        S.op(ACT, lambda: nc.scalar.activation(out=q_[:], in_=bk[:, :], func=AF.Square),
                             rd=[bkey], wr=[("sq", j % 2)])
                        S.op(DVE, lambda: nc.vector.scalar_tensor_tensor(out=hid[:, cb * 4 + j, :], in0=bk[:, :],
                                                                         scalar=0.0, in1=q_[:], op0=ALU.is_gt,
                                                                         op1=ALU.mult),
                             rd=[bkey, ("sq", j % 2)], wr=[("hid", cb * 4 + j)])
                steps.append((loads, comp))
            w2 = w_ff2[l]
            for n in range(4):
                for kb in range(4):
                    loads = [(wview(16, 512), wsrc(w2, kb * 2048, 16, n * 512, 512))]

                    def comp(slot, key, n=n, kb=kb):
                        w3 = wview(16, 512)(slot)
                        for s in range(4):
                            bk, bkey = bankx((n % 2) * 4 + s)
                            for k in range(16):
                                S.op(PE, lambda: nc.tensor.matmul(
                                    bk[:, :], lhsT=hid[:, kb * 16 + k, s * 128:(s + 1) * 128], rhs=w3[:, k, :],
                                    start=(kb == 0 and k == 0), stop=(kb == 3 and k == 15)),
                                    rd=[key, ("hid", kb * 16 + k)], wr=[bkey])
                            if kb == 3:
                                S.evac(fo[s][:, n * 512:(n + 1) * 512], bk[:, :], rd=[bkey], wr=[("fo", s, n)])
                    steps.append((loads, comp))
            ring.run(steps)
            for s in range(4):
                b = s % 2
                S.dma(SP, ("x4", b), out=xt[b][:], in_=d_x1[tok0 + s * 128:tok0 + (s + 1) * 128, :],
                      wr=[("x4", b)])
                fk = [("fo", s, n) for n in range(4)]
                ssv = ss[:, b:b + 1]
                sk = ("ss", b)
                S.op(ACT, lambda: nc.scalar.activation(out=junk[:], in_=fo[s][:], func=AF.Square, accum_out=ssv),
                     rd=fk, wr=["junk", sk])
                S.op(ACT, lambda: nc.scalar.activation(out=ssv, in_=ssv, func=AF.Sqrt, bias=eps_t[:], scale=1.0 / D),
                     rd=[sk, "eps"], wr=[sk])
                S.op(DVE, lambda: nc.vector.reciprocal(out=ssv, in_=ssv), rd=[sk], wr=[sk])
                S.op(DVE, lambda: nc.vector.tensor_tensor(out=fo[s][:], in0=fo[s][:], in1=G2[:], op=ALU.mult),
                     rd=fk + ["G2"], wr=[("fog", s)])
                S.op(DVE, lambda: nc.vector.scalar_tensor_tensor(out=xt[b][:], in0=fo[s][:], scalar=ssv, in1=xt[b][:],
                                                                 op0=ALU.mult, op1=ALU.add),
                     rd=[("fog", s), sk, ("x4", b)], wr=[("x4", b)])
                if l == DEPTH - 1:
                    dst = y_s[tok0 + s * 128:tok0 + (s + 1) * 128, :] if tt < 4 else y_p[s * 128:(s + 1) * 128, :]
                else:
                    dst = d_xres[tok0 + s * 128:tok0 + (s + 1) * 128, :]
                S.dma(SP, ("x4", b), out=dst, in_=xt[b][:], rd=[("x4", b)], wr=[("dxo", tt, s)])
            S.barrier(bar_t[:])


    conv_ch = [S.chan(reserve=True) for _ in range(DEPTH)]

    def convert_weights(l):
        ch = conv_ch[l]
        wr_ = [("wconv",)]
        for r in range(0, D, 256):
            S.dma(POOL, ch, out=d_wob[r:r + 256, :], in_=w_out[l, r:r + 256, :], rd=(), wr=wr_, max_dma_last_dim=8192)
        for r in range(0, D, 128):
            S.dma(POOL, ch, out=d_w1b[r:r + 128, :], in_=w_ff1[l, r:r + 128, :], rd=(), wr=wr_, max_dma_last_dim=8192)
        for r in range(0, DFF, 512):
            S.dma(POOL, ch, out=d_w2b[r:r + 512, :], in_=w_ff2[l, r:r + 512, :], rd=(), wr=wr_, max_dma_last_dim=8192)

    def phase3_all(l):
        with contextlib.ExitStack() as st:
            buf16 = sb(st, "buf16", [128, 16, 512], BF16)
            b16f = buf16[:].rearrange("p c t -> p (c t)")
            G = sb(st, "G12", [128, D], F32)
            xt = [sb(st, f"x3_{i}", [128, D], F32) for i in range(2)]
            yo = [sb(st, f"yo{i}", [128, D], F32) for i in range(4)]
            h2T = sb(st, "h2T", [128, 16, 512], BF16)
            hid = sb(st, "hid", [128, 64, 512], BF16)
            sq = [sb(st, "sq0", [128, 512], F32)] * 2
            junk = h2T[:, 0:4, :].rearrange("p c t -> p (c t)")
            jkeys = [("hT", c) for c in range(4)]
            small = {"junk": junk, "ss": sb(st, "ss3", [128, 8], F32), "jkeys": jkeys}
            ss = small["ss"]
            wl, w1, w2 = d_wob, d_w1b, d_w2b
            wk = [("wconv",)]
            for tt in range(5):
                cond = 0 if tt < 4 else 1
                tok0 = tt * 512
                for g in range(4):
                    S.dma(SP, ("b16", g), out=buf16[:, 4 * g:4 * g + 4, :],
                          in_=d_yT[g * 512:(g + 1) * 512, tok0:tok0 + 512].rearrange("(c p) t -> p c t", p=128),
                          wr=[("b16", g)])
                S.dma(SP, "G", out=G[:], in_=d_G[0, cond, :].partition_broadcast(128), wr=["G"])
                srcs = tile_src(l, tt)
                steps = []
                for n in range(4):
                    loads = [(wview(16, 512), wsrc(wl, 0, 16, n * 512, 512), wk)]

                    def comp(slot, key, n=n):
                        w3 = wview(16, 512)(slot)
                        for s_ in range(4):
                            bk, bkey = nb()
                            for k in range(16):
                                S.op(PE, lambda: nc.tensor.matmul(bk[:, :], lhsT=buf16[:, k, s_ * 128:(s_ + 1) * 128],
                                                                  rhs=w3[:, k, :], start=(k == 0), stop=(k == 15)),
                                     rd=[key, ("b16", k // 4)], wr=[bkey])
                            S.evac(yo[s_][:, n * 512:(n + 1) * 512], bk[:, :], rd=[bkey], wr=[("yo", s_, n)])
                    steps.append((loads, comp))
                ring.run(steps)
                for s_ in range(4):
                    b = s_ % 2
                    S.dma(SP, ("x3", b), out=xt[b][:], in_=srcs[s_], wr=[("x3", b)])
                    yk = [("yo", s_, n) for n in range(4)]
                    ssv = ss[:, 4 + b:5 + b]
                    sk = ("ss", 4 + b)
                    S.op(ACT, lambda: nc.scalar.activation(out=junk[:], in_=yo[s_][:], func=AF.Square, accum_out=ssv),
                         rd=yk, wr=jkeys + [sk])
                    S.op(ACT, lambda: nc.scalar.activation(out=ssv, in_=ssv, func=AF.Sqrt, bias=eps_t[:],
                                                           scale=1.0 / D), rd=[sk, "eps"], wr=[sk])
                    S.op(DVE, lambda: nc.vector.reciprocal(out=ssv, in_=ssv), rd=[sk], wr=[sk])
                    S.op(DVE, lambda: nc.vector.tensor_tensor(out=yo[s_][:], in0=yo[s_][:], in1=G[:], op=ALU.mult),
                         rd=yk + ["G"], wr=yk)
                    S.op(DVE, lambda: nc.vector.scalar_tensor_tensor(out=xt[b][:], in0=yo[s_][:], scalar=ssv,
                                                                     in1=xt[b][:], op0=ALU.mult, op1=ALU.add),
                         rd=yk + [sk, ("x3", b)], wr=[("x3", b)])
                    S.dma(SP, ("x3", b), out=d_x1[tok0 + s_ * 128:tok0 + (s_ + 1) * 128, :], in_=xt[b][:],
                          rd=[("x3", b)], wr=[("dx1", tt, s_)])
                    rms_front("p3", xt[b][:], ("x3", b), b16f[:, s_ * D:(s_ + 1) * D], ("b16", s_), small, b)
                transpose_mod([b16f[:, s_ * D:(s_ + 1) * D] for s_ in range(4)], [("b16", s_) for s_ in range(4)],
                              h2T, 2, cond)
                steps = []
                for cb in range(16):
                    loads = [(wview(16, 512), wsrc(w1, 0, 16, cb * 512, 512), wk)]

                    def comp(slot, key, cb=cb):
                        w3 = wview(16, 512)(slot)
                        for j in range(4):
                            bk, bkey = nb()
                            for k in range(16):
                                S.op(PE, lambda: nc.tensor.matmul(bk[:, :], lhsT=w3[:, k, j * 128:(j + 1) * 128],
                                                                  rhs=h2T[:, k, :], start=(k == 0), stop=(k == 15)),
                                     rd=[key, ("hT", k)], wr=[bkey])
                            q_ = sq[j % 2]
                            S.op(ACT, lambda: nc.scalar.activation(out=q_[:], in_=bk[:, :], func=AF.Square),
                                 rd=[bkey], wr=[("sq", 0)])
                            S.op(DVE, lambda: nc.vector.scalar_tensor_tensor(out=hid[:, cb * 4 + j, :], in0=bk[:, :],
                                                                             scalar=0.0, in1=q_[:], op0=ALU.is_gt,
                                                                             op1=ALU.mult),
                                 rd=[bkey, ("sq", 0)], wr=[("hid", cb * 4 + j)])
                    steps.append((loads, comp))
                for n in range(4):
                    for kb in range(4):
                        loads = [(wview(16, 512), wsrc(w2, kb * 2048, 16, n * 512, 512), wk)]

                        def comp(slot, key, n=n, kb=kb):
                            w3 = wview(16, 512)(slot)
                            for s_ in range(4):
                                bk, bkey = bankx((n % 2) * 4 + s_)
                                for k in range(16):
                                    S.op(PE, lambda: nc.tensor.matmul(
                                        bk[:, :], lhsT=hid[:, kb * 16 + k, s_ * 128:(s_ + 1) * 128], rhs=w3[:, k, :],
                                        start=(kb == 0 and k == 0), stop=(kb == 3 and k == 15)),
                                        rd=[key, ("hid", kb * 16 + k)], wr=[bkey])
                                if kb == 3:
                                    S.evac(yo[s_][:, n * 512:(n + 1) * 512], bk[:, :], rd=[bkey], wr=[("yo", s_, n)])
                        steps.append((loads, comp))
                ring.run(steps)
                S.dma(SP, "G", out=G[:], in_=d_G[1, cond, :].partition_broadcast(128), wr=["G"])
                for s_ in range(4):
                    b = s_ % 2
                    S.dma(SP, ("x3", b), out=xt[b][:], in_=d_x1[tok0 + s_ * 128:tok0 + (s_ + 1) * 128, :],
                          rd=[("dx1", tt, s_)], wr=[("x3", b)])
                    fk = [("yo", s_, n) for n in range(4)]
                    ssv = ss[:, 6 + b:7 + b]
                    sk = ("ss", 6 + b)
                    S.op(ACT, lambda: nc.scalar.activation(out=junk[:], in_=yo[s_][:], func=AF.Square, accum_out=ssv),
                         rd=fk, wr=jkeys + [sk])
                    S.op(ACT, lambda: nc.scalar.activation(out=ssv, in_=ssv, func=AF.Sqrt, bias=eps_t[:],
                                                           scale=1.0 / D), rd=[sk, "eps"], wr=[sk])
                    S.op(DVE, lambda: nc.vector.reciprocal(out=ssv, in_=ssv), rd=[sk], wr=[sk])
                    S.op(DVE, lambda: nc.vector.tensor_tensor(out=yo[s_][:], in0=yo[s_][:], in1=G[:], op=ALU.mult),
                         rd=fk + ["G"], wr=fk)
                    S.op(DVE, lambda: nc.vector.scalar_tensor_tensor(out=xt[b][:], in0=yo[s_][:], scalar=ssv,
                                                                     in1=xt[b][:], op0=ALU.mult, op1=ALU.add),
                         rd=fk + [sk, ("x3", b)], wr=[("x3", b)])
                    if l == DEPTH - 1:
                        dst = y_s[tok0 + s_ * 128:tok0 + (s_ + 1) * 128, :] if tt < 4 \
                            else y_p[s_ * 128:(s_ + 1) * 128, :]
                    else:
                        dst = d_xres[tok0 + s_ * 128:tok0 + (s_ + 1) * 128, :]
                    S.dma(SP, ("x3", b), out=dst, in_=xt[b][:], rd=[("x3", b)], wr=[("dxo", tt, s_)])
            S.barrier(bar_t[:])

    def phase_hgrn(l, tok0, N, sample, sq, outer=None):
        nch = N // 128
        ncg = min(4, nch)
        with scope(outer) as st:
            maskr = sb(st, "maskr", [128, 2, 2, 128], F32)
            gainb = sb(st, "gainb", [128, 512], F32)
            qT = sb(st, "hqT", [128, N], BF16)
            v = sb(st, "hv", [128, nch, 128], BF16)
            gate = sb(st, "hgate", [128, nch, 128], BF16)
            lf = [sb(st, f"lf{d}", [128, N], F32) for d in range(2)]
            kk = [sb(st, f"kk{d}", [128, N], BF16) for d in range(2)]
            Pp = [sb(st, f"Pp{d}", [128, N], F32) for d in range(2)]
            etmp = sb(st, "etmp", [128, N], F32)
            Qt = [sb(st, f"Qt{d}", [128, N], BF16) for d in range(2)]
            Qtc = [sb(st, f"Qtc{d}", [128, N], BF16) for d in range(2)]
            Kneg = [sb(st, f"Kneg{d}", [128, N], BF16) for d in range(2)]
            Kpos = [sb(st, f"Kpos{d}", [128, N], BF16) for d in range(2)]
            Qh = [sb(st, f"Qh{d}", [128, N], BF16) for d in range(2)]
            KhT = [sb(st, f"KhT{d}", [128, N], BF16) for d in range(2)]
            Khtm = [sb(st, f"Khtm{d}", [128, nch, 128], BF16) for d in range(2)]
            Sin = [sb(st, f"Sin{d}", [128, nch, 128], BF16) for d in range(2)]
            Sst = [sb(st, f"Sst{d}", [128, 128], F32) for d in range(2)]
            sc = sb(st, "hsc", [128, 2, 4, nch], F32)
            sc64 = sb(st, "hsc64", [128, 2, 2 * nch], F32)
            Qm = [sb(st, f"Qm{d}", [128, N], BF16) for d in range(2)]
            Km = [sb(st, f"Km{d}", [128, N], BF16) for d in range(2)]
            scT = [sb(st, f"scT{i}", [128, 2, 2, 128], BF16) for i in range(2)]
            for i in range(2):
                S.op(DVE, lambda: nc.vector.memset(scT[i][:], 0.0), wr=[("scT", i)])
            ssq = sb(st, "hssq", [128, 4], F32)
            hjunk = sb(st, "hjunk", [128, 128], BF16)
            ytm = sb(st, "ytm", [128, 4, 128], F32)
            ytb = sb(st, "ytb", [128, 4, 128], BF16)
            yst = sb(st, "hyst", [128, 512], BF16)
            for d in range(2):
                for r in range(2):
                    S.dma(SP, ("maskr", d, r), out=maskr[:, d, r, :], in_=c_mask[d, :, :], wr=[("maskr", d, r)])
            mask_all = [("maskr", d, r) for d in range(2) for r in range(2)]
            S.dma(SP, "gainb", out=gainb[:], in_=hg_gain[l, :].partition_broadcast(128), wr=["gainb"])
            for h in range(4):
                hs = slice(h * 128, (h + 1) * 128)
                S.dma(SP, "hqT", out=qT[:], in_=d_hqT[hs, tok0:tok0 + N], wr=["hqT"])
                S.dma(SP, "hv", out=v[:], in_=d_hv[tok0:tok0 + N, hs].rearrange("(c p) d -> p c d", p=128), wr=["hv"])
                S.dma(SP, "hgate", out=gate[:], in_=d_gate[tok0:tok0 + N, hs].rearrange("(c p) d -> p c d", p=128),
                      wr=["hgate"])
                for d in range(2):
                    S.dma(SP, ("lf", d), out=lf[d][:], in_=d_lf[d, hs, tok0:tok0 + N],
                          wr=[("lf", d), (("lf", d), 0), (("lf", d), 1)])
                    S.dma(SP, ("kk", d), out=kk[d][:], in_=d_kk[d, hs, tok0:tok0 + N], wr=[("kk", d)])
                    if sample:
                        S.dma(SP, ("Sst", d), out=Sst[d][:], in_=st_in[l, d, h, :, :], wr=[("Sst", d)])
                    else:
                        S.op(DVE, lambda: nc.vector.memset(Sst[d][:], 0.0), wr=[("Sst", d)])
                for d in range(2):
                    Pk = ("Pp", d)
                    ekeys = [("etmp", 0), ("etmp", 1)]
                    S.op(DVE, lambda: nc.vector.memset(etmp[:], 1.0), wr=ekeys)
                    S.op(DVE, lambda: nc.vector.tensor_tensor_scan(out=Pp[d][:], data0=etmp[:], data1=lf[d][:],
                                                                   initial=0.0, op0=ALU.mult, op1=ALU.add),
                         rd=ekeys + [("lf", d)], wr=[Pk])
                    dtmp = lf[d]
                    dk = ("lf", d)
                    Pv = Pp[d][:].rearrange("p (c t) -> p c t", t=128)
                    r_, a_, b_, dec_ = (sc[:, d, i, :] for i in range(4))
                    sk = ("hsc", d)
                    Pv64 = Pp[d][:].rearrange("p (c t) -> p c t", t=64)
                    r64 = sc64[:, d, :]
                    if d == 0:
                        S.op(DVE, lambda: nc.vector.tensor_copy(out=r_, in_=Pv[:, :, 63]), rd=[Pk], wr=[sk])
                        S.op(DVE, lambda: nc.vector.tensor_copy(out=b_, in_=Pv[:, :, 127]), rd=[Pk], wr=[sk])
                        S.op(DVE, lambda: nc.vector.memset(a_[:, 0:1], 0.0), wr=[sk])
                        if nch > 1:
                            S.op(DVE, lambda: nc.vector.tensor_copy(out=a_[:, 1:nch], in_=Pv[:, 0:nch - 1, 127]),
                                 rd=[Pk], wr=[sk])
                    else:
                        S.op(DVE, lambda: nc.vector.tensor_scalar(out=a_, in0=Pv[:, :, 127], scalar1=-1.0, scalar2=None,
                                                                  op0=ALU.mult), rd=[Pk], wr=[sk])
                        S.op(DVE, lambda: nc.vector.tensor_tensor(out=Pp[d][:], in0=lf[d][:], in1=Pp[d][:],
                                                                  op=ALU.subtract), rd=[Pk, ("lf", d)], wr=[Pk])
                        S.op(DVE, lambda: nc.vector.tensor_copy(out=r_, in_=Pv[:, :, 64]), rd=[Pk], wr=[sk])
                        S.op(DVE, lambda: nc.vector.tensor_copy(out=b_, in_=Pv[:, :, 0]), rd=[Pk], wr=[sk])
                    S.op(DVE, lambda: nc.vector.tensor_copy(out=r64, in_=Pv64[:, :, 31 + d]), rd=[Pk], wr=[sk])
                    S.op(DVE, lambda: nc.vector.tensor_tensor(out=dec_, in0=b_, in1=a_, op=ALU.subtract),
                         rd=[sk], wr=[sk])
                    S.op(ACT, lambda: nc.scalar.activation(out=dec_, in_=dec_, func=AF.Exp), rd=[sk], wr=[sk])
                    dv = dtmp[:].rearrange("p (c t) -> p c t", t=128)

                    NH = 2 if N >= 1024 else 1
                    HL = N // NH

                    def bsub(scal, w=128):
                        n_ = HL // w
                        for hf in range(NH):
                            sl = slice(hf * HL, (hf + 1) * HL)
                            S.op(DVE, lambda: nc.vector.tensor_tensor(
                                out=dtmp[:, sl].rearrange("p (c t) -> p c t", t=w),
                                in0=Pp[d][:, sl].rearrange("p (c t) -> p c t", t=w),
                                in1=scal[:, hf * n_:(hf + 1) * n_].unsqueeze(2).broadcast_to([128, n_, w]),
                                op=ALU.subtract), rd=[Pk, sk], wr=[(dk, hf)])

                    def expmul(mode, src, srckey, dst, dstkey):
                        for hf in range(NH):
                            sl = slice(hf * HL, (hf + 1) * HL)
                            ek = ("etmp", hf)
                            if mode == "exp":
                                S.op(ACT, lambda: nc.scalar.activation(out=etmp[:, sl], in_=dtmp[:, sl], func=AF.Exp),
                                     rd=[(dk, hf)], wr=[ek])
                            else:
                                rs = 1.0 if mode == "expnegmax" else -1.0
                                es = 1.0 if mode == "expm1negmin" else -1.0
                                S.op(ACT, lambda: nc.scalar.activation(out=etmp[:, sl], in_=dtmp[:, sl], func=AF.Relu,
                                                                       scale=rs), rd=[(dk, hf)], wr=[ek])
                                S.op(ACT, lambda: nc.scalar.activation(out=etmp[:, sl], in_=etmp[:, sl], func=AF.Exp,
                                                                       scale=es), rd=[ek], wr=[ek])
                            if mode == "expm1negmin":
                                S.op(DVE, lambda: nc.vector.scalar_tensor_tensor(out=dst[:, sl], in0=etmp[:, sl],
                                                                                 scalar=-1.0, in1=src[:, sl],
                                                                                 op0=ALU.add, op1=ALU.mult),
                                     rd=[ek, srckey], wr=[dstkey])
                            else:
                                S.op(DVE, lambda: nc.vector.tensor_tensor(out=dst[:, sl], in0=src[:, sl],
                                                                          in1=etmp[:, sl], op=ALU.mult),
                                     rd=[ek, srckey], wr=[dstkey])
                    bsub(r64, 64)
                    expmul("exp", qT, "hqT", Qt[d], ("Qt", d))
                    expmul("expmin", qT, "hqT", Qtc[d], ("Qtc", d))
                    expmul("expnegmax", kk[d], ("kk", d), Kneg[d], ("Kneg", d))
                    expmul("expm1negmin", kk[d], ("kk", d), Kpos[d], ("Kpos", d))
                    bsub(a_)
                    expmul("expmin", qT, "hqT", Qh[d], ("Qh", d))
                    bsub(b_)
                    expmul("expnegmax", kk[d], ("kk", d), KhT[d], ("KhT", d))
                    bsub(r_)
                    expmul("expmin", qT, "hqT", Qm[d], ("Qm", d))
                    expmul("expnegmax", kk[d], ("kk", d), Km[d], ("Km", d))
                    for c0 in range(0, nch, 4):
                        bk, bkey = bankx((c0 // 4) % 2)
                        pb = bk[:, :].bitcast(BF16)
                        nn = min(4, nch - c0)
                        for cc in range(nn):
                            c = c0 + cc
                            S.op(PE, lambda: nc.tensor.transpose(out=pb[:, cc * 128:(cc + 1) * 128],
                                                                 in_=KhT[d][:, c * 128:(c + 1) * 128],
                                                                 identity=ident_b[:]),
                                 rd=[("KhT", d), "ident_b"], wr=[bkey])
                        S.evac(Khtm[d][:, c0:c0 + nn, :], pb[:, 0:nn * 128].rearrange("p (c k) -> p c k", k=128),
                               rd=[bkey], wr=[("Khtm", d, c0)])
                    for c in range(nch):
                        bk, bkey = bankx(4 + c // 4)
                        S.op(PE, lambda: nc.tensor.matmul(bk[:, (c % 4) * 128:(c % 4 + 1) * 128], lhsT=Khtm[d][:, c, :],
                                                          rhs=v[:, c, :], start=True, stop=True),
                             rd=[("Khtm", d, (c // 4) * 4), "hv"], wr=[bkey])
                    order = range(nch) if d == 0 else range(nch - 1, -1, -1)
                    for c in order:
                        bk, bkey = bankx(4 + c // 4)
                        S.op(ACT, lambda: nc.scalar.activation(out=Sin[d][:, c, :], in_=Sst[d][:], func=AF.Copy),
                             rd=[("Sst", d)], wr=[("Sin", d)])
                        S.op(DVE, lambda: nc.vector.scalar_tensor_tensor(
                            out=Sst[d][:], in0=Sst[d][:], scalar=dec_[:, c:c + 1],
                            in1=bk[:, (c % 4) * 128:(c % 4 + 1) * 128], op0=ALU.mult, op1=ALU.add),
                            rd=[("Sst", d), sk, bkey], wr=[("Sst", d)])
                    if not sample:
                        S.dma(SP, ("Sst", d), out=o_st[sq, l, d, h, :, :], in_=Sst[d][:], rd=[("Sst", d)],
                              wr=[("ost", d, h)])
                for c0 in range(0, nch, ncg):
                    bo, bokey = bankx(2 + (c0 // ncg) % 2)
                    for cg in range(0, ncg, 2):
                        gi = (c0 + cg) // 2
                        bs, bskey = bankx(gi % 2)
                        for d in range(2):
                            for cc in range(2):
                                c = c0 + cg + cc
                                A_ = slice(c * 128, c * 128 + 64)
                                B_ = slice(c * 128 + 64, (c + 1) * 128)
                                R0 = (d * 2 + cc) * 128
                                rdk = [("Kneg", d), ("Kpos", d), ("Qt", d), ("Qtc", d), ("Km", d), ("Qm", d)]
                                for (po, X_, co) in ((slice(0, 64), A_, R0), (slice(64, 128), B_, R0 + 64)):
                                    S.op(PE, lambda: nc.tensor.matmul(bs[po, co:co + 64], lhsT=Kneg[d][:, X_],
                                                                      rhs=Qt[d][:, X_], start=True, stop=False),
                                         rd=rdk, wr=[bskey])
                                    S.op(PE, lambda: nc.tensor.matmul(bs[po, co:co + 64], lhsT=Kpos[d][:, X_],
                                                                      rhs=Qtc[d][:, X_], start=False, stop=True),
                                         rd=rdk, wr=[bskey])
                                if d == 0:
                                    S.op(PE, lambda: nc.tensor.matmul(bs[0:64, R0 + 64:R0 + 128], lhsT=Km[d][:, A_],
                                                                      rhs=Qm[d][:, B_], start=True, stop=True),
                                         rd=rdk, wr=[bskey])
                                else:
                                    S.op(PE, lambda: nc.tensor.matmul(bs[64:128, R0:R0 + 64], lhsT=Km[d][:, B_],
                                                                      rhs=Qm[d][:, A_], start=True, stop=True),
                                         rd=rdk, wr=[bskey])
                        sT = scT[gi % 2]
                        sTk = ("scT", gi % 2)
                        bs4 = bs[:, :].rearrange("p (d c t) -> p d c t", d=2, c=2)
                        mku = maskr[:].bitcast(mybir.dt.uint32)
                        for (ps_, d_, ts_) in ((slice(0, 64), 0, slice(0, 128)), (slice(0, 64), 1, slice(0, 64)),
                                               (slice(64, 128), 0, slice(64, 128)), (slice(64, 128), 1, slice(0, 128))):
                            S.op(DVE, lambda: nc.vector.copy_predicated(out=sT[ps_, d_, :, ts_], mask=mku[ps_, d_, :, ts_],
                                                                        data=bs4[ps_, d_, :, ts_]),
                                 rd=[bskey] + mask_all, wr=[sTk])
                        for cc in range(2):
                            c = c0 + cg + cc
                            cs = slice(c * 128, (c + 1) * 128)
                            reg = bo[:, (cg + cc) * 128:(cg + cc + 1) * 128]
                            S.op(PE, lambda: nc.tensor.matmul(reg, lhsT=sT[:, 0, cc, :], rhs=v[:, c, :], start=True,
                                                              stop=False), rd=[sTk, "hv"], wr=[bokey])
                            S.op(PE, lambda: nc.tensor.matmul(reg, lhsT=sT[:, 1, cc, :], rhs=v[:, c, :], start=False,
                                                              stop=False), rd=[sTk, "hv"], wr=[bokey])
                            S.op(PE, lambda: nc.tensor.matmul(reg, lhsT=Qh[0][:, cs], rhs=Sin[0][:, c, :], start=False,
                                                              stop=False), rd=[("Qh", 0), ("Sin", 0)], wr=[bokey])
                            S.op(PE, lambda: nc.tensor.matmul(reg, lhsT=Qh[1][:, cs], rhs=Sin[1][:, c, :], start=False,
                                                              stop=True), rd=[("Qh", 1), ("Sin", 1)], wr=[bokey])
                    for cc in range(ncg):
                        S.op(ACT, lambda: nc.scalar.activation(out=hjunk[:], in_=bo[:, cc * 128:(cc + 1) * 128],
                                                               func=AF.Square, accum_out=ssq[:, cc:cc + 1]),
                             rd=[bokey], wr=["hjunk", ("hssq", cc)])
                    sqk = [("hssq", cc) for cc in range(ncg)]
                    S.op(ACT, lambda: nc.scalar.activation(out=ssq[:, 0:ncg], in_=ssq[:, 0:ncg], func=AF.Sqrt,
                                                           bias=eps_t[:], scale=1.0 / 128), rd=sqk + ["eps"], wr=sqk)
                    S.op(DVE, lambda: nc.vector.reciprocal(out=ssq[:, 0:ncg], in_=ssq[:, 0:ncg]), rd=sqk, wr=sqk)
                    for cc in range(ncg):
                        S.op(DVE, lambda: nc.vector.scalar_tensor_tensor(
                            out=ytm[:, cc, :], in0=bo[:, cc * 128:(cc + 1) * 128], scalar=ssq[:, cc:cc + 1],
                            in1=gainb[:, hs], op0=ALU.mult, op1=ALU.mult),
                            rd=[bokey, ("hssq", cc), "gainb"], wr=[("ytm", cc)])
                    ytk = [("ytm", cc) for cc in range(ncg)]
                    S.op(DVE, lambda: nc.vector.tensor_tensor(out=ytb[:, 0:ncg, :], in0=ytm[:, 0:ncg, :],
                                                              in1=gate[:, c0:c0 + ncg, :], op=ALU.mult),
                         rd=ytk + ["hgate"], wr=["ytb"])
                    bt, btkey = bankx(6 + (c0 // ncg) % 2)
                    pb = bt[:, :].bitcast(BF16)
                    for cc in range(ncg):
                        S.op(PE, lambda: nc.tensor.transpose(out=pb[:, cc * 128:(cc + 1) * 128], in_=ytb[:, cc, :],
                                                             identity=ident_b[:]), rd=["ytb", "ident_b"], wr=[btkey])
                    S.evac(yst[:, 0:ncg * 128], pb[:, 0:ncg * 128], rd=[btkey], wr=["hyst"])
                    S.dma(SP, "hyst", out=d_yT[512 + h * 128:512 + (h + 1) * 128,
                                               tok0 + c0 * 128:tok0 + (c0 + ncg) * 128],
                          in_=yst[:, 0:ncg * 128], rd=["hyst"], wr=[("dyTh", h, tok0, c0)])
            if outer is None:
                S.barrier(bar_t[:])

    plan = []
    seqs = [(0, NSQ, True, 0, c_dftL, 0, CTX + NSQ),
            (NSQ, NP, False, 0, c_dftP, CTX + NSQ, NP),
            (NSQ + NP, NP, False, 1, c_dftP, CTX + NSQ + NP, NP)]
    for l in range(DEPTH):
        plan.append(("p0", l))
        plan.append(("p1c", l))
        for tt in range(5):
            plan.append(("p1", l, tt))
        plan.append(("fn", l, 0))
        plan.append(("conv", l))
        plan.append(("hg", l, 0))
        plan.append(("at", l, 0))
        plan.append(("pmix", l))
        plan.append(("p3", l))
    for item in plan:
        k = item[0]
        if k == "p0":
            phase0(item[1])
        elif k == "p1c":
            phase1_ctx(item[1])
        elif k == "p1":
            phase1(item[1], item[2])
        elif k in ("fn", "hg", "at"):
            tok0, N, smp, sq, dft, klo, nk = seqs[item[2]]
            if k == "fn":
                phase_fn(item[1], tok0, N, dft)
            elif k == "hg":
                phase_hgrn(item[1], tok0, N, smp, sq)
            else:
                phase_attn(item[1], tok0, N, klo, nk)
        elif k == "pmix":
            with contextlib.ExitStack() as real:
                rf, rh, ra = Reuse(real, "fn"), Reuse(real, "hg"), Reuse(real, "at")
                for si in (1, 2):
                    tok0, N, smp, sq, dft, klo, nk = seqs[si]
                    phase_fn(item[1], tok0, N, dft, outer=rf)
                for si in (1, 2):
                    tok0, N, smp, sq, dft, klo, nk = seqs[si]
                    phase_hgrn(item[1], tok0, N, smp, sq, outer=rh)
                for si in (1, 2):
                    tok0, N, smp, sq, dft, klo, nk = seqs[si]
                    phase_attn(item[1], tok0, N, klo, nk, outer=ra)
                S.barrier(bar_t[:])
        elif k == "conv":
            convert_weights(item[1])
        elif k == "p3":
            phase3_all(item[1])
        elif k == "p3a":
            phase3a(item[1], item[2])
        elif k == "p3b":
            phase3b(item[1], item[2])
        if stop_after is not None and item == stop_after:
            break

    for c in S.chans:
        if c.count > 0:
            S._need(SP, ("C", c, c.count), True)
    es.close()
    return nc


def _consts():
    f = np.float64
    ident = np.eye(128, dtype=np.float32)
    t = np.arange(NSQ)
    r = (t // 64).astype(f)
    col = (t % 64).astype(f)
    nf = 16
    inv = 10000.0 ** (-np.arange(nf, dtype=f) / nf)
    ar = r[:, None] * inv
    ac = col[:, None] * inv
    ang = np.concatenate([ar, ar, ac, ac], axis=-1)
    cosT = np.cos(ang).T.astype(np.float32)
    sinT = np.sin(ang).T.astype(np.float32)
    R = np.zeros((128, 64), np.float32)
    for a in range(2):
        for j in range(16):
            i0 = a * 32 + j
            i1 = a * 32 + 16 + j
            R[i1, i0] = -1.0
            R[i0, i1] = 1.0

    def dft(n):
        k = np.arange(n)
        a = 2 * np.pi * ((k[:, None] * k[None, :]) % n) / n
        return np.cos(a), np.sin(a)
    cL, sL = dft(NSQ)
    dftL = np.stack([cL, -sL]).astype(np.float32) / np.sqrt(NSQ).astype(np.float32)
    cP, sP = dft(NP)
    dftP = np.stack([cP, -sP]).astype(np.float32) / np.sqrt(NP).astype(np.float32)
    cd, sd = dft(128)
    dftd = np.concatenate([cd, sd], axis=1).astype(np.float32) / np.float32(np.sqrt(128))
    s_i = np.arange(128)[:, None]
    t_i = np.arange(128)[None, :]
    mask = np.stack([(s_i <= t_i), (s_i >= t_i)]).astype(np.float32)
    return dict(c_ident=ident, c_cosT=np.ascontiguousarray(cosT), c_sinT=np.ascontiguousarray(sinT), c_rmat=R,
                c_dftL=np.ascontiguousarray(dftL.astype(np.float32)), c_dftP=np.ascontiguousarray(dftP.astype(np.float32)),
                c_dftd=np.ascontiguousarray(dftd), c_mask=mask)


def _col(v):
    v = np.asarray(v)
    sh = v.shape
    v2 = v.reshape(sh[:-1] + (sh[-1] // 128, 128))
    return np.ascontiguousarray(np.moveaxis(v2, -1, 0))


def make_in_maps(inp):
    A = {k: np.ascontiguousarray(np.asarray(v, dtype=np.float32)) for k, v in inp.items()}
    shared = dict(
        w_ada=A["w_ada"], b_ada=A["b_ada"], bcol=_col(A["b_ada"]),
        gcol=_col(np.stack([A["g_pre_mix"], A["g_pre_ff"]], axis=1)),
        g_post_mix=A["g_post_mix"], g_post_ff=A["g_post_ff"], w_in=A["w_in"],
        lbcol=_col(A["hg_lb"]), hg_gain=A["hg_gain"], qnorm=A["mla_q_norm"], kvnorm=A["mla_kv_norm"],
        w_q_b=A["w_q_b"], w_kv_b=A["w_kv_b"], w_out=A["w_out"], w_ff1=A["w_ff1"], w_ff2=A["w_ff2"],
    )
    shared.update(_consts())
    maps = []
    for i in range(8):
        m = dict(shared)
        m["x_s"] = A["x_sample"][i]
        m["x_p"] = np.ascontiguousarray(A["x_prompt"][2 * i:2 * i + 2].reshape(2 * NP, D))
        m["ccol"] = _col(np.stack([A["c"][i], A["c_ctx"]], axis=0)).reshape(128, 2, 16).transpose(0, 2, 1).copy()
        m["cckv"] = A["cache_ckv"][i]
        m["ckr"] = A["cache_krope"][i]
        m["st_in"] = A["state_hgrn"][i]
        maps.append(m)
    return maps


_NC_CACHE = {}


def kernel(**inputs):
    if "nc" not in _NC_CACHE:
        _NC_CACHE["nc"] = build()
    nc = _NC_CACHE["nc"]
    maps = make_in_maps(inputs)
    res = run_bass_kernel_spmd(nc, maps, core_ids=list(range(8)))
    R = res.results
    y_p = np.concatenate([R[i]["y_p"].reshape(2, NP, D) for i in range(8)], axis=0)
    y_s = np.stack([R[i]["y_s"] for i in range(8)], axis=0)
    ockv = np.concatenate([R[i]["o_ckv"] for i in range(8)], axis=0)
    okr = np.concatenate([R[i]["o_kr"] for i in range(8)], axis=0)
    ost = np.concatenate([R[i]["o_st"] for i in range(8)], axis=0)
    return (y_p.astype(np.float32), y_s.astype(np.float32), ockv.astype(np.float32), okr.astype(np.float32),
            ost.astype(np.float32))
```

```python
import bisect
import contextlib
import numpy as np
import concourse.bass as bass
import concourse.mybir as mybir
from concourse.bass_utils import run_bass_kernel_spmd

F32 = mybir.dt.float32
BF16 = mybir.dt.bfloat16
AF = mybir.ActivationFunctionType
ALU = mybir.AluOpType
AX = mybir.AxisListType

D = 2048
NT = 2560
NSQ = 2048
NP = 256
CTX = 512
NK = CTX + NT
DEPTH = 2
INC = 4416
DFF = 8192
EPS = 1e-6
NSLOT = 3
DBG_STAGE = [99]
SLOT = 8192


class Eng:
    def __init__(self, name, q, sem, self_sync):
        self.name = name
        self.q = q
        self.sem = sem
        self.n = 0
        self.last = None
        self.sig_idx = []
        self.sig_cnt = []
        self.count = 0
        self.seen = {}
        self.self_sync = self_sync

    def value_for(self, idx):
        if self.sig_idx and self.sig_idx[-1] >= idx:
            j = bisect.bisect_left(self.sig_idx, idx)
            return self.sig_cnt[j]
        self.last.then_inc(self.sem, 1)
        self.count += 1
        self.sig_idx.append(self.n - 1)
        self.sig_cnt.append(self.count)
        return self.count


class Chan:
    def __init__(self, sem):
        self.sem = sem
        self.count = 0


class Sched:
    def __init__(self, nc, es, nchan):
        self.nc = nc
        mk = lambda n: es.enter_context(nc.semaphore(n))
        self.pe = Eng("pe", nc.tensor, mk("s_pe"), False)
        self.act = Eng("act", nc.scalar, mk("s_act"), True)
        self.dve = Eng("dve", nc.vector, mk("s_dve"), True)
        self.pool = Eng("pool", nc.gpsimd, mk("s_pool"), True)
        self.sp = Eng("sp", nc.sync, mk("s_sp"), False)
        self.chans = [Chan(mk(f"s_ch{i}")) for i in range(nchan)]
        self.chan_i = 0
        self.reserved = set()
        self.keychan = {}
        self.res = {}
        self.flip = 0

    def chan(self, reserve=False):
        while True:
            c = self.chans[self.chan_i % len(self.chans)]
            self.chan_i += 1
            if c not in self.reserved:
                break
        if reserve:
            self.reserved.add(c)
        return c

    def _need(self, w, tok, raw):
        if tok[0] == "E":
            e, idx = tok[1], tok[2]
            if e is w and not w.self_sync:
                return
            v = e.value_for(idx)
            key = e
        else:
            key, v = tok[1], tok[2]
        if w.seen.get(key, 0) >= v:
            return
        w.q.wait_ge(key.sem, v)
        w.seen[key] = v

    def _deps(self, w, rd, wr):
        for k in rd:
            r = self.res.get(k)
            if r is not None and r[0] is not None:
                self._need(w, r[0], True)
        for k in wr:
            r = self.res.get(k)
            if r is not None:
                if r[0] is not None:
                    self._need(w, r[0], False)
                for t in r[1].values():
                    self._need(w, t, False)

    def _commit(self, tok, owner, rd, wr):
        for k in rd:
            r = self.res.get(k)
            if r is None:
                r = self.res[k] = [None, {}]
            r[1][owner] = tok
        for k in wr:
            self.res[k] = [tok, {}]

    def op(self, eng, fn, rd=(), wr=()):
        self._deps(eng, rd, wr)
        inst = fn()
        eng.last = inst
        tok = ("E", eng, eng.n)
        eng.n += 1
        if eng.self_sync:
            eng.value_for(eng.n - 1)
        self._commit(tok, eng, rd, wr)
        return inst

    def dma(self, q, ck, out, in_, rd=(), wr=(), **kw):
        if isinstance(ck, Chan):
            ch = ck
        else:
            ch = self.keychan.get(ck)
            if ch is None:
                ch = self.keychan[ck] = self.chan()
                assert len(self.keychan) <= len(self.chans) - len(self.reserved), "out of DMA channels"
        self._deps(q, rd, wr)
        q.q.dma_start(out=out, in_=in_, **kw).then_inc(ch.sem, 16)
        ch.count += 16
        tok = ("C", ch, ch.count)
        self._commit(tok, ch, rd, wr)

    def barrier(self, bar_tile):
        d = self.dve
        for e in (self.pe, self.act, self.pool):
            if e.n > 0:
                self._need(d, ("E", e, e.n - 1), True)
        for c in self.chans:
            if c.count > 0:
                self._need(d, ("C", c, c.count), True)
        self.op(d, lambda: self.nc.vector.memset(bar_tile, 0.0), wr=[("bar",)])
        tok = self.res[("bar",)][0]
        for e in (self.pe, self.act, self.sp):
            self._need(e, tok, True)
        self.res = {k: v for k, v in self.res.items() if k[0] in ("ring", "wconv")}
        self.keychan = {}
        self.chan_i = 0

    def evac(self, out, in_, rd, wr):
        self.flip ^= 1
        nc = self.nc
        if self.flip:
            return self.op(self.act, lambda: nc.scalar.activation(out=out, in_=in_, func=AF.Copy), rd, wr)
        return self.op(self.dve, lambda: nc.vector.tensor_copy(out=out, in_=in_), rd, wr)


class Ring:
    def __init__(self, S, es):
        nc = S.nc
        self.S = S
        self.slots = [es.enter_context(nc.sbuf_tensor(f"ring{i}", [128, SLOT], BF16)) for i in range(NSLOT)]
        self.ch = [S.chan(reserve=True) for _ in range(NSLOT)]
        self.i = 0

    def run(self, steps):
        S = self.S
        n = len(steps)
        base = self.i
        issued = 0

        def issue(j):
            si = (base + j) % NSLOT
            key = ("ring", si)
            for ld in steps[j][0]:
                vf, src = ld[0], ld[1]
                rdk = ld[2] if len(ld) > 2 else ()
                S.dma(S.pool, self.ch[si], out=vf(self.slots[si]), in_=src, rd=rdk, wr=[key], max_dma_last_dim=8192)

        for j in range(n):
            while issued < min(n, j + NSLOT):
                issue(issued)
                issued += 1
            si = (base + j) % NSLOT
            steps[j][1](self.slots[si], ("ring", si))
        self.i = base + n


def wview(kc, cols):
    return lambda slot: slot[:, 0:kc * cols].rearrange("p (k c) -> p k c", k=kc)


def wsrc(w2d, r0, kc, c0, cols):
    return w2d[r0:r0 + 128 * kc, c0:c0 + cols].rearrange("(k p) c -> p k c", p=128)


def build(stop_after=None, debug_outs=()):
    nc = bass.Bass("TRN2", target_bir_lowering=False)
    es = contextlib.ExitStack()
    dbg = set(debug_outs)

    def din(name, shape, dt=F32):
        return nc.dram_tensor(name, list(shape), dt, kind="ExternalInput").ap()

    def dout(name, shape, dt=F32):
        return nc.dram_tensor(name, list(shape), dt, kind="ExternalOutput").ap()

    def dscr(name, shape, dt):
        kind = "ExternalOutput" if name in dbg else "Internal"
        return nc.dram_tensor(name, list(shape), dt, kind=kind).ap()

    x_s = din("x_s", [NSQ, D])
    x_p = din("x_p", [2 * NP, D])
    ccol = din("ccol", [128, 16, 2])
    cckv = din("cckv", [DEPTH, CTX, 512])
    ckr = din("ckr", [DEPTH, CTX, 64])
    st_in = din("st_in", [DEPTH, 2, 4, 128, 128])
    w_ada = din("w_ada", [DEPTH, D, 6 * D])
    b_ada = din("b_ada", [DEPTH, 6 * D])
    bcol = din("bcol", [128, DEPTH, 96])
    gcol = din("gcol", [128, DEPTH, 2, 16])
    g_post_mix = din("g_post_mix", [DEPTH, D])
    g_post_ff = din("g_post_ff", [DEPTH, D])
    w_in = din("w_in", [DEPTH, D, INC])
    lbcol = din("lbcol", [128, DEPTH, 2, 4])
    hg_gain = din("hg_gain", [DEPTH, 512])
    qnorm = din("qnorm", [DEPTH, 768])
    kvnorm = din("kvnorm", [DEPTH, 512])
    w_q_b = din("w_q_b", [DEPTH, 768, 1536])
    w_kv_b = din("w_kv_b", [DEPTH, 512, 2048])
    w_out = din("w_out", [DEPTH, D, D])
    w_ff1 = din("w_ff1", [DEPTH, D, DFF])
    w_ff2 = din("w_ff2", [DEPTH, DFF, D])
    c_ident = din("c_ident", [128, 128])
    c_cosT = din("c_cosT", [64, NSQ])
    c_sinT = din("c_sinT", [64, NSQ])
    c_rmat = din("c_rmat", [128, 64])
    c_dftL = din("c_dftL", [2, NSQ, NSQ])
    c_dftP = din("c_dftP", [2, NP, NP])
    c_dftd = din("c_dftd", [128, 256])
    c_mask = din("c_mask", [2, 128, 128])

    y_s = dout("y_s", [NSQ, D])
    y_p = dout("y_p", [2 * NP, D])
    o_ckv = dout("o_ckv", [2, DEPTH, NP, 512])
    o_kr = dout("o_kr", [2, DEPTH, NP, 64])
    o_st = dout("o_st", [2, DEPTH, 2, 4, 128, 128])

    d_uT = dscr("d_uT", [512, NT], BF16)
    d_hqT = dscr("d_hqT", [512, NT], BF16)
    d_hv = dscr("d_hv", [NT, 512], BF16)
    d_lf = dscr("d_lf", [2, 512, NT], F32)
    d_kk = dscr("d_kk", [2, 512, NT], BF16)
    d_gate = dscr("d_gate", [NT, 512], BF16)
    d_QT = dscr("d_QT", [8, 192, NT], BF16)
    d_KT = dscr("d_KT", [8, 128, NK], BF16)
    d_KR = dscr("d_KR", [64, NK], BF16)
    d_V = dscr("d_V", [NK, 1024], BF16)
    d_yT = dscr("d_yT", [D, NT], BF16)
    d_x1 = dscr("d_x1", [NT, D], F32)
    d_h2T = dscr("d_h2T", [D, NT], BF16)
    d_xres = dscr("d_xres", [NT, D], F32)
    d_wob = dscr("d_wob", [D, D], BF16)
    d_w1b = dscr("d_w1b", [D, DFF], BF16)
    d_w2b = dscr("d_w2b", [DFF, D], BF16)
    d_G = dscr("d_G", [2, 2, D], F32)

    S = Sched(nc, es, nchan=40)
    PE, ACT, DVE, POOL, SP = S.pe, S.act, S.dve, S.pool, S.sp
    ring = Ring(S, es)

    uniq = [0]

    class Reuse:
        def __init__(self, st, tag):
            self.st = st
            self.tag = tag
            self.cache = {}

    def sb(st, name, shape, dt):
        if isinstance(st, Reuse):
            k = (st.tag, name, tuple(shape), str(dt))
            if k not in st.cache:
                uniq[0] += 1
                st.cache[k] = st.st.enter_context(nc.sbuf_tensor(f"{name}_{uniq[0]}", list(shape), dt))
            return st.cache[k]
        uniq[0] += 1
        return st.enter_context(nc.sbuf_tensor(f"{name}_{uniq[0]}", list(shape), dt))

    def scope(outer):
        return contextlib.nullcontext(outer) if outer is not None else contextlib.ExitStack()

    banks = [es.enter_context(nc.psum_tensor(f"bank{i}", [128, 512], F32)) for i in range(8)]
    bank_i = [0]

    def nb():
        i = bank_i[0] % 8
        bank_i[0] += 1
        return banks[i], ("ps", i)

    ident_f = sb(es, "ident_f", [128, 128], F32)
    ident_b = sb(es, "ident_b", [128, 128], BF16)
    bar_t = sb(es, "bar_t", [128, 2], F32)
    modc = sb(es, "modc", [128, 4, 16, 2], F32)
    AB = sb(es, "AB", [128, 4, 16, 2], F32)
    gcol_t = sb(es, "gcol_t", [128, DEPTH, 2, 16], F32)
    bcol_t = sb(es, "bcol_t", [128, DEPTH, 96], F32)
    lb_t = sb(es, "lb_t", [128, DEPTH, 2, 4], F32)
    oml_t = sb(es, "oml_t", [128, DEPTH, 2, 4], F32)
    eps_t = sb(es, "eps_t", [128, 1], F32)

    ch0 = S.chan()
    S.dma(SP, "ident_f", out=ident_f[:], in_=c_ident[:, :], wr=["ident_f"])
    S.dma(SP, "gcol", out=gcol_t[:], in_=gcol[:, :, :, :], wr=["gcol"])
    S.dma(SP, "bcol", out=bcol_t[:], in_=bcol[:, :, :], wr=["bcol"])
    S.dma(SP, "lbraw", out=lb_t[:], in_=lbcol[:, :, :, :], wr=["lbraw"])
    S.op(DVE, lambda: nc.vector.tensor_copy(out=ident_b[:], in_=ident_f[:]), rd=["ident_f"], wr=["ident_b"])
    S.op(DVE, lambda: nc.vector.memset(eps_t[:], EPS), wr=["eps"])
    with contextlib.ExitStack() as st:
        t0 = sb(st, "lbtmp", [128, 8], F32)
        S.op(DVE, lambda: nc.vector.tensor_tensor(out=t0[:], in0=lb_t[:, 0].rearrange("p a b -> p (a b)"),
                                                  in1=lb_t[:, 1].rearrange("p a b -> p (a b)"), op=ALU.subtract),
             rd=["lbraw"], wr=["lbtmp"])
        S.op(ACT, lambda: nc.scalar.activation(out=t0[:], in_=t0[:], func=AF.Exp), rd=["lbtmp"], wr=["lbtmp"])
        S.op(DVE, lambda: nc.vector.tensor_scalar(out=t0[:], in0=t0[:], scalar1=1.0, scalar2=None, op0=ALU.add),
             rd=["lbtmp"], wr=["lbtmp"])
        S.op(DVE, lambda: nc.vector.reciprocal(out=lb_t[:, 1].rearrange("p a b -> p (a b)"), in_=t0[:]),
             rd=["lbtmp"], wr=["lbraw"])
        S.op(DVE, lambda: nc.vector.memset(lb_t[:, 0].rearrange("p a b -> p (a b)"), 0.0), rd=["lbraw"], wr=["lbraw"])
        S.op(DVE, lambda: nc.vector.tensor_scalar(out=oml_t[:].rearrange("p l a b -> p (l a b)"),
                                                  in0=lb_t[:].rearrange("p l a b -> p (l a b)"),
                                                  scalar1=-1.0, scalar2=1.0, op0=ALU.mult, op1=ALU.add),
             rd=["lbraw"], wr=["oml"])
        S.barrier(bar_t[:])

    def phase0(l):
        with contextlib.ExitStack() as st:
            cc = sb(st, "cc", [128, 16, 2], F32)
            s2 = sb(st, "s2", [128, 16, 2], BF16)
            srep = sb(st, "srep", [128, 16, 2, 128], BF16)
            bg = sb(st, "bg", [128, 2, D], F32)
            gp = sb(st, "gp", [128, 2, D], F32)
            gst = sb(st, "gst", [128, 2, 2, 512], F32)
            ch = S.chan()
            S.dma(SP, "cc", out=cc[:], in_=ccol[:, :, :], wr=["cc"])
            for v in range(2):
                S.dma(SP, ("bg", v), out=bg[:, v, :], in_=b_ada[l, (2 + 3 * v) * D:(3 + 3 * v) * D].partition_broadcast(128),
                      wr=[("bg", v)])
            S.dma(SP, ("gp", 0), out=gp[:, 0, :], in_=g_post_mix[l, :].partition_broadcast(128), wr=[("gp", 0)])
            S.dma(SP, ("gp", 1), out=gp[:, 1, :], in_=g_post_ff[l, :].partition_broadcast(128), wr=[("gp", 1)])
            S.op(ACT, lambda: nc.scalar.activation(out=s2[:], in_=cc[:], func=AF.Silu), rd=["cc"], wr=["s2"])
            S.op(DVE, lambda: nc.vector.tensor_copy(
                out=srep[:].rearrange("p k c m -> p (k c) m"),
                in_=s2[:].rearrange("p k c -> p (k c)").unsqueeze(2).broadcast_to([128, 32, 128])),
                rd=["s2"], wr=["srep"])
            steps = []
            wl = w_ada[l]
            vec_of = {0: 0, 1: 1, 3: 2, 4: 3}
            gch = S.chan()
            for sec in range(6):
                for j in range(4):
                    c0 = sec * D + j * 512
                    loads = [(wview(16, 512), wsrc(wl, 0, 16, c0, 512))]
                    if sec in vec_of:
                        def comp(slot, key, sec=sec, j=j):
                            w3 = wview(16, 512)(slot)
                            bk, bkey = nb()
                            for sub in range(4):
                                for k in range(16):
                                    S.op(PE, lambda: nc.tensor.matmul(
                                        bk[:, sub * 2:sub * 2 + 2], lhsT=w3[:, k, sub * 128:(sub + 1) * 128],
                                        rhs=s2[:, k, :], start=(k == 0), stop=(k == 15)),
                                        rd=[key, "s2"], wr=[bkey])
                            vi = vec_of[sec]
                            ch0_ = sec * 16 + j * 4
                            S.op(DVE, lambda: nc.vector.tensor_tensor(
                                out=modc[:, vi, j * 4:(j + 1) * 4, :],
                                in0=bk[:, 0:8].rearrange("p (s c) -> p s c", c=2),
                                in1=bcol_t[:, l, ch0_:ch0_ + 4].unsqueeze(2).broadcast_to([128, 4, 2]),
                                op=ALU.add), rd=[bkey, "bcol"], wr=[("modc", vi)])
                    else:
                        def comp(slot, key, sec=sec, j=j):
                            w3 = wview(16, 512)(slot)
                            v = 0 if sec == 2 else 1
                            for cond in range(2):
                                bk, bkey = nb()
                                for k in range(16):
                                    S.op(PE, lambda: nc.tensor.matmul(
                                        bk[:, :], lhsT=srep[:, k, cond, :], rhs=w3[:, k, :],
                                        start=(k == 0), stop=(k == 15)), rd=[key, "srep"], wr=[bkey])
                                gk = ("gst", v, cond)
                                S.op(DVE, lambda: nc.vector.tensor_tensor(
                                    out=gst[:, v, cond, :], in0=bk[:, :], in1=bg[:, v, j * 512:(j + 1) * 512],
                                    op=ALU.add), rd=[bkey, ("bg", v)], wr=[gk])
                                S.op(DVE, lambda: nc.vector.tensor_tensor(
                                    out=gst[:, v, cond, :], in0=gst[:, v, cond, :], in1=gp[:, v, j * 512:(j + 1) * 512],
                                    op=ALU.mult), rd=[gk, ("gp", v)], wr=[gk])
                                S.dma(SP, gk, out=d_G[v, cond, j * 512:(j + 1) * 512].unsqueeze(0),
                                      in_=gst[0:1, v, cond, :], rd=[gk], wr=[("dG", v, cond, j)])
                    steps.append((loads, comp))
            ring.run(steps)
            for half in range(2):
                S.op(DVE, lambda: nc.vector.tensor_scalar(
                    out=AB[:, 2 * half].rearrange("p k c -> p (k c)"),
                    in0=modc[:, 2 * half + 1].rearrange("p k c -> p (k c)"),
                    scalar1=1.0, scalar2=None, op0=ALU.add), rd=[("modc", 2 * half + 1)], wr=[("AB", 2 * half)])
                S.op(DVE, lambda: nc.vector.tensor_tensor(
                    out=AB[:, 2 * half], in0=AB[:, 2 * half],
                    in1=gcol_t[:, l, half, :].unsqueeze(2).broadcast_to([128, 16, 2]), op=ALU.mult),
                    rd=[("AB", 2 * half), "gcol"], wr=[("AB", 2 * half)])
                S.op(DVE, lambda: nc.vector.tensor_copy(out=AB[:, 2 * half + 1], in_=modc[:, 2 * half]),
                     rd=[("modc", 2 * half)], wr=[("AB", 2 * half + 1)])
            S.barrier(bar_t[:])

    def rms_front(st_name, xt, xkey, xn, xnkey, small, idx):
        junk = small["junk"]
        ss = small["ss"][:, idx:idx + 1]
        S.op(ACT, lambda: nc.scalar.activation(out=junk[:], in_=xt, func=AF.Square, accum_out=ss),
             rd=[xkey], wr=list(small.get("jkeys", ["junk"])) + [("ss", idx)])
        S.op(ACT, lambda: nc.scalar.activation(out=ss, in_=ss, func=AF.Sqrt, bias=eps_t[:], scale=1.0 / D),
             rd=[("ss", idx), "eps"], wr=[("ss", idx)])
        S.op(DVE, lambda: nc.vector.reciprocal(out=ss, in_=ss), rd=[("ss", idx)], wr=[("ss", idx)])
        S.op(DVE, lambda: nc.vector.tensor_scalar(out=xn, in0=xt, scalar1=ss, scalar2=None, op0=ALU.mult),
             rd=[xkey, ("ss", idx)], wr=[xnkey])

    def transpose_mod(xn_tiles, xn_keys, hT, Aidx, cond):
        for g in range(8):
            bk, bkey = nb()
            pb = bk[:, :].bitcast(BF16).rearrange("p (c t) -> p c t", c=2)
            for cc in range(2):
                c = g * 2 + cc
                for sub in range(4):
                    S.op(PE, lambda: nc.tensor.transpose(
                        out=pb[:, cc, sub * 128:(sub + 1) * 128], in_=xn_tiles[sub][:, c * 128:(c + 1) * 128],
                        identity=ident_b[:]), rd=[xn_keys[sub], "ident_b"], wr=[bkey])
            for cc in range(2):
                c = g * 2 + cc
                S.op(ACT, lambda: nc.scalar.activation(
                    out=hT[:, c, :], in_=pb[:, cc, :], func=AF.Identity,
                    scale=AB[:, Aidx, c, cond:cond + 1], bias=AB[:, Aidx + 1, c, cond:cond + 1]),
                    rd=[bkey, ("AB", Aidx), ("AB", Aidx + 1)], wr=[("hT", c)])

    def tile_src(l, tt):
        if l == 0:
            if tt < 4:
                return [x_s[tt * 512 + s * 128: tt * 512 + (s + 1) * 128, :] for s in range(4)]
            return [x_p[s * 128:(s + 1) * 128, :] for s in range(4)]
        return [d_xres[tt * 512 + s * 128: tt * 512 + (s + 1) * 128, :] for s in range(4)]

    def kv_backend(st, l, ckv_tm, ckv_keys, kr_tm, kr_keys, key0, pos0):
        ckvT = sb(st, "ckvT", [128, 4, 512], BF16)
        krT = sb(st, "krT", [64, 512], F32)
        krb = sb(st, "krb", [128, 512], BF16)
        kst = sb(st, "kst", [128, 2, 512], BF16)
        vst = sb(st, "vst", [128, 4, 1024], BF16)
        chk = S.chan()
        chv = S.chan()
        for s in range(4):
            bk, bkey = nb()
            for c in range(4):
                S.op(PE, lambda: nc.tensor.transpose(out=bk[:, c * 128:(c + 1) * 128],
                                                     in_=ckv_tm[s][:, c * 128:(c + 1) * 128], identity=ident_f[:]),
                     rd=[ckv_keys[s], "ident_f"], wr=[bkey])
            S.evac(ckvT[:, :, s * 128:(s + 1) * 128], bk[:, :].rearrange("p (c t) -> p c t", c=4),
                   rd=[bkey], wr=[("ckvT", s)])
            bk2, bkey2 = nb()
            S.op(PE, lambda: nc.tensor.transpose(out=bk2[0:64, 0:128], in_=kr_tm[s], identity=ident_f[:]),
                 rd=[kr_keys[s], "ident_f"], wr=[bkey2])
            S.evac(krT[:, s * 128:(s + 1) * 128], bk2[0:64, 0:128], rd=[bkey2], wr=[("krT", s)])
        ckv_all = [("ckvT", s) for s in range(4)]
        krT_all = [("krT", s) for s in range(4)]
        S.op(DVE, lambda: nc.vector.memset(krb[64:128, :], 0.0), wr=["krbpad"])
        S.op(DVE, lambda: nc.vector.tensor_copy(out=krb[0:64, :], in_=krT[:]), rd=krT_all, wr=["krb"])
        if pos0 is not None:
            rope_apply(st, krT[:], krT_all, krb[:, :], ["krb", "krbpad"], pos0, krb[0:64, :], "krb2")
            S.dma(SP, "krb", out=d_KR[:, key0:key0 + 512], in_=krb[0:64, :], rd=["krb2"], wr=[("dKR", key0)])
        else:
            S.dma(SP, "krb", out=d_KR[:, key0:key0 + 512], in_=krb[0:64, :], rd=["krb"], wr=[("dKR", key0)])
        steps = []
        wl = w_kv_b[l]
        for hp in range(4):
            loads = [(wview(4, 512), wsrc(wl, 0, 4, hp * 512, 512))]

            def comp(slot, key, hp=hp):
                w3 = wview(4, 512)(slot)
                for hh in range(2):
                    bk, bkey = nb()
                    for k in range(4):
                        S.op(PE, lambda: nc.tensor.matmul(bk[:, :], lhsT=w3[:, k, hh * 256:hh * 256 + 128],
                                                          rhs=ckvT[:, k, :], start=(k == 0), stop=(k == 3)),
                             rd=[key] + ckv_all, wr=[bkey])
                    S.evac(kst[:, hh, :], bk[:, :], rd=[bkey], wr=[("kst", hh)])
                    S.dma(SP, ("kst", hh), out=d_KT[hp * 2 + hh, :, key0:key0 + 512], in_=kst[:, hh, :],
                          rd=[("kst", hh)], wr=[("dKT", hp * 2 + hh, key0)])
                wv = w3[:, :, :].rearrange("p k (h c) -> p k h c", h=2)
                for s in range(4):
                    bk, bkey = nb()
                    for k in range(4):
                        S.op(PE, lambda: nc.tensor.matmul(bk[:, 0:256].rearrange("p (h c) -> p h c", h=2),
                                                          lhsT=ckvT[:, k, s * 128:(s + 1) * 128],
                                                          rhs=wv[:, k, :, 128:256], start=(k == 0), stop=(k == 3)),
                             rd=[key, ("ckvT", s)], wr=[bkey])
                    S.evac(vst[:, s, hp * 256:(hp + 1) * 256], bk[:, 0:256], rd=[bkey], wr=[("vst", s)])
            steps.append((loads, comp))
        ring.run(steps)
        for s in range(4):
            S.dma(SP, ("vst", s), out=d_V[key0 + s * 128:key0 + (s + 1) * 128, :], in_=vst[:, s, :],
                  rd=[("vst", s)], wr=[("dV", key0, s)])

    rope_tabs = {}

    def rope_apply(st, x_f32, xkeys, xb, xbkey, pos0, out_b, outkey):
        cosT, sinT, rmb = rope_tabs["cos"], rope_tabs["sin"], rope_tabs["rm"]
        cnt = rope_tabs["n"] = rope_tabs.get("n", 0) + 1
        t1 = rope_tabs["t1"][cnt % 2]
        t1k = ("ropet1", cnt % 2)
        S.op(DVE, lambda: nc.vector.tensor_tensor(out=t1[:], in0=x_f32, in1=cosT[:, pos0:pos0 + 512], op=ALU.mult),
             rd=list(xkeys) + ["ropetab"], wr=[t1k])
        if DBG_STAGE[0] == 96 and DBG_STAGE[1] <= 1:
            S.op(DVE, lambda: nc.vector.tensor_copy(out=out_b, in_=t1[:]), rd=[t1k], wr=[outkey])
            return
        bk, bkey = nb()
        S.op(PE, lambda: nc.tensor.matmul(bk[0:64, :], lhsT=rmb[:], rhs=xb, start=True, stop=True),
             rd=list(xbkey) + ["ropetab"], wr=[bkey])
        if DBG_STAGE[0] == 96 and DBG_STAGE[1] <= 2:
            S.op(DVE, lambda: nc.vector.tensor_copy(out=out_b, in_=bk[0:64, :]), rd=[t1k, bkey], wr=[outkey])
            return
        t2 = rope_tabs["t2"][cnt % 2]
        t2k = ("ropet2", cnt % 2)
        S.op(DVE, lambda: nc.vector.tensor_tensor(out=t2[:], in0=bk[0:64, :], in1=sinT[:, pos0:pos0 + 512],
                                                  op=ALU.mult), rd=[bkey, "ropetab"], wr=[t2k])
        S.op(DVE, lambda: nc.vector.tensor_tensor(out=out_b, in0=t1[:], in1=t2[:], op=ALU.add),
             rd=[t1k, t2k] + list(xbkey), wr=[outkey])

    def load_rope_tabs(st):
        cosT = sb(st, "cosT", [64, NSQ], F32)
        sinT = sb(st, "sinT", [64, NSQ], F32)
        rmf = sb(st, "rmf", [128, 64], F32)
        rmb = sb(st, "rmb", [128, 64], BF16)
        ch = S.chan()
        S.dma(SP, "ropetab_c", out=cosT[:], in_=c_cosT[:, :], wr=["ropetab_c"])
        S.dma(SP, "ropetab_s", out=sinT[:], in_=c_sinT[:, :], wr=["ropetab_s"])
        S.dma(SP, "rmf", out=rmf[:], in_=c_rmat[:, :], wr=["rmf"])
        S.op(DVE, lambda: nc.vector.tensor_copy(out=rmb[:], in_=rmf[:]), rd=["rmf", "ropetab_c", "ropetab_s"],
             wr=["ropetab"])
        rope_tabs.update(cos=cosT, sin=sinT, rm=rmb,
                         t1=[sb(st, f"ropet1_{i}", [64, 512], F32) for i in range(2)],
                         t2=[sb(st, f"ropet2_{i}", [64, 512], F32) for i in range(2)])

    def phase1_ctx(l):
        with contextlib.ExitStack() as st:
            ck = [sb(st, f"cck{s}", [128, 512], F32) for s in range(4)]
            kr = [sb(st, f"ckr{s}", [128, 64], F32) for s in range(4)]
            ch = S.chan()
            for s in range(4):
                S.dma(SP, ("cck", s), out=ck[s][:], in_=cckv[l, s * 128:(s + 1) * 128, :], wr=[("cck", s)])
                S.dma(SP, ("ckr", s), out=kr[s][:], in_=ckr[l, s * 128:(s + 1) * 128, :], wr=[("ckr", s)])
            kv_backend(st, l, [t[:] for t in ck], [("cck", s) for s in range(4)],
                       [t[:] for t in kr], [("ckr", s) for s in range(4)], 0, None)
            S.barrier(bar_t[:])

    def phase1(l, tt):
        cond = 0 if tt < 4 else 1
        tok0 = tt * 512
        is_s = tt < 4
        with contextlib.ExitStack() as st:
            hT = sb(st, "hT", [128, 16, 512], BF16)
            with contextlib.ExitStack() as st2:
                xt = [sb(st2, f"xt{i}", [128, D], F32) for i in range(2)]
                xn = [sb(st2, f"xn{i}", [128, D], BF16) for i in range(4)]
                small = {"junk": sb(st2, "junk", [128, D], BF16), "ss": sb(st2, "ss", [128, 8], F32)}
                chx = [S.chan() for _ in range(2)]
                srcs = tile_src(l, tt)
                for s in range(4):
                    S.dma(SP, ("xt", s % 2), out=xt[s % 2][:], in_=srcs[s], wr=[("xt", s % 2)])
                    rms_front("p1", xt[s % 2][:], ("xt", s % 2), xn[s][:], ("xn", s), small, s)
                transpose_mod([t for t in xn], [("xn", s) for s in range(4)], hT, 0, cond)
                S.barrier(bar_t[:])
            small = {"junk": sb(st, "junk", [128, 768], BF16), "ss": sb(st, "ss", [128, 8], F32)}
            if DBG_STAGE[0] <= 1:
                return
            fst = sb(st, "fst", [128, 4, 512], BF16)
            lst = sb(st, "lst", [128, 4, 512], F32)
            sg = [sb(st, f"sg{i}", [128, 512], F32) for i in range(2)]
            tst = sb(st, "tst", [128, 4, 512], BF16)
            mla = [sb(st, f"mla{i}", [128, 1344], F32) for i in range(4)]
            qn = sb(st, "qn", [128, 768], BF16)
            ckvn = [sb(st, f"ckvn{i}", [128, 512], F32) for i in range(4)]
            qnT = sb(st, "qnT", [128, 6, 512], BF16)
            qnb = sb(st, "qnb", [128, 768], F32)
            kvb = sb(st, "kvb", [128, 512], F32)
            qst = sb(st, "qst", [128, 2, 512], BF16)
            qrb = sb(st, "qrb", [128, 2, 512], BF16)
            S.op(DVE, lambda: nc.vector.memset(qrb[64:128, :, :], 0.0), wr=["qrbpad"])
            qro = sb(st, "qro", [64, 2, 512], BF16)
            qxf = sb(st, "qxf", [64, 2, 512], F32)
            if is_s:
                load_rope_tabs(st)
            chs = S.chan()
            S.dma(SP, "qnb", out=qnb[:], in_=qnorm[l, :].partition_broadcast(128), wr=["qnb"])
            S.dma(SP, "kvb", out=kvb[:], in_=kvnorm[l, :].partition_broadcast(128), wr=["kvb"])
            hT_all = [("hT", c) for c in range(16)]
            wl = w_in[l]
            cho = S.chan()
            steps = []

            def fm_group(c0, post):
                loads = [(wview(16, 512), wsrc(wl, 0, 16, c0, 512))]

                def comp(slot, key):
                    w3 = wview(16, 512)(slot)
                    for j in range(4):
                        bk, bkey = nb()
                        for k in range(16):
                            S.op(PE, lambda: nc.tensor.matmul(bk[:, :], lhsT=w3[:, k, j * 128:(j + 1) * 128],
                                                              rhs=hT[:, k, :], start=(k == 0), stop=(k == 15)),
                                 rd=[key, ("hT", k)], wr=[bkey])
                        post(j, bk, bkey)
                steps.append((loads, comp))

            def tm_group(c0, ncols, post):
                loads = [(wview(16, ncols), wsrc(wl, 0, 16, c0, ncols))]

                def comp(slot, key):
                    w3 = wview(16, ncols)(slot)
                    for s in range(4):
                        bk, bkey = nb()
                        for k in range(16):
                            S.op(PE, lambda: nc.tensor.matmul(bk[:, 0:ncols], lhsT=hT[:, k, s * 128:(s + 1) * 128],
                                                              rhs=w3[:, k, :], start=(k == 0), stop=(k == 15)),
                                 rd=[key, ("hT", k)], wr=[bkey])
                        post(s, bk, bkey)
                steps.append((loads, comp))

            def post_plain(dst):
                def post(j, bk, bkey):
                    S.evac(fst[:, j, :], bk[:, :], rd=[bkey], wr=[("fst", j)])
                    S.dma(SP, ("fst", j), out=dst[j * 128:(j + 1) * 128, tok0:tok0 + 512], in_=fst[:, j, :],
                          rd=[("fst", j)], wr=[("dfm", id(dst), j, tt)])
                return post

            def post_gate(dirn):
                def post(j, bk, bkey):
                    g = sg[j % 2]
                    gk = ("sg", j % 2)
                    S.op(ACT, lambda: nc.scalar.activation(out=g[:], in_=bk[:, :], func=AF.Sigmoid),
                         rd=[bkey], wr=[gk])
                    S.op(DVE, lambda: nc.vector.tensor_scalar(out=g[:], in0=g[:], scalar1=oml_t[:, l, dirn, j:j + 1],
                                                              scalar2=lb_t[:, l, dirn, j:j + 1], op0=ALU.mult,
                                                              op1=ALU.add), rd=[gk, "oml", "lbraw"], wr=[gk])
                    S.op(DVE, lambda: nc.vector.tensor_scalar(out=fst[:, j, :], in0=g[:], scalar1=-1.0, scalar2=1.0,
                                                              op0=ALU.mult, op1=ALU.add), rd=[gk], wr=[("fst", j)])
                    S.op(DVE, lambda: nc.vector.tensor_scalar(out=g[:], in0=g[:], scalar1=1e-30, scalar2=None,
                                                              op0=ALU.max), rd=[gk], wr=[gk])
                    S.op(ACT, lambda: nc.scalar.activation(out=lst[:, j, :], in_=g[:], func=AF.Ln),
                         rd=[gk], wr=[("lst", j)])
                    S.dma(SP, ("fst", j), out=d_kk[dirn, j * 128:(j + 1) * 128, tok0:tok0 + 512], in_=fst[:, j, :],
                          rd=[("fst", j)], wr=[("dkk", dirn, j, tt)])
                    S.dma(SP, ("lst", j), out=d_lf[dirn, j * 128:(j + 1) * 128, tok0:tok0 + 512], in_=lst[:, j, :],
                          rd=[("lst", j)], wr=[("dlf", dirn, j, tt)])
                return post

            def post_tm(dst, silu):
                def post(s, bk, bkey):
                    if silu:
                        S.op(ACT, lambda: nc.scalar.activation(out=tst[:, s, :], in_=bk[:, :], func=AF.Silu),
                             rd=[bkey], wr=[("tst", s)])
                    else:
                        S.evac(tst[:, s, :], bk[:, :], rd=[bkey], wr=[("tst", s)])
                    S.dma(SP, ("tst", s), out=dst[tok0 + s * 128:tok0 + (s + 1) * 128, :], in_=tst[:, s, :],
                          rd=[("tst", s)], wr=[("dtm", id(dst), s, tt)])
                return post

            def post_mla(off, ncols):
                def post(s, bk, bkey):
                    S.evac(mla[s][:, off:off + ncols], bk[:, 0:ncols], rd=[bkey], wr=[("mla", s, off)])
                return post

            fm_group(0, post_plain(d_uT))
            fm_group(512, post_plain(d_hqT))
            tm_group(1024, 512, post_tm(d_hv, False))
            fm_group(1536, post_gate(0))
            fm_group(2048, post_gate(1))
            tm_group(2560, 512, post_tm(d_gate, True))
            tm_group(3072, 512, post_mla(0, 512))
            tm_group(3584, 512, post_mla(512, 512))
            tm_group(4096, 320, post_mla(1024, 320))
            if DBG_STAGE[0] <= 2:
                steps = steps[:DBG_STAGE[1]]
            ring.run(steps)
            if DBG_STAGE[0] <= 2:
                S.barrier(bar_t[:])
                return

            ss = small["ss"]
            junk = small["junk"]
            for s in range(4):
                mk = [("mla", s, 0), ("mla", s, 512), ("mla", s, 1024)]
                for (i, (a, b)) in enumerate(((0, 768), (768, 1280))):
                    sidx = 4 + i
                    ssv = ss[:, sidx:sidx + 1]
                    n = b - a
                    S.op(ACT, lambda: nc.scalar.activation(out=junk[:, 0:n], in_=mla[s][:, a:b], func=AF.Square,
                                                           accum_out=ssv), rd=mk, wr=["junk", ("ss", sidx)])
                    S.op(ACT, lambda: nc.scalar.activation(out=ssv, in_=ssv, func=AF.Sqrt, bias=eps_t[:],
                                                           scale=1.0 / n), rd=[("ss", sidx), "eps"], wr=[("ss", sidx)])
                    S.op(DVE, lambda: nc.vector.reciprocal(out=ssv, in_=ssv), rd=[("ss", sidx)], wr=[("ss", sidx)])
                S.op(DVE, lambda: nc.vector.scalar_tensor_tensor(out=qn[:], in0=mla[s][:, 0:768], scalar=ss[:, 4:5],
                                                                 in1=qnb[:], op0=ALU.mult, op1=ALU.mult),
                     rd=mk + [("ss", 4), "qnb"], wr=["qn"])
                S.op(DVE, lambda: nc.vector.scalar_tensor_tensor(out=ckvn[s][:], in0=mla[s][:, 768:1280],
                                                                 scalar=ss[:, 5:6], in1=kvb[:], op0=ALU.mult,
                                                                 op1=ALU.mult),
                     rd=mk + [("ss", 5), "kvb"], wr=[("ckvn", s)])
                if not is_s:
                    sq, r0 = divmod(s * 128, NP)
                    S.dma(SP, ("ckvn", s), out=o_ckv[sq, l, r0:r0 + 128, :], in_=ckvn[s][:], rd=[("ckvn", s)],
                          wr=[("ockv", s)])
                    S.dma(SP, ("mla", s), out=o_kr[sq, l, r0:r0 + 128, :], in_=mla[s][:, 1280:1344], rd=mk,
                          wr=[("okr", s)])
                bk, bkey = nb()
                pb = bk[:, :].bitcast(BF16)
                for c in range(6):
                    S.op(PE, lambda: nc.tensor.transpose(out=pb[:, c * 128:(c + 1) * 128],
                                                         in_=qn[:, c * 128:(c + 1) * 128], identity=ident_b[:]),
                         rd=["qn", "ident_b"], wr=[bkey])
                S.evac(qnT[:, :, s * 128:(s + 1) * 128], pb[:, 0:768].rearrange("p (c t) -> p c t", c=6),
                       rd=[bkey], wr=[("qnT", s)])
            qnT_all = [("qnT", s) for s in range(4)]
            if DBG_STAGE[0] <= 3:
                S.barrier(bar_t[:])
                return

            steps = []
            wq = w_q_b[l]
            for hp in range(4):
                loads = [(wview(6, 384), wsrc(wq, 0, 6, hp * 384, 384))]

                def comp(slot, key, hp=hp):
                    w3 = wview(6, 384)(slot)
                    for hh in range(2):
                        h = hp * 2 + hh
                        bk, bkey = nb()
                        for k in range(6):
                            S.op(PE, lambda: nc.tensor.matmul(bk[:, :], lhsT=w3[:, k, hh * 192:hh * 192 + 128],
                                                              rhs=qnT[:, k, :], start=(k == 0), stop=(k == 5)),
                                 rd=[key] + qnT_all, wr=[bkey])
                        S.evac(qst[:, hh, :], bk[:, :], rd=[bkey], wr=[("qst", hh)])
                        S.dma(SP, ("qst", hh), out=d_QT[h, 0:128, tok0:tok0 + 512], in_=qst[:, hh, :], rd=[("qst", hh)],
                              wr=[("dQT", h, 0, tt)])
                        bk2, bkey2 = nb()
                        for k in range(6):
                            S.op(PE, lambda: nc.tensor.matmul(bk2[0:64, :], lhsT=w3[:, k, hh * 192 + 128:hh * 192 + 192],
                                                              rhs=qnT[:, k, :], start=(k == 0), stop=(k == 5)),
                                 rd=[key] + qnT_all, wr=[bkey2])
                        S.op(ACT, lambda: nc.scalar.activation(out=qxf[:, hh, :], in_=bk2[0:64, :], func=AF.Copy),
                             rd=[bkey2], wr=[("qxf", hh)])
                        S.op(DVE, lambda: nc.vector.tensor_copy(out=qrb[0:64, hh, :], in_=qxf[:, hh, :]),
                             rd=[("qxf", hh)], wr=[("qrb", hh)])
                        if is_s and DBG_STAGE[0] not in (98, 97):
                            rope_apply(st, qxf[:, hh, :], [("qxf", hh)], qrb[:, hh, :], [("qrb", hh), "qrbpad"], tok0, qro[:, hh, :],
                                       ("qro", hh))
                            S.dma(SP, ("qro", hh), out=d_QT[h, 128:192, tok0:tok0 + 512], in_=qro[:, hh, :],
                                  rd=[("qro", hh)], wr=[("dQT", h, 1, tt)])
                        else:
                            S.dma(SP, ("qrb", hh), out=d_QT[h, 128:192, tok0:tok0 + 512], in_=qrb[0:64, hh, :],
                                  rd=[("qrb", hh)], wr=[("dQT", h, 1, tt)])
                steps.append((loads, comp))
            ring.run(steps)

            kv_backend(st, l, [t[:] for t in ckvn], [("ckvn", s) for s in range(4)],
                       [mla[s][:, 1280:1344] for s in range(4)],
                       [("mla", s, 1024) for s in range(4)], CTX + tok0, tok0 if (is_s and DBG_STAGE[0] not in (98,)) else None)
            S.barrier(bar_t[:])


    def bankx(i):
        return banks[i], ("ps", i)

    def phase_fn(l, tok0, N, dft, outer=None):
        nch = N // 128
        ncol = min(512, N)
        nblk = N // ncol
        with scope(outer) as st:
            uT = sb(st, "uT", [128, 4, N], BF16)
            ab = sb(st, "ab", [128, nch, 4, 256], BF16)
            ddf = sb(st, "ddf", [128, 256], F32)
            ddb = sb(st, "ddb", [128, 256], BF16)
            yst = sb(st, "yst", [128, 4, 512], BF16)
            S.dma(SP, "ddf", out=ddf[:], in_=c_dftd[:, :], wr=["ddf"])
            S.op(DVE, lambda: nc.vector.tensor_copy(out=ddb[:], in_=ddf[:]), rd=["ddf"], wr=["ddb"])
            for h in range(4):
                S.dma(SP, ("uT", h), out=uT[:, h, :], in_=d_uT[h * 128:(h + 1) * 128, tok0:tok0 + N], wr=[("uT", h)])
            for m in range(nch):
                for hp in range(2):
                    bk, bkey = nb()
                    for hh in range(2):
                        h = hp * 2 + hh
                        S.op(PE, lambda: nc.tensor.matmul(bk[:, hh * 256:(hh + 1) * 256],
                                                          lhsT=uT[:, h, m * 128:(m + 1) * 128], rhs=ddb[:],
                                                          start=True, stop=True), rd=[("uT", h), "ddb"], wr=[bkey])
                    S.evac(ab[:, m, hp * 2:(hp + 1) * 2, :], bk[:, :].rearrange("p (h c) -> p h c", h=2),
                           rd=[bkey], wr=[("ab", m, hp)])
            ab_all = [("ab", m, hp) for m in range(nch) for hp in range(2)]
            steps = []
            for nbk in range(nblk):
                for part in range(2):
                    src = dft[part, :, nbk * ncol:(nbk + 1) * ncol].rearrange("(k p) c -> p k c", p=128)
                    loads = [(wview(nch, ncol), src)]

                    def comp(slot, key, nbk=nbk, part=part):
                        w3 = wview(nch, ncol)(slot)
                        for h in range(4):
                            bk, bkey = bankx((nbk % 2) * 4 + h)
                            for m in range(nch):
                                S.op(PE, lambda: nc.tensor.matmul(
                                    bk[:, 0:ncol], lhsT=ab[:, m, h, part * 128:(part + 1) * 128], rhs=w3[:, m, :],
                                    start=(part == 0 and m == 0), stop=(part == 1 and m == nch - 1)),
                                    rd=[key] + ab_all, wr=[bkey])
                            if part == 1:
                                S.evac(yst[:, h, 0:ncol], bk[:, 0:ncol], rd=[bkey], wr=[("yst", h)])
                                S.dma(SP, ("yst", h), out=d_yT[h * 128:(h + 1) * 128,
                                                              tok0 + nbk * ncol:tok0 + (nbk + 1) * ncol],
                                      in_=yst[:, h, 0:ncol], rd=[("yst", h)], wr=[("dyT", h, tok0, nbk)])
                    steps.append((loads, comp))
            ring.run(steps)
            if outer is None:
                S.barrier(bar_t[:])

    def phase_attn(l, tok0, N, key_lo, nkeys, outer=None):
        nkc = nkeys // 128
        qb = min(512, N)
        nqb = N // qb
        scale = 192.0 ** -0.5
        with scope(outer) as st:
            kr = sb(st, "kr", [128, nkeys], BF16)
            ones = sb(st, "ones", [128, 128], BF16)
            qn_ = [sb(st, f"aqn{i}", [128, N], BF16) for i in range(2)]
            qr_ = [sb(st, f"aqr{i}", [128, N], BF16) for i in range(2)]
            kn_ = [sb(st, f"akn{i}", [128, nkeys], BF16) for i in range(2)]
            v_ = [sb(st, f"av{i}", [128, nkc, 128], BF16) for i in range(2)]
            pT = [sb(st, f"pT{i}", [128, 512], BF16) for i in range(4)]
            rec = [sb(st, f"rec{i}", [128, 512], F32) for i in range(2)]
            yst = [sb(st, f"ayst{i}", [128, 512], BF16) for i in range(2)]
            S.op(DVE, lambda: nc.vector.memset(kr[64:128, :], 0.0), wr=["krpad"])
            for i in range(2):
                S.op(DVE, lambda: nc.vector.memset(qr_[i][64:128, :], 0.0), wr=[("aqrpad", i)])
            S.dma(SP, "kr", out=kr[0:64, :], in_=d_KR[:, key_lo:key_lo + nkeys], wr=["kr"])
            S.op(DVE, lambda: nc.vector.memset(ones[:], 1.0), wr=["ones"])
            pi = 0
            qcount = 0
            for h in range(8):
                b = h % 2
                S.dma(SP, ("aqn", b), out=qn_[b][:], in_=d_QT[h, 0:128, tok0:tok0 + N], wr=[("aqn", b)])
                S.dma(SP, ("aqr", b), out=qr_[b][0:64, :], in_=d_QT[h, 128:192, tok0:tok0 + N], wr=[("aqr", b)])
                S.dma(SP, ("akn", b), out=kn_[b][:], in_=d_KT[h, :, key_lo:key_lo + nkeys], wr=[("akn", b)])
                S.dma(SP, ("av", b), out=v_[b][:],
                      in_=d_V[key_lo:key_lo + nkeys, h * 128:(h + 1) * 128].rearrange("(c p) d -> p c d", p=128),
                      wr=[("av", b)])
                for qi in range(nqb):
                    par = qcount % 2
                    qcount += 1
                    bo, bokey = bankx(par * 2)
                    br, brkey = bankx(par * 2 + 1)
                    qs = slice(qi * qb, (qi + 1) * qb)
                    LOOK = 3
                    pend = []

                    def emit_scores(kc):
                        nonlocal pi
                        bs, bskey = bankx(4 + (pi % 4))
                        p_t = pT[pi % 4]
                        pkey = ("pT", pi % 4)
                        pi += 1
                        ks = slice(kc * 128, (kc + 1) * 128)
                        S.op(PE, lambda: nc.tensor.matmul(bs[:, 0:qb], lhsT=kn_[b][:, ks], rhs=qn_[b][:, qs],
                                                          start=True, stop=False),
                             rd=[("akn", b), ("aqn", b)], wr=[bskey])
                        S.op(PE, lambda: nc.tensor.matmul(bs[:, 0:qb], lhsT=kr[:, ks], rhs=qr_[b][:, qs],
                                                          start=False, stop=True),
                             rd=["kr", "krpad", ("aqr", b), ("aqrpad", b)], wr=[bskey])
                        S.op(ACT, lambda: nc.scalar.activation(out=p_t[:, 0:qb], in_=bs[:, 0:qb], func=AF.Exp,
                                                               scale=scale), rd=[bskey], wr=[pkey])
                        pend.append((kc, p_t, pkey))

                    def emit_pv():
                        kc, p_t, pkey = pend.pop(0)
                        S.op(PE, lambda: nc.tensor.matmul(bo[:, 0:qb], lhsT=v_[b][:, kc, :], rhs=p_t[:, 0:qb],
                                                          start=(kc == 0), stop=(kc == nkc - 1)),
                             rd=[("av", b), pkey], wr=[bokey])
                        S.op(PE, lambda: nc.tensor.matmul(br[:, 0:qb], lhsT=ones[:], rhs=p_t[:, 0:qb],
                                                          start=(kc == 0), stop=(kc == nkc - 1)),
                             rd=["ones", pkey], wr=[brkey])
                    for kc in range(nkc):
                        emit_scores(kc)
                        if len(pend) > LOOK:
                            emit_pv()
                    while pend:
                        emit_pv()
                    S.op(DVE, lambda: nc.vector.reciprocal(out=rec[par][:, 0:qb], in_=br[:, 0:qb]),
                         rd=[brkey], wr=[("rec", par)])
                    S.op(DVE, lambda: nc.vector.tensor_tensor(out=yst[par][:, 0:qb], in0=bo[:, 0:qb],
                                                              in1=rec[par][:, 0:qb], op=ALU.mult),
                         rd=[bokey, ("rec", par)], wr=[("ayst", par)])
                    S.dma(SP, ("ayst", par), out=d_yT[1024 + h * 128:1024 + (h + 1) * 128,
                                                      tok0 + qi * qb:tok0 + (qi + 1) * qb],
                          in_=yst[par][:, 0:qb], rd=[("ayst", par)], wr=[("dyTa", h, tok0, qi)])
            if outer is None:
                S.barrier(bar_t[:])

    def phase3a(l, tt):
        cond = 0 if tt < 4 else 1
        tok0 = tt * 512
        with contextlib.ExitStack() as st:
            yT = sb(st, "yT", [128, 16, 512], BF16)
            G1 = sb(st, "G1", [128, D], F32)
            xt = [sb(st, f"x3_{i}", [128, D], F32) for i in range(4)]
            yo = [sb(st, f"yo{i}", [128, D], F32) for i in range(4)]
            xn = [sb(st, f"xn3_{i}", [128, D], BF16) for i in range(4)]
            h2T = sb(st, "h2T", [128, 16, 512], BF16)
            small = {"junk": sb(st, "junk3", [128, D], BF16), "ss": sb(st, "ss3", [128, 8], F32)}
            for c in range(16):
                S.dma(SP, ("yT", c), out=yT[:, c, :], in_=d_yT[c * 128:(c + 1) * 128, tok0:tok0 + 512],
                      wr=[("yT", c)])
            S.dma(SP, "G1", out=G1[:], in_=d_G[0, cond, :].partition_broadcast(128), wr=["G1"])
            srcs = tile_src(l, tt)
            for s in range(4):
                S.dma(SP, ("x3", s), out=xt[s][:], in_=srcs[s], wr=[("x3", s)])
            yT_all = [("yT", c) for c in range(16)]
            steps = []
            wl = w_out[l]
            for n in range(4):
                loads = [(wview(16, 512), wsrc(wl, 0, 16, n * 512, 512))]

                def comp(slot, key, n=n):
                    w3 = wview(16, 512)(slot)
                    for s in range(4):
                        bk, bkey = nb()
                        for k in range(16):
                            S.op(PE, lambda: nc.tensor.matmul(bk[:, :], lhsT=yT[:, k, s * 128:(s + 1) * 128],
                                                              rhs=w3[:, k, :], start=(k == 0), stop=(k == 15)),
                                 rd=[key, ("yT", k)], wr=[bkey])
                        S.evac(yo[s][:, n * 512:(n + 1) * 512], bk[:, :], rd=[bkey], wr=[("yo", s, n)])
                steps.append((loads, comp))
            ring.run(steps)
            junk = small["junk"]
            ss = small["ss"]
            for s in range(4):
                yk = [("yo", s, n) for n in range(4)]
                ssv = ss[:, 4 + (s % 2):5 + (s % 2)]
                sk = ("ss", 4 + (s % 2))
                S.op(ACT, lambda: nc.scalar.activation(out=junk[:], in_=yo[s][:], func=AF.Square, accum_out=ssv),
                     rd=yk, wr=["junk", sk])
                S.op(ACT, lambda: nc.scalar.activation(out=ssv, in_=ssv, func=AF.Sqrt, bias=eps_t[:], scale=1.0 / D),
                     rd=[sk, "eps"], wr=[sk])
                S.op(DVE, lambda: nc.vector.reciprocal(out=ssv, in_=ssv), rd=[sk], wr=[sk])
                S.op(DVE, lambda: nc.vector.tensor_tensor(out=yo[s][:], in0=yo[s][:], in1=G1[:], op=ALU.mult),
                     rd=yk + ["G1"], wr=[("yog", s)])
                S.op(DVE, lambda: nc.vector.scalar_tensor_tensor(out=xt[s][:], in0=yo[s][:], scalar=ssv, in1=xt[s][:],
                                                                 op0=ALU.mult, op1=ALU.add),
                     rd=[("yog", s), sk, ("x3", s)], wr=[("x3", s)])
                S.dma(SP, ("x3", s), out=d_x1[tok0 + s * 128:tok0 + (s + 1) * 128, :], in_=xt[s][:],
                      rd=[("x3", s)], wr=[("dx1", tt, s)])
                rms_front("p3", xt[s][:], ("x3", s), xn[s][:], ("xn", s), small, s % 2)
            transpose_mod(xn, [("xn", s) for s in range(4)], h2T, 2, cond)
            for c in range(16):
                S.dma(SP, ("hT", c), out=d_h2T[c * 128:(c + 1) * 128, tok0:tok0 + 512], in_=h2T[:, c, :],
                      rd=[("hT", c)], wr=[("dh2T", tt, c)])
            S.barrier(bar_t[:])

    def phase3b(l, tt):
        cond = 0 if tt < 4 else 1
        tok0 = tt * 512
        with contextlib.ExitStack() as st:
            h2T = sb(st, "h2Tb", [128, 16, 512], BF16)
            hid = sb(st, "hid", [128, 64, 512], BF16)
            G2 = sb(st, "G2", [128, D], F32)
            fo = [sb(st, f"fo{i}", [128, D], F32) for i in range(4)]
            xt = [sb(st, f"x4_{i}", [128, D], F32) for i in range(2)]
            sq = [sb(st, f"sq{i}", [128, 512], F32) for i in range(2)]
            junk = sb(st, "junk4", [128, D], BF16)
            ss = sb(st, "ss4", [128, 8], F32)
            for c in range(16):
                S.dma(SP, ("h2T", c), out=h2T[:, c, :], in_=d_h2T[c * 128:(c + 1) * 128, tok0:tok0 + 512],
                      wr=[("h2T", c)])
            S.dma(SP, "G2", out=G2[:], in_=d_G[1, cond, :].partition_broadcast(128), wr=["G2"])
            h_all = [("h2T", c) for c in range(16)]
            steps = []
            w1 = w_ff1[l]
            for cb in range(16):
                loads = [(wview(16, 512), wsrc(w1, 0, 16, cb * 512, 512))]

                def comp(slot, key, cb=cb):
                    w3 = wview(16, 512)(slot)
                    for j in range(4):
                        bk, bkey = nb()
                        for k in range(16):
                            S.op(PE, lambda: nc.tensor.matmul(bk[:, :], lhsT=w3[:, k, j * 128:(j + 1) * 128],
                                                              rhs=h2T[:, k, :], start=(k == 0), stop=(k == 15)),
                                 rd=[key, ("h2T", k)], wr=[bkey])
                        q_ = sq[j % 2]
                        S.op(ACT, lambda: nc.scalar.activation(out=q_[:], in_=bk[:, :], func=AF.Square),
                             rd=[bkey], wr=[("sq", j % 2)])
                        S.op(DVE, lambda: nc.vector.scalar_tensor_tensor(out=hid[:, cb * 4 + j, :], in0=bk[:, :],
                                                                         scalar=0.0, in1=q_[:], op0=ALU.is_gt,
                                                                         op1=ALU.mult),
                             rd=[bkey, ("sq", j % 2)], wr=[("hid", cb * 4 + j)])
                steps.append((loads, comp))
            w2 = w_ff2[l]
            for n in range(4):
                for kb in range(4):
                    loads = [(wview(16, 512), wsrc(w2, kb * 2048, 16, n * 512, 512))]

                    def comp(slot, key, n=n, kb=kb):
                        w3 = wview(16, 512)(slot)
                        for s in range(4):
                            bk, bkey = bankx((n % 2) * 4 + s)
                            for k in range(16):
                                S.op(PE, lambda: nc.tensor.matmul(
                                    bk[:, :], lhsT=hid[:, kb * 16 + k, s * 128:(s + 1) * 128], rhs=w3[:, k, :],
                                    start=(kb == 0 and k == 0), stop=(kb == 3 and k == 15)),
                                    rd=[key, ("hid", kb * 16 + k)], wr=[bkey])
                            if kb == 3:
                                S.evac(fo[s][:, n * 512:(n + 1) * 512], bk[:, :], rd=[bkey], wr=[("fo", s, n)])
                    steps.append((loads, comp))
            ring.run(steps)
            for s in range(4):
                b = s % 2
                S.dma(SP, ("x4", b), out=xt[b][:], in_=d_x1[tok0 + s * 128:tok0 + (s + 1) * 128, :],
                      wr=[("x4", b)])
                fk = [("fo", s, n) for n in range(4)]
                ssv = ss[:, b:b + 1]
                sk = ("ss", b)
                S.op(ACT, lambda: nc.scalar.activation(out=junk[:], in_=fo[s][:], func=AF.Square, accum_out=ssv),
                     rd=fk, wr=["junk", sk])
                S.op(ACT, lambda: nc.scalar.activation(out=ssv, in_=ssv, func=AF.Sqrt, bias=eps_t[:], scale=1.0 / D),
                     rd=[sk, "eps"], wr=[sk])
                S.op(DVE, lambda: nc.vector.reciprocal(out=ssv, in_=ssv), rd=[sk], wr=[sk])
                S.op(DVE, lambda: nc.vector.tensor_tensor(out=fo[s][:], in0=fo[s][:], in1=G2[:], op=ALU.mult),
                     rd=fk + ["G2"], wr=[("fog", s)])
                S.op(DVE, lambda: nc.vector.scalar_tensor_tensor(out=xt[b][:], in0=fo[s][:], scalar=ssv, in1=xt[b][:],
                                                                 op0=ALU.mult, op1=ALU.add),
                     rd=[("fog", s), sk, ("x4", b)], wr=[("x4", b)])
                if l == DEPTH - 1:
                    dst = y_s[tok0 + s * 128:tok0 + (s + 1) * 128, :] if tt < 4 else y_p[s * 128:(s + 1) * 128, :]
                else:
                    dst = d_xres[tok0 + s * 128:tok0 + (s + 1) * 128, :]
                S.dma(SP, ("x4", b), out=dst, in_=xt[b][:], rd=[("x4", b)], wr=[("dxo", tt, s)])
            S.barrier(bar_t[:])


    conv_ch = [S.chan(reserve=True) for _ in range(DEPTH)]

    def convert_weights(l):
        ch = conv_ch[l]
        wr_ = [("wconv",)]
        for r in range(0, D, 256):
            S.dma(POOL, ch, out=d_wob[r:r + 256, :], in_=w_out[l, r:r + 256, :], rd=(), wr=wr_, max_dma_last_dim=8192)
        for r in range(0, D, 128):
            S.dma(POOL, ch, out=d_w1b[r:r + 128, :], in_=w_ff1[l, r:r + 128, :], rd=(), wr=wr_, max_dma_last_dim=8192)
        for r in range(0, DFF, 512):
            S.dma(POOL, ch, out=d_w2b[r:r + 512, :], in_=w_ff2[l, r:r + 512, :], rd=(), wr=wr_, max_dma_last_dim=8192)

    def phase3_all(l):
        with contextlib.ExitStack() as st:
            buf16 = sb(st, "buf16", [128, 16, 512], BF16)
            b16f = buf16[:].rearrange("p c t -> p (c t)")
            G = sb(st, "G12", [128, D], F32)
            xt = [sb(st, f"x3_{i}", [128, D], F32) for i in range(2)]
            yo = [sb(st, f"yo{i}", [128, D], F32) for i in range(4)]
            h2T = sb(st, "h2T", [128, 16, 512], BF16)
            hid = sb(st, "hid", [128, 64, 512], BF16)
            sq = [sb(st, "sq0", [128, 512], F32)] * 2
            junk = h2T[:, 0:4, :].rearrange("p c t -> p (c t)")
            jkeys = [("hT", c) for c in range(4)]
            small = {"junk": junk, "ss": sb(st, "ss3", [128, 8], F32), "jkeys": jkeys}
            ss = small["ss"]
            wl, w1, w2 = d_wob, d_w1b, d_w2b
            wk = [("wconv",)]
            for tt in range(5):
                cond = 0 if tt < 4 else 1
                tok0 = tt * 512
                for g in range(4):
                    S.dma(SP, ("b16", g), out=buf16[:, 4 * g:4 * g + 4, :],
                          in_=d_yT[g * 512:(g + 1) * 512, tok0:tok0 + 512].rearrange("(c p) t -> p c t", p=128),
                          wr=[("b16", g)])
                S.dma(SP, "G", out=G[:], in_=d_G[0, cond, :].partition_broadcast(128), wr=["G"])
                srcs = tile_src(l, tt)
                steps = []
                for n in range(4):
                    loads = [(wview(16, 512), wsrc(wl, 0, 16, n * 512, 512), wk)]

                    def comp(slot, key, n=n):
                        w3 = wview(16, 512)(slot)
                        for s_ in range(4):
                            bk, bkey = nb()
                            for k in range(16):
                                S.op(PE, lambda: nc.tensor.matmul(bk[:, :], lhsT=buf16[:, k, s_ * 128:(s_ + 1) * 128],
                                                                  rhs=w3[:, k, :], start=(k == 0), stop=(k == 15)),
                                     rd=[key, ("b16", k // 4)], wr=[bkey])
                            S.evac(yo[s_][:, n * 512:(n + 1) * 512], bk[:, :], rd=[bkey], wr=[("yo", s_, n)])
                    steps.append((loads, comp))
                ring.run(steps)
                for s_ in range(4):
                    b = s_ % 2
                    S.dma(SP, ("x3", b), out=xt[b][:], in_=srcs[s_], wr=[("x3", b)])
                    yk = [("yo", s_, n) for n in range(4)]
                    ssv = ss[:, 4 + b:5 + b]
                    sk = ("ss", 4 + b)
                    S.op(ACT, lambda: nc.scalar.activation(out=junk[:], in_=yo[s_][:], func=AF.Square, accum_out=ssv),
                         rd=yk, wr=jkeys + [sk])
                    S.op(ACT, lambda: nc.scalar.activation(out=ssv, in_=ssv, func=AF.Sqrt, bias=eps_t[:],
                                                           scale=1.0 / D), rd=[sk, "eps"], wr=[sk])
                    S.op(DVE, lambda: nc.vector.reciprocal(out=ssv, in_=ssv), rd=[sk], wr=[sk])
                    S.op(DVE, lambda: nc.vector.tensor_tensor(out=yo[s_][:], in0=yo[s_][:], in1=G[:], op=ALU.mult),
                         rd=yk + ["G"], wr=yk)
                    S.op(DVE, lambda: nc.vector.scalar_tensor_tensor(out=xt[b][:], in0=yo[s_][:], scalar=ssv,
                                                                     in1=xt[b][:], op0=ALU.mult, op1=ALU.add),
                         rd=yk + [sk, ("x3", b)], wr=[("x3", b)])
                    S.dma(SP, ("x3", b), out=d_x1[tok0 + s_ * 128:tok0 + (s_ + 1) * 128, :], in_=xt[b][:],
                          rd=[("x3", b)], wr=[("dx1", tt, s_)])
                    rms_front("p3", xt[b][:], ("x3", b), b16f[:, s_ * D:(s_ + 1) * D], ("b16", s_), small, b)
                transpose_mod([b16f[:, s_ * D:(s_ + 1) * D] for s_ in range(4)], [("b16", s_) for s_ in range(4)],
                              h2T, 2, cond)
                steps = []
                for cb in range(16):
                    loads = [(wview(16, 512), wsrc(w1, 0, 16, cb * 512, 512), wk)]

                    def comp(slot, key, cb=cb):
                        w3 = wview(16, 512)(slot)
                        for j in range(4):
                            bk, bkey = nb()
                            for k in range(16):
                                S.op(PE, lambda: nc.tensor.matmul(bk[:, :], lhsT=w3[:, k, j * 128:(j + 1) * 128],
                                                                  rhs=h2T[:, k, :], start=(k == 0), stop=(k == 15)),
                                     rd=[key, ("hT", k)], wr=[bkey])
                            q_ = sq[j % 2]
                            S.op(ACT, lambda: nc.scalar.activation(out=q_[:], in_=bk[:, :], func=AF.Square),
                                 rd=[bkey], wr=[("sq", 0)])
                            S.op(DVE, lambda: nc.vector.scalar_tensor_tensor(out=hid[:, cb * 4 + j, :], in0=bk[:, :],
                                                                             scalar=0.0, in1=q_[:], op0=ALU.is_gt,
                                                                             op1=ALU.mult),
                                 rd=[bkey, ("sq", 0)], wr=[("hid", cb * 4 + j)])
                    steps.append((loads, comp))
                for n in range(4):
                    for kb in range(4):
                        loads = [(wview(16, 512), wsrc(w2, kb * 2048, 16, n * 512, 512), wk)]

                        def comp(slot, key, n=n, kb=kb):
                            w3 = wview(16, 512)(slot)
                            for s_ in range(4):
                                bk, bkey = bankx((n % 2) * 4 + s_)
                                for k in range(16):
                                    S.op(PE, lambda: nc.tensor.matmul(
                                        bk[:, :], lhsT=hid[:, kb * 16 + k, s_ * 128:(s_ + 1) * 128], rhs=w3[:, k, :],
                                        start=(kb == 0 and k == 0), stop=(kb == 3 and k == 15)),
                                        rd=[key, ("hid", kb * 16 + k)], wr=[bkey])
                                if kb == 3:
                                    S.evac(yo[s_][:, n * 512:(n + 1) * 512], bk[:, :], rd=[bkey], wr=[("yo", s_, n)])
                        steps.append((loads, comp))
                ring.run(steps)
                S.dma(SP, "G", out=G[:], in_=d_G[1, cond, :].partition_broadcast(128), wr=["G"])
                for s_ in range(4):
                    b = s_ % 2
                    S.dma(SP, ("x3", b), out=xt[b][:], in_=d_x1[tok0 + s_ * 128:tok0 + (s_ + 1) * 128, :],
                          rd=[("dx1", tt, s_)], wr=[("x3", b)])
                    fk = [("yo", s_, n) for n in range(4)]
                    ssv = ss[:, 6 + b:7 + b]
                    sk = ("ss", 6 + b)
                    S.op(ACT, lambda: nc.scalar.activation(out=junk[:], in_=yo[s_][:], func=AF.Square, accum_out=ssv),
                         rd=fk, wr=jkeys + [sk])
                    S.op(ACT, lambda: nc.scalar.activation(out=ssv, in_=ssv, func=AF.Sqrt, bias=eps_t[:],
                                                           scale=1.0 / D), rd=[sk, "eps"], wr=[sk])
                    S.op(DVE, lambda: nc.vector.reciprocal(out=ssv, in_=ssv), rd=[sk], wr=[sk])
                    S.op(DVE, lambda: nc.vector.tensor_tensor(out=yo[s_][:], in0=yo[s_][:], in1=G[:], op=ALU.mult),
                         rd=fk + ["G"], wr=fk)
                    S.op(DVE, lambda: nc.vector.scalar_tensor_tensor(out=xt[b][:], in0=yo[s_][:], scalar=ssv,
                                                                     in1=xt[b][:], op0=ALU.mult, op1=ALU.add),
                         rd=fk + [sk, ("x3", b)], wr=[("x3", b)])
                    if l == DEPTH - 1:
                        dst = y_s[tok0 + s_ * 128:tok0 + (s_ + 1) * 128, :] if tt < 4 \
                            else y_p[s_ * 128:(s_ + 1) * 128, :]
                    else:
                        dst = d_xres[tok0 + s_ * 128:tok0 + (s_ + 1) * 128, :]
                    S.dma(SP, ("x3", b), out=dst, in_=xt[b][:], rd=[("x3", b)], wr=[("dxo", tt, s_)])
            S.barrier(bar_t[:])

    def phase_hgrn(l, tok0, N, sample, sq, outer=None):
        nch = N // 128
        ncg = min(4, nch)
        with scope(outer) as st:
            maskr = sb(st, "maskr", [128, 2, 2, 128], F32)
            gainb = sb(st, "gainb", [128, 512], F32)
            qT = sb(st, "hqT", [128, N], BF16)
            v = sb(st, "hv", [128, nch, 128], BF16)
            gate = sb(st, "hgate", [128, nch, 128], BF16)
            lf = [sb(st, f"lf{d}", [128, N], F32) for d in range(2)]
            kk = [sb(st, f"kk{d}", [128, N], BF16) for d in range(2)]
            Pp = [sb(st, f"Pp{d}", [128, N], F32) for d in range(2)]
            etmp = sb(st, "etmp", [128, N], F32)
            Qt = [sb(st, f"Qt{d}", [128, N], BF16) for d in range(2)]
            Qtc = [sb(st, f"Qtc{d}", [128, N], BF16) for d in range(2)]
            Kneg = [sb(st, f"Kneg{d}", [128, N], BF16) for d in range(2)]
            Kpos = [sb(st, f"Kpos{d}", [128, N], BF16) for d in range(2)]
            Qh = [sb(st, f"Qh{d}", [128, N], BF16) for d in range(2)]
            KhT = [sb(st, f"KhT{d}", [128, N], BF16) for d in range(2)]
            Khtm = [sb(st, f"Khtm{d}", [128, nch, 128], BF16) for d in range(2)]
            Sin = [sb(st, f"Sin{d}", [128, nch, 128], BF16) for d in range(2)]
            Sst = [sb(st, f"Sst{d}", [128, 128], F32) for d in range(2)]
            sc = sb(st, "hsc", [128, 2, 4, nch], F32)
            sc64 = sb(st, "hsc64", [128, 2, 2 * nch], F32)
            Qm = [sb(st, f"Qm{d}", [128, N], BF16) for d in range(2)]
            Km = [sb(st, f"Km{d}", [128, N], BF16) for d in range(2)]
            scT = [sb(st, f"scT{i}", [128, 2, 2, 128], BF16) for i in range(2)]
            for i in range(2):
                S.op(DVE, lambda: nc.vector.memset(scT[i][:], 0.0), wr=[("scT", i)])
            ssq2 = [sb(st, f"hssq{i}", [128, 4], F32) for i in range(2)]
            hjunk = sb(st, "hjunk", [128, 128], BF16)
            ytm2 = [sb(st, f"ytm{i}", [128, 4, 128], F32) for i in range(2)]
            ytb2 = [sb(st, f"ytb{i}", [128, 4, 128], BF16) for i in range(2)]
            yst2 = [sb(st, f"hyst{i}", [128, 512], BF16) for i in range(2)]
            for d in range(2):
                for r in range(2):
                    S.dma(SP, ("maskr", d, r), out=maskr[:, d, r, :], in_=c_mask[d, :, :], wr=[("maskr", d, r)])
            mask_all = [("maskr", d, r) for d in range(2) for r in range(2)]
            S.dma(SP, "gainb", out=gainb[:], in_=hg_gain[l, :].partition_broadcast(128), wr=["gainb"])
            for h in range(4):
                hs = slice(h * 128, (h + 1) * 128)
                S.dma(SP, "hqT", out=qT[:], in_=d_hqT[hs, tok0:tok0 + N], wr=["hqT"])
                S.dma(SP, "hv", out=v[:], in_=d_hv[tok0:tok0 + N, hs].rearrange("(c p) d -> p c d", p=128), wr=["hv"])
                S.dma(SP, "hgate", out=gate[:], in_=d_gate[tok0:tok0 + N, hs].rearrange("(c p) d -> p c d", p=128),
                      wr=["hgate"])
                for d in range(2):
                    S.dma(SP, ("lf", d), out=lf[d][:], in_=d_lf[d, hs, tok0:tok0 + N],
                          wr=[("lf", d), (("lf", d), 0), (("lf", d), 1)])
                    S.dma(SP, ("kk", d), out=kk[d][:], in_=d_kk[d, hs, tok0:tok0 + N], wr=[("kk", d)])
                    if sample:
                        S.dma(SP, ("Sst", d), out=Sst[d][:], in_=st_in[l, d, h, :, :], wr=[("Sst", d)])
                    else:
                        S.op(DVE, lambda: nc.vector.memset(Sst[d][:], 0.0), wr=[("Sst", d)])
                for d in range(2):
                    Pk = ("Pp", d)
                    ekeys = [("etmp", 0), ("etmp", 1)]
                    S.op(DVE, lambda: nc.vector.memset(etmp[:], 1.0), wr=ekeys)
                    S.op(DVE, lambda: nc.vector.tensor_tensor_scan(out=Pp[d][:], data0=etmp[:], data1=lf[d][:],
                                                                   initial=0.0, op0=ALU.mult, op1=ALU.add),
                         rd=ekeys + [("lf", d)], wr=[Pk])
                    dtmp = lf[d]
                    dk = ("lf", d)
                    Pv = Pp[d][:].rearrange("p (c t) -> p c t", t=128)
                    r_, a_, b_, dec_ = (sc[:, d, i, :] for i in range(4))
                    sk = ("hsc", d)
                    Pv64 = Pp[d][:].rearrange("p (c t) -> p c t", t=64)
                    r64 = sc64[:, d, :]
                    if d == 0:
                        S.op(DVE, lambda: nc.vector.tensor_copy(out=r_, in_=Pv[:, :, 63]), rd=[Pk], wr=[sk])
                        S.op(DVE, lambda: nc.vector.tensor_copy(out=b_, in_=Pv[:, :, 127]), rd=[Pk], wr=[sk])
                        S.op(DVE, lambda: nc.vector.memset(a_[:, 0:1], 0.0), wr=[sk])
                        if nch > 1:
                            S.op(DVE, lambda: nc.vector.tensor_copy(out=a_[:, 1:nch], in_=Pv[:, 0:nch - 1, 127]),
                                 rd=[Pk], wr=[sk])
                    else:
                        S.op(DVE, lambda: nc.vector.tensor_scalar(out=a_, in0=Pv[:, :, 127], scalar1=-1.0, scalar2=None,
                                                                  op0=ALU.mult), rd=[Pk], wr=[sk])
                        S.op(DVE, lambda: nc.vector.tensor_tensor(out=Pp[d][:], in0=lf[d][:], in1=Pp[d][:],
                                                                  op=ALU.subtract), rd=[Pk, ("lf", d)], wr=[Pk])
                        S.op(DVE, lambda: nc.vector.tensor_copy(out=r_, in_=Pv[:, :, 64]), rd=[Pk], wr=[sk])
                        S.op(DVE, lambda: nc.vector.tensor_copy(out=b_, in_=Pv[:, :, 0]), rd=[Pk], wr=[sk])
                    S.op(DVE, lambda: nc.vector.tensor_copy(out=r64, in_=Pv64[:, :, 31 + d]), rd=[Pk], wr=[sk])
                    S.op(DVE, lambda: nc.vector.tensor_tensor(out=dec_, in0=b_, in1=a_, op=ALU.subtract),
                         rd=[sk], wr=[sk])
                    S.op(ACT, lambda: nc.scalar.activation(out=dec_, in_=dec_, func=AF.Exp), rd=[sk], wr=[sk])
                    dv = dtmp[:].rearrange("p (c t) -> p c t", t=128)

                    NH = 2 if N >= 1024 else 1
                    HL = N // NH

                    def bsub(scal, w=128):
                        n_ = HL // w
                        for hf in range(NH):
                            sl = slice(hf * HL, (hf + 1) * HL)
                            S.op(DVE, lambda: nc.vector.tensor_tensor(
                                out=dtmp[:, sl].rearrange("p (c t) -> p c t", t=w),
                                in0=Pp[d][:, sl].rearrange("p (c t) -> p c t", t=w),
                                in1=scal[:, hf * n_:(hf + 1) * n_].unsqueeze(2).broadcast_to([128, n_, w]),
                                op=ALU.subtract), rd=[Pk, sk], wr=[(dk, hf)])

                    def expmul(mode, src, srckey, dst, dstkey):
                        for hf in range(NH):
                            sl = slice(hf * HL, (hf + 1) * HL)
                            ek = ("etmp", hf)
                            if mode == "exp":
                                S.op(ACT, lambda: nc.scalar.activation(out=etmp[:, sl], in_=dtmp[:, sl], func=AF.Exp),
                                     rd=[(dk, hf)], wr=[ek])
                            else:
                                rs = 1.0 if mode == "expnegmax" else -1.0
                                es = 1.0 if mode == "expm1negmin" else -1.0
                                S.op(ACT, lambda: nc.scalar.activation(out=etmp[:, sl], in_=dtmp[:, sl], func=AF.Relu,
                                                                       scale=rs), rd=[(dk, hf)], wr=[ek])
                                S.op(ACT, lambda: nc.scalar.activation(out=etmp[:, sl], in_=etmp[:, sl], func=AF.Exp,
                                                                       scale=es), rd=[ek], wr=[ek])
                            if mode == "expm1negmin":
                                S.op(DVE, lambda: nc.vector.scalar_tensor_tensor(out=dst[:, sl], in0=etmp[:, sl],
                                                                                 scalar=-1.0, in1=src[:, sl],
                                                                                 op0=ALU.add, op1=ALU.mult),
                                     rd=[ek, srckey], wr=[dstkey])
                            else:
                                S.op(DVE, lambda: nc.vector.tensor_tensor(out=dst[:, sl], in0=src[:, sl],
                                                                          in1=etmp[:, sl], op=ALU.mult),
                                     rd=[ek, srckey], wr=[dstkey])
                    bsub(r64, 64)
                    expmul("exp", qT, "hqT", Qt[d], ("Qt", d))
                    expmul("expmin", qT, "hqT", Qtc[d], ("Qtc", d))
                    expmul("expnegmax", kk[d], ("kk", d), Kneg[d], ("Kneg", d))
                    expmul("expm1negmin", kk[d], ("kk", d), Kpos[d], ("Kpos", d))
                    bsub(a_)
                    expmul("expmin", qT, "hqT", Qh[d], ("Qh", d))
                    bsub(b_)
                    expmul("expnegmax", kk[d], ("kk", d), KhT[d], ("KhT", d))
                    bsub(r_)
                    expmul("expmin", qT, "hqT", Qm[d], ("Qm", d))
                    expmul("expnegmax", kk[d], ("kk", d), Km[d], ("Km", d))
                    for c0 in range(0, nch, 4):
                        bk, bkey = bankx((c0 // 4) % 2)
                        pb = bk[:, :].bitcast(BF16)
                        nn = min(4, nch - c0)
                        for cc in range(nn):
                            c = c0 + cc
                            S.op(PE, lambda: nc.tensor.transpose(out=pb[:, cc * 128:(cc + 1) * 128],
                                                                 in_=KhT[d][:, c * 128:(c + 1) * 128],
                                                                 identity=ident_b[:]),
                                 rd=[("KhT", d), "ident_b"], wr=[bkey])
                        S.evac(Khtm[d][:, c0:c0 + nn, :], pb[:, 0:nn * 128].rearrange("p (c k) -> p c k", k=128),
                               rd=[bkey], wr=[("Khtm", d, c0)])
                    for c in range(nch):
                        bk, bkey = bankx(4 + c // 4)
                        S.op(PE, lambda: nc.tensor.matmul(bk[:, (c % 4) * 128:(c % 4 + 1) * 128], lhsT=Khtm[d][:, c, :],
                                                          rhs=v[:, c, :], start=True, stop=True),
                             rd=[("Khtm", d, (c // 4) * 4), "hv"], wr=[bkey])
                    order = range(nch) if d == 0 else range(nch - 1, -1, -1)
                    for c in order:
                        bk, bkey = bankx(4 + c // 4)
                        S.op(DVE, lambda: nc.vector.tensor_copy(out=Sin[d][:, c, :], in_=Sst[d][:]),
                             rd=[("Sst", d)], wr=[("Sin", d)])
                        S.op(DVE, lambda: nc.vector.scalar_tensor_tensor(
                            out=Sst[d][:], in0=Sst[d][:], scalar=dec_[:, c:c + 1],
                            in1=bk[:, (c % 4) * 128:(c % 4 + 1) * 128], op0=ALU.mult, op1=ALU.add),
                            rd=[("Sst", d), sk, bkey], wr=[("Sst", d)])
                    if not sample:
                        S.dma(SP, ("Sst", d), out=o_st[sq, l, d, h, :, :], in_=Sst[d][:], rd=[("Sst", d)],
                              wr=[("ost", d, h)])
                for c0 in range(0, nch, ncg):
                    bo, bokey = bankx(2 + (c0 // ncg) % 2)
                    for cg in range(0, ncg, 2):
                        gi = (c0 + cg) // 2
                        bs, bskey = bankx(gi % 2)
                        for d in range(2):
                            for cc in range(2):
                                c = c0 + cg + cc
                                A_ = slice(c * 128, c * 128 + 64)
                                B_ = slice(c * 128 + 64, (c + 1) * 128)
                                R0 = (d * 2 + cc) * 128
                                rdk = [("Kneg", d), ("Kpos", d), ("Qt", d), ("Qtc", d), ("Km", d), ("Qm", d)]
                                for (po, X_, co) in ((slice(0, 64), A_, R0), (slice(64, 128), B_, R0 + 64)):
                                    S.op(PE, lambda: nc.tensor.matmul(bs[po, co:co + 64], lhsT=Kneg[d][:, X_],
                                                                      rhs=Qt[d][:, X_], start=True, stop=False),
                                         rd=rdk, wr=[bskey])
                                    S.op(PE, lambda: nc.tensor.matmul(bs[po, co:co + 64], lhsT=Kpos[d][:, X_],
                                                                      rhs=Qtc[d][:, X_], start=False, stop=True),
                                         rd=rdk, wr=[bskey])
                                if d == 0:
                                    S.op(PE, lambda: nc.tensor.matmul(bs[0:64, R0 + 64:R0 + 128], lhsT=Km[d][:, A_],
                                                                      rhs=Qm[d][:, B_], start=True, stop=True),
                                         rd=rdk, wr=[bskey])
                                else:
                                    S.op(PE, lambda: nc.tensor.matmul(bs[64:128, R0:R0 + 64], lhsT=Km[d][:, B_],
                                                                      rhs=Qm[d][:, A_], start=True, stop=True),
                                         rd=rdk, wr=[bskey])
                        sT = scT[gi % 2]
                        sTk = ("scT", gi % 2)
                        bs4 = bs[:, :].rearrange("p (d c t) -> p d c t", d=2, c=2)
                        mku = maskr[:].bitcast(mybir.dt.uint32)
                        for (ps_, d_, ts_) in ((slice(0, 64), 0, slice(0, 128)), (slice(0, 64), 1, slice(0, 64)),
                                               (slice(64, 128), 0, slice(64, 128)), (slice(64, 128), 1, slice(0, 128))):
                            S.op(DVE, lambda: nc.vector.copy_predicated(out=sT[ps_, d_, :, ts_], mask=mku[ps_, d_, :, ts_],
                                                                        data=bs4[ps_, d_, :, ts_]),
                                 rd=[bskey] + mask_all, wr=[sTk])
                        for cc in range(2):
                            c = c0 + cg + cc
                            cs = slice(c * 128, (c + 1) * 128)
                            reg = bo[:, (cg + cc) * 128:(cg + cc + 1) * 128]
                            S.op(PE, lambda: nc.tensor.matmul(reg, lhsT=sT[:, 0, cc, :], rhs=v[:, c, :], start=True,
                                                              stop=False), rd=[sTk, "hv"], wr=[bokey])
                            S.op(PE, lambda: nc.tensor.matmul(reg, lhsT=sT[:, 1, cc, :], rhs=v[:, c, :], start=False,
                                                              stop=False), rd=[sTk, "hv"], wr=[bokey])
                            S.op(PE, lambda: nc.tensor.matmul(reg, lhsT=Qh[0][:, cs], rhs=Sin[0][:, c, :], start=False,
                                                              stop=False), rd=[("Qh", 0), ("Sin", 0)], wr=[bokey])
                            S.op(PE, lambda: nc.tensor.matmul(reg, lhsT=Qh[1][:, cs], rhs=Sin[1][:, c, :], start=False,
                                                              stop=True), rd=[("Qh", 1), ("Sin", 1)], wr=[bokey])
                    gp_ = (c0 // ncg) % 2
                    ssq, ytm, ytb, yst = ssq2[gp_], ytm2[gp_], ytb2[gp_], yst2[gp_]
                    for cc in range(ncg):
                        S.op(ACT, lambda: nc.scalar.activation(out=hjunk[:], in_=bo[:, cc * 128:(cc + 1) * 128],
                                                               func=AF.Square, accum_out=ssq[:, cc:cc + 1]),
                             rd=[bokey], wr=["hjunk", ("hssq", gp_, cc)])
                    sqk = [("hssq", gp_, cc) for cc in range(ncg)]
                    S.op(ACT, lambda: nc.scalar.activation(out=ssq[:, 0:ncg], in_=ssq[:, 0:ncg], func=AF.Sqrt,
                                                           bias=eps_t[:], scale=1.0 / 128), rd=sqk + ["eps"], wr=sqk)
                    S.op(DVE, lambda: nc.vector.reciprocal(out=ssq[:, 0:ncg], in_=ssq[:, 0:ncg]), rd=sqk, wr=sqk)
                    for cc in range(ncg):
                        S.op(DVE, lambda: nc.vector.scalar_tensor_tensor(
                            out=ytm[:, cc, :], in0=bo[:, cc * 128:(cc + 1) * 128], scalar=ssq[:, cc:cc + 1],
                            in1=gainb[:, hs], op0=ALU.mult, op1=ALU.mult),
                            rd=[bokey, ("hssq", gp_, cc), "gainb"], wr=[("ytm", gp_, cc)])
                    ytk = [("ytm", gp_, cc) for cc in range(ncg)]
                    S.op(DVE, lambda: nc.vector.tensor_tensor(out=ytb[:, 0:ncg, :], in0=ytm[:, 0:ncg, :],
                                                              in1=gate[:, c0:c0 + ncg, :], op=ALU.mult),
                         rd=ytk + ["hgate"], wr=[("ytb", gp_)])
                    bt, btkey = bankx(6 + (c0 // ncg) % 2)
                    pb = bt[:, :].bitcast(BF16)
                    for cc in range(ncg):
                        S.op(PE, lambda: nc.tensor.transpose(out=pb[:, cc * 128:(cc + 1) * 128], in_=ytb[:, cc, :],
                                                             identity=ident_b[:]), rd=[("ytb", gp_), "ident_b"], wr=[btkey])
                    S.evac(yst[:, 0:ncg * 128], pb[:, 0:ncg * 128], rd=[btkey], wr=[("hyst", gp_)])
                    S.dma(SP, ("hyst", gp_), out=d_yT[512 + h * 128:512 + (h + 1) * 128,
                                               tok0 + c0 * 128:tok0 + (c0 + ncg) * 128],
                          in_=yst[:, 0:ncg * 128], rd=[("hyst", gp_)], wr=[("dyTh", h, tok0, c0)])
            if outer is None:
                S.barrier(bar_t[:])

    plan = []
    seqs = [(0, NSQ, True, 0, c_dftL, 0, CTX + NSQ),
            (NSQ, NP, False, 0, c_dftP, CTX + NSQ, NP),
            (NSQ + NP, NP, False, 1, c_dftP, CTX + NSQ + NP, NP)]
    for l in range(DEPTH):
        plan.append(("p0", l))
        plan.append(("p1c", l))
        for tt in range(5):
            plan.append(("p1", l, tt))
        plan.append(("fn", l, 0))
        plan.append(("conv", l))
        plan.append(("hg", l, 0))
        plan.append(("at", l, 0))
        plan.append(("pmix", l))
        plan.append(("p3", l))
    for item in plan:
        k = item[0]
        if k == "p0":
            phase0(item[1])
        elif k == "p1c":
            phase1_ctx(item[1])
        elif k == "p1":
            phase1(item[1], item[2])
        elif k in ("fn", "hg", "at"):
            tok0, N, smp, sq, dft, klo, nk = seqs[item[2]]
            if k == "fn":
                phase_fn(item[1], tok0, N, dft)
            elif k == "hg":
                phase_hgrn(item[1], tok0, N, smp, sq)
            else:
                phase_attn(item[1], tok0, N, klo, nk)
        elif k == "pmix":
            with contextlib.ExitStack() as real:
                rf, rh, ra = Reuse(real, "fn"), Reuse(real, "hg"), Reuse(real, "at")
                for si in (1, 2):
                    tok0, N, smp, sq, dft, klo, nk = seqs[si]
                    phase_fn(item[1], tok0, N, dft, outer=rf)
                for si in (1, 2):
                    tok0, N, smp, sq, dft, klo, nk = seqs[si]
                    phase_hgrn(item[1], tok0, N, smp, sq, outer=rh)
                for si in (1, 2):
                    tok0, N, smp, sq, dft, klo, nk = seqs[si]
                    phase_attn(item[1], tok0, N, klo, nk, outer=ra)
                S.barrier(bar_t[:])
        elif k == "conv":
            convert_weights(item[1])
        elif k == "p3":
            phase3_all(item[1])
        elif k == "p3a":
            phase3a(item[1], item[2])
        elif k == "p3b":
            phase3b(item[1], item[2])
        if stop_after is not None and item == stop_after:
            break

    for c in S.chans:
        if c.count > 0:
            S._need(SP, ("C", c, c.count), True)
    es.close()
    return nc


def _consts():
    f = np.float64
    ident = np.eye(128, dtype=np.float32)
    t = np.arange(NSQ)
    r = (t // 64).astype(f)
    col = (t % 64).astype(f)
    nf = 16
    inv = 10000.0 ** (-np.arange(nf, dtype=f) / nf)
    ar = r[:, None] * inv
    ac = col[:, None] * inv
    ang = np.concatenate([ar, ar, ac, ac], axis=-1)
    cosT = np.cos(ang).T.astype(np.float32)
    sinT = np.sin(ang).T.astype(np.float32)
    R = np.zeros((128, 64), np.float32)
    for a in range(2):
        for j in range(16):
            i0 = a * 32 + j
            i1 = a * 32 + 16 + j
            R[i1, i0] = -1.0
            R[i0, i1] = 1.0

    def dft(n):
        k = np.arange(n)
        a = 2 * np.pi * ((k[:, None] * k[None, :]) % n) / n
        return np.cos(a), np.sin(a)
    cL, sL = dft(NSQ)
    dftL = np.stack([cL, -sL]).astype(np.float32) / np.sqrt(NSQ).astype(np.float32)
    cP, sP = dft(NP)
    dftP = np.stack([cP, -sP]).astype(np.float32) / np.sqrt(NP).astype(np.float32)
    cd, sd = dft(128)
    dftd = np.concatenate([cd, sd], axis=1).astype(np.float32) / np.float32(np.sqrt(128))
    s_i = np.arange(128)[:, None]
    t_i = np.arange(128)[None, :]
    mask = np.stack([(s_i <= t_i), (s_i >= t_i)]).astype(np.float32)
    return dict(c_ident=ident, c_cosT=np.ascontiguousarray(cosT), c_sinT=np.ascontiguousarray(sinT), c_rmat=R,
                c_dftL=np.ascontiguousarray(dftL.astype(np.float32)), c_dftP=np.ascontiguousarray(dftP.astype(np.float32)),
                c_dftd=np.ascontiguousarray(dftd), c_mask=mask)


def _col(v):
    v = np.asarray(v)
    sh = v.shape
    v2 = v.reshape(sh[:-1] + (sh[-1] // 128, 128))
    return np.ascontiguousarray(np.moveaxis(v2, -1, 0))


def make_in_maps(inp):
    A = {k: np.ascontiguousarray(np.asarray(v, dtype=np.float32)) for k, v in inp.items()}
    shared = dict(
        w_ada=A["w_ada"], b_ada=A["b_ada"], bcol=_col(A["b_ada"]),
        gcol=_col(np.stack([A["g_pre_mix"], A["g_pre_ff"]], axis=1)),
        g_post_mix=A["g_post_mix"], g_post_ff=A["g_post_ff"], w_in=A["w_in"],
        lbcol=_col(A["hg_lb"]), hg_gain=A["hg_gain"], qnorm=A["mla_q_norm"], kvnorm=A["mla_kv_norm"],
        w_q_b=A["w_q_b"], w_kv_b=A["w_kv_b"], w_out=A["w_out"], w_ff1=A["w_ff1"], w_ff2=A["w_ff2"],
    )
    shared.update(_consts())
    maps = []
    for i in range(8):
        m = dict(shared)
        m["x_s"] = A["x_sample"][i]
        m["x_p"] = np.ascontiguousarray(A["x_prompt"][2 * i:2 * i + 2].reshape(2 * NP, D))
        m["ccol"] = _col(np.stack([A["c"][i], A["c_ctx"]], axis=0)).reshape(128, 2, 16).transpose(0, 2, 1).copy()
        m["cckv"] = A["cache_ckv"][i]
        m["ckr"] = A["cache_krope"][i]
        m["st_in"] = A["state_hgrn"][i]
        maps.append(m)
    return maps


_NC_CACHE = {}


def kernel(**inputs):
    if "nc" not in _NC_CACHE:
        _NC_CACHE["nc"] = build()
    nc = _NC_CACHE["nc"]
    maps = make_in_maps(inputs)
    res = run_bass_kernel_spmd(nc, maps, core_ids=list(range(8)))
    R = res.results
    y_p = np.concatenate([R[i]["y_p"].reshape(2, NP, D) for i in range(8)], axis=0)
    y_s = np.stack([R[i]["y_s"] for i in range(8)], axis=0)
    ockv = np.concatenate([R[i]["o_ckv"] for i in range(8)], axis=0)
    okr = np.concatenate([R[i]["o_kr"] for i in range(8)], axis=0)
    ost = np.concatenate([R[i]["o_st"] for i in range(8)], axis=0)
    return (y_p.astype(np.float32), y_s.astype(np.float32), ockv.astype(np.float32), okr.astype(np.float32),
            ost.astype(np.float32))
```
